# Optimizing a Trainium2 kernel written in Bass

```python
import math
import jax, jax.numpy as jnp
from jax import lax
import numpy as np

D_MODEL = 1024
BATCH = 16
SEQ = 2048
DEPTH = 2

MLA_HEADS = 8
MLA_NOPE = 64
MLA_ROPE = 32
MLA_V = 64
MLA_Q_LORA = 384
MLA_KV_LORA = 256
ROPE_THETA = 10000.0
DIFF_HEADS = 4
DIFF_HD = 64
DIFF_VD = 2 * DIFF_HD
SWA_HEADS = 16
SWA_KV_HEADS = 4
SWA_HD = 64
WINDOW = 128
BLOCK = 128
D_FF = 4 * D_MODEL
N_EVEN = (DEPTH + 1) // 2
N_ODD = DEPTH // 2
ALPHA = (2 * DEPTH) ** 0.25
BETA = (8 * DEPTH) ** -0.25
LN_EPS = 1e-5
RMS_EPS = 1e-6

EVEN_SIZES = [MLA_Q_LORA, MLA_KV_LORA, MLA_ROPE,
              DIFF_HEADS * 2 * DIFF_HD, DIFF_HEADS * 2 * DIFF_HD, DIFF_HEADS * DIFF_VD]
EVEN_IN = sum(EVEN_SIZES)
EVEN_OUT = MLA_HEADS * MLA_V + DIFF_HEADS * DIFF_VD
ODD_SIZES = [SWA_HEADS * SWA_HD, SWA_KV_HEADS * SWA_HD, SWA_KV_HEADS * SWA_HD]
ODD_IN = sum(ODD_SIZES)
ODD_OUT = SWA_HEADS * SWA_HD

kernel_name = "hybrid_mla_diff_swa_deepnorm_encoder"


def _offsets(sizes):
    return [int(v) for v in np.cumsum(sizes)[:-1]]


def layer_norm(x, g, b):
    xf = x.astype(jnp.float32)
    mu = jnp.mean(xf, axis=-1, keepdims=True)
    var = jnp.mean(jnp.square(xf - mu), axis=-1, keepdims=True)
    y = (xf - mu) * lax.rsqrt(var + LN_EPS) * g.astype(jnp.float32) + b.astype(jnp.float32)
    return y.astype(x.dtype)


def rms_norm(x, g):
    xf = x.astype(jnp.float32)
    y = xf * lax.rsqrt(jnp.mean(xf * xf, axis=-1, keepdims=True) + RMS_EPS)
    return (y * g.astype(jnp.float32)).astype(x.dtype)


def alibi_slopes(n):
    return 2.0 ** (-8.0 * jnp.arange(1, n + 1, dtype=jnp.float32) / n)


def rope(x, pos):
    half = x.shape[-1] // 2
    inv = ROPE_THETA ** (-jnp.arange(half, dtype=jnp.float32) / half)
    ang = pos.astype(jnp.float32)[:, None] * inv[None, :]
    cos = jnp.cos(ang)[:, None, :]
    sin = jnp.sin(ang)[:, None, :]
    x1 = x[..., :half].astype(jnp.float32)
    x2 = x[..., half:].astype(jnp.float32)
    return jnp.concatenate([x1 * cos - x2 * sin, x2 * cos + x1 * sin], axis=-1).astype(x.dtype)


def to_blocks(a):
    B, S = a.shape[:2]
    return a.reshape(B, S // BLOCK, BLOCK, *a.shape[2:]).swapaxes(0, 1)


def from_blocks(o):
    nb, B = o.shape[:2]
    return o.swapaxes(0, 1).reshape(B, nb * BLOCK, *o.shape[3:])


def mla_attention(q_nope, q_rope, k_nope, k_rope, v):
    scale = (MLA_NOPE + MLA_ROPE) ** -0.5

    def block(args):
        qn, qr = args
        s = (jnp.einsum('bqhd,bkhd->bhqk', qn, k_nope)
             + jnp.einsum('bqhr,bkr->bhqk', qr, k_rope)).astype(jnp.float32) * scale
        p = jax.nn.softmax(s, axis=-1)
        return jnp.einsum('bhqk,bkhd->bqhd', p.astype(v.dtype), v)

    return from_blocks(lax.map(block, (to_blocks(q_nope), to_blocks(q_rope))))


def diff_attention(q, k, v, lam):
    S = q.shape[1]
    scale = DIFF_HD ** -0.5
    slopes = alibi_slopes(DIFF_HEADS)
    kpos = jnp.arange(S)
    nb = S // BLOCK

    def block(args):
        i, qb = args
        qpos = i * BLOCK + jnp.arange(BLOCK)
        dist = jnp.abs(qpos[:, None] - kpos[None, :]).astype(jnp.float32)
        bias = -slopes[:, None, None, None] * dist[None, None]
        s = jnp.einsum('bqhmd,bkhmd->bhmqk', qb, k).astype(jnp.float32) * scale + bias
        p = jax.nn.softmax(s, axis=-1)
        a = p[:, :, 0] - lam * p[:, :, 1]
        return jnp.einsum('bhqk,bkhe->bqhe', a.astype(v.dtype), v)

    return from_blocks(lax.map(block, (jnp.arange(nb), to_blocks(q))))


def window_gqa_attention(q, k, v, sink):
    B, S, H, D = q.shape
    G = H // SWA_KV_HEADS
    nb = S // BLOCK
    span = BLOCK + 2 * WINDOW
    scale = D ** -0.5
    slopes = alibi_slopes(H).reshape(SWA_KV_HEADS, G)
    sink_f = sink.astype(jnp.float32).reshape(SWA_KV_HEADS, G, 1, 1)
    kp = jnp.pad(k, ((0, 0), (WINDOW, WINDOW), (0, 0), (0, 0)))
    vp = jnp.pad(v, ((0, 0), (WINDOW, WINDOW), (0, 0), (0, 0)))
    rel = jnp.arange(span)[None, :] - WINDOW - jnp.arange(BLOCK)[:, None]
    dist = jnp.abs(rel).astype(jnp.float32)

    def block(args):
        i, qb = args
        kb = lax.dynamic_slice_in_dim(kp, i * BLOCK, span, axis=1)
        vb = lax.dynamic_slice_in_dim(vp, i * BLOCK, span, axis=1)
        kpos = i * BLOCK - WINDOW + jnp.arange(span)
        valid = (jnp.abs(rel) <= WINDOW) & ((kpos >= 0) & (kpos < S))[None, :]
        qg = qb.reshape(B, BLOCK, SWA_KV_HEADS, G, D)
        s = jnp.einsum('bqngd,bsnd->bngqs', qg, kb).astype(jnp.float32) * scale
        s = s - slopes[:, :, None, None] * dist[None, None]
        s = jnp.where(valid[None, None, None], s, -jnp.inf)
        sink_col = jnp.broadcast_to(sink_f, (B, SWA_KV_HEADS, G, BLOCK, 1))
        p = jax.nn.softmax(jnp.concatenate([s, sink_col], axis=-1), axis=-1)[..., :-1]
        o = jnp.einsum('bngqs,bsnd->bqngd', p.astype(vb.dtype), vb)
        return o.reshape(B, BLOCK, H, D)

    return from_blocks(lax.map(block, (jnp.arange(nb), to_blocks(q))))


def even_mixer(x, w_in, q_norm, kv_norm, w_uq, w_ukv, lam_q1, lam_k1, lam_q2, lam_k2,
               diff_norm, w_out, lambda_init):
    B, S, _ = x.shape
    pos = jnp.arange(S)
    h = x @ w_in
    c_q, c_kv, k_rope, dq, dk, dv = jnp.split(h, _offsets(EVEN_SIZES), axis=-1)
    q = (rms_norm(c_q, q_norm) @ w_uq).reshape(B, S, MLA_HEADS, MLA_NOPE + MLA_ROPE)
    q_nope = q[..., :MLA_NOPE]
    q_rope = rope(q[..., MLA_NOPE:], pos)
    kv = (rms_norm(c_kv, kv_norm) @ w_ukv).reshape(B, S, MLA_HEADS, MLA_NOPE + MLA_V)
    k_nope = kv[..., :MLA_NOPE]
    v_mla = kv[..., MLA_NOPE:]
    k_rope = rope(k_rope[:, :, None, :], pos)[:, :, 0, :]
    o_mla = mla_attention(q_nope, q_rope, k_nope, k_rope, v_mla).reshape(B, S, MLA_HEADS * MLA_V)
    lam = (jnp.exp(jnp.sum(lam_q1.astype(jnp.float32) * lam_k1.astype(jnp.float32)))
           - jnp.exp(jnp.sum(lam_q2.astype(jnp.float32) * lam_k2.astype(jnp.float32)))
           + lambda_init)
    o_diff = diff_attention(dq.reshape(B, S, DIFF_HEADS, 2, DIFF_HD),
                            dk.reshape(B, S, DIFF_HEADS, 2, DIFF_HD),
                            dv.reshape(B, S, DIFF_HEADS, DIFF_VD), lam)
    o_diff = rms_norm(o_diff, diff_norm) * (1.0 - lambda_init)
    o = jnp.concatenate([o_mla, o_diff.reshape(B, S, DIFF_HEADS * DIFF_VD)], axis=-1)
    return o @ w_out


def odd_mixer(x, w_in, sink, w_out):
    B, S, _ = x.shape
    q, k, v = jnp.split(x @ w_in, _offsets(ODD_SIZES), axis=-1)
    o = window_gqa_attention(q.reshape(B, S, SWA_HEADS, SWA_HD),
                             k.reshape(B, S, SWA_KV_HEADS, SWA_HD),
                             v.reshape(B, S, SWA_KV_HEADS, SWA_HD), sink)
    return o.reshape(B, S, ODD_OUT) @ w_out


def sqrelu_mlp(x, w1, b1, w2, b2):
    return jnp.square(jax.nn.relu(x @ w1 + b1)) @ w2 + b2


def setup_inputs(seed: int = 0) -> dict:
    key = jax.random.key(seed)
    ks = jax.random.split(key, 24)

    def nrm(k, shape, scale):
        return jax.random.normal(k, shape, jnp.float32) * scale

    def gain(k, shape):
        return 1.0 + nrm(k, shape, 0.02)

    ev_cols = jnp.concatenate([jnp.ones((EVEN_IN - DIFF_HEADS * DIFF_VD,), jnp.float32),
                               jnp.full((DIFF_HEADS * DIFF_VD,), BETA, jnp.float32)])
    ukv_cols = jnp.tile(jnp.concatenate([jnp.ones((MLA_NOPE,), jnp.float32),
                                         jnp.full((MLA_V,), BETA, jnp.float32)]), MLA_HEADS)
    od_cols = jnp.concatenate([jnp.ones((ODD_IN - SWA_KV_HEADS * SWA_HD,), jnp.float32),
                               jnp.full((SWA_KV_HEADS * SWA_HD,), BETA, jnp.float32)])
    return {
        "x": nrm(ks[0], (BATCH, SEQ, D_MODEL), 1.0),
        "ev_w_in": nrm(ks[1], (N_EVEN, D_MODEL, EVEN_IN), D_MODEL ** -0.5) * ev_cols,
        "ev_q_norm": gain(ks[2], (N_EVEN, MLA_Q_LORA)),
        "ev_kv_norm": gain(ks[3], (N_EVEN, MLA_KV_LORA)),
        "ev_w_uq": nrm(ks[4], (N_EVEN, MLA_Q_LORA, MLA_HEADS * (MLA_NOPE + MLA_ROPE)), MLA_Q_LORA ** -0.5),
        "ev_w_ukv": nrm(ks[5], (N_EVEN, MLA_KV_LORA, MLA_HEADS * (MLA_NOPE + MLA_V)), MLA_KV_LORA ** -0.5) * ukv_cols,
        "ev_lam_q1": nrm(ks[6], (N_EVEN, DIFF_HD), 0.1),
        "ev_lam_k1": nrm(ks[7], (N_EVEN, DIFF_HD), 0.1),
        "ev_lam_q2": nrm(ks[8], (N_EVEN, DIFF_HD), 0.1),
        "ev_lam_k2": nrm(ks[9], (N_EVEN, DIFF_HD), 0.1),
        "ev_diff_norm": gain(ks[10], (N_EVEN, DIFF_VD)),
        "ev_w_out": nrm(ks[11], (N_EVEN, EVEN_OUT, D_MODEL), BETA * EVEN_OUT ** -0.5),
        "od_w_in": nrm(ks[12], (N_ODD, D_MODEL, ODD_IN), D_MODEL ** -0.5) * od_cols,
        "od_sink": nrm(ks[13], (N_ODD, SWA_HEADS), 0.5),
        "od_w_out": nrm(ks[14], (N_ODD, ODD_OUT, D_MODEL), BETA * ODD_OUT ** -0.5),
        "ln1_g": gain(ks[15], (DEPTH, D_MODEL)),
        "ln1_b": nrm(ks[16], (DEPTH, D_MODEL), 0.02),
        "ln2_g": gain(ks[17], (DEPTH, D_MODEL)),
        "ln2_b": nrm(ks[18], (DEPTH, D_MODEL), 0.02),
        "ffn_w1": nrm(ks[19], (DEPTH, D_MODEL, D_FF), D_MODEL ** -0.5),
        "ffn_b1": nrm(ks[20], (DEPTH, D_FF), 0.02),
        "ffn_w2": nrm(ks[21], (DEPTH, D_FF, D_MODEL), BETA * D_FF ** -0.5),
        "ffn_b2": nrm(ks[22], (DEPTH, D_MODEL), 0.02),
    }


def reference(x, ev_w_in, ev_q_norm, ev_kv_norm, ev_w_uq, ev_w_ukv, ev_lam_q1, ev_lam_k1,
              ev_lam_q2, ev_lam_k2, ev_diff_norm, ev_w_out, od_w_in, od_sink, od_w_out,
              ln1_g, ln1_b, ln2_g, ln2_b, ffn_w1, ffn_b1, ffn_w2, ffn_b2):
    for layer in range(DEPTH):
        j = layer // 2
        if layer % 2 == 0:
            lambda_init = 0.8 - 0.6 * math.exp(-0.3 * layer)
            y = even_mixer(x, ev_w_in[j], ev_q_norm[j], ev_kv_norm[j], ev_w_uq[j], ev_w_ukv[j],
                           ev_lam_q1[j], ev_lam_k1[j], ev_lam_q2[j], ev_lam_k2[j],
                           ev_diff_norm[j], ev_w_out[j], lambda_init)
        else:
            y = odd_mixer(x, od_w_in[j], od_sink[j], od_w_out[j])
        x = layer_norm(ALPHA * x + y, ln1_g[layer], ln1_b[layer])
        x = layer_norm(ALPHA * x + sqrelu_mlp(x, ffn_w1[layer], ffn_b1[layer], ffn_w2[layer], ffn_b2[layer]),
                       ln2_g[layer], ln2_b[layer])
    return x
```

```python
import contextlib
import math
import numpy as np
import concourse.bass as bass
import concourse.mybir as mybir
from concourse.bass_utils import run_bass_kernel_spmd

F32 = mybir.dt.float32
BF16 = mybir.dt.bfloat16
AF = mybir.ActivationFunctionType
ALU = mybir.AluOpType

D = 1024
SEQ = 2048
NSEQ = 2
DFF = 4096
ALPHA = 4 ** 0.25
LN_EPS = 1e-5
RMS_EPS = 1e-6
EVEN_IN = 2208
ODD_IN = 1536
LAMBDA_INIT0 = 0.8 - 0.6 * math.exp(0.0)


class _Op:
    __slots__ = ("eng", "fn", "deps", "dma", "sig", "needed", "idx")

    def __init__(self, eng, fn, deps, dma, idx):
        self.eng, self.fn, self.deps, self.dma, self.idx = eng, fn, deps, dma, idx
        self.sig = None
        self.needed = False


class Sched:
    ENGS = ("pe", "act", "dve", "pool", "sp")

    def __init__(self, nc, n_dma_sems=32):
        self.nc = nc
        self.ops = []
        self.last_w = {}
        self.readers = {}
        self.n_dma_sems = n_dma_sems
        self.last_barrier = 0
        self.persist_ops = set()
        self.persist_w = {}

    def add(self, eng, fn, reads=(), writes=(), dma=False, persist=False):
        idx = len(self.ops)
        if persist:
            self.persist_ops.add(idx)
            for r in writes:
                self.persist_w[r] = idx
        deps = set()
        for r in reads:
            w = self.last_w.get(r)
            if w is not None:
                deps.add(w)
        for r in writes:
            w = self.last_w.get(r)
            if w is not None:
                deps.add(w)
            deps.update(self.readers.get(r, ()))
        for r in reads:
            self.readers.setdefault(r, []).append(idx)
        for r in writes:
            self.last_w[r] = idx
            self.readers[r] = []
        deps.discard(idx)
        self.ops.append(_Op(eng, fn, deps, dma, idx))
        return idx

    def barrier(self):
        last = {}
        for op in self.ops:
            if not op.dma and op.fn is not None:
                last[op.eng] = op.idx
        deps = set(last.values())
        deps.update(op.idx for op in self.ops[self.last_barrier:] if op.dma and op.idx not in self.persist_ops)
        for e in self.ENGS:
            idx = len(self.ops)
            self.ops.append(_Op(e, None, set(deps), False, idx))
        self.last_barrier = len(self.ops)
        self.last_w = dict(self.persist_w)
        self.readers = {}

    @staticmethod
    def _skip(dop, op):
        return dop.eng == "pe" and op.eng == "pe" and not dop.dma and not op.dma

    def emit(self, es, final_wait_ops=()):
        nc = self.nc
        ops = self.ops
        for op in ops:
            for d in op.deps:
                if not self._skip(ops[d], op):
                    ops[d].needed = True
        for i in final_wait_ops:
            ops[i].needed = True
        eng_sem = {e: es.enter_context(nc.semaphore("s_" + e)) for e in self.ENGS}
        dma_sems = [es.enter_context(nc.semaphore("d%d" % i)) for i in range(self.n_dma_sems)]
        eng_cnt = {e: 0 for e in self.ENGS}
        dma_cnt = [0] * self.n_dma_sems
        rr = 0
        for op in ops:
            if op.dma:
                s = rr % self.n_dma_sems
                rr += 1
                prev = dma_cnt[s]
                dma_cnt[s] += 16
                op.sig = (dma_sems[s], dma_cnt[s], prev)
            elif op.needed:
                eng_cnt[op.eng] += 1
                op.sig = (eng_sem[op.eng], eng_cnt[op.eng], None)
        self.counts = dict(eng_cnt)
        by_eng = {e: [op for op in ops if op.eng == e] for e in self.ENGS}
        block = es.enter_context(nc.Block())

        def replay(ename, e):
            known = {}

            def wait(sem, val):
                if val <= 0 or known.get(sem.num, 0) >= val:
                    return
                e.wait_ge(sem, val)
                known[sem.num] = val

            for op in by_eng[ename]:
                for d in sorted(op.deps):
                    dop = ops[d]
                    if self._skip(dop, op):
                        continue
                    wait(dop.sig[0], dop.sig[1])
                if op.fn is None:
                    continue
                if op.dma:
                    wait(op.sig[0], op.sig[2])
                ins = op.fn(e)
                if op.dma:
                    ins.then_inc(op.sig[0], 16)
                elif op.sig is not None:
                    ins.then_inc(op.sig[0], 1)
            if ename == "sp":
                for i in final_wait_ops:
                    wait(ops[i].sig[0], ops[i].sig[1])

        @block.sync
        def _(e):
            replay("sp", e)

        @block.tensor
        def _(e):
            replay("pe", e)

        @block.scalar
        def _(e):
            replay("act", e)

        @block.vector
        def _(e):
            replay("dve", e)

        @block.gpsimd
        def _(e):
            replay("pool", e)


class Arena:
    def __init__(self, nc, es, name, nbytes):
        self.nbytes = nbytes
        self.t = es.enter_context(nc.sbuf_tensor(name, [128, nbytes // 2], BF16))
        self.off = 0

    def reset(self):
        self.off = 0

    def take(self, shape, dtype):
        n = 1
        for s in shape:
            n *= s
        nb = n * (4 if dtype == F32 else 2)
        nb = (nb + 31) // 32 * 32
        assert self.off + nb <= self.nbytes, ("arena overflow", self.off, nb, self.nbytes)
        ap = self.t[:, self.off // 2:(self.off + nb) // 2]
        self.off += nb
        if dtype == F32:
            ap = ap.bitcast(F32)
        ap = ap[:, 0:n]
        if len(shape) == 2:
            ap = ap.rearrange("p (a b) -> p a b", a=shape[0])
        elif len(shape) == 3:
            ap = ap.rearrange("p (a b c) -> p a b c", a=shape[0], b=shape[1])
        return ap


def build_program(layers=(0, 1), nseq=NSEQ):
    nc = bass.Bass("TRN2", target_bir_lowering=False)

    def din(name, shape):
        return nc.dram_tensor(name, list(shape), F32, kind="ExternalInput").ap()

    x_d = din("x", [nseq, SEQ, D])
    ev_w_in = din("ev_w_in", [D, EVEN_IN])
    ev_q_norm = din("ev_q_norm", [128, 3])
    ev_kv_norm = din("ev_kv_norm", [128, 2])
    ev_w_uq = din("ev_w_uq", [384, 768])
    ev_w_ukv = din("ev_w_ukv", [256, 1024])
    lamv = din("ev_lam", [128, 256])
    ev_diff_norm = din("ev_diff_norm", [128, 1])
    ev_w_out = din("ev_w_out", [D, D])
    od_w_in = din("od_w_in", [D, ODD_IN])
    od_sink = din("od_sink", [128, 16])
    od_w_out = din("od_w_out", [D, D])
    ln_p = din("ln_p", [128, 64])
    ffn_w1 = din("ffn_w1", [2, D, DFF])
    ffn_b1 = din("ffn_b1", [128, 64])
    ffn_w2 = din("ffn_w2", [2, DFF, D])
    ffn_b2 = din("ffn_b2", [128, 16])
    c_ident = din("c_ident", [128, 128])
    c_rope = din("c_rope", [2, 96, SEQ])
    c_dtab = din("c_dtab", [5, 128, 512])
    c_swa = din("c_swa", [128, 384])
    out_d = nc.dram_tensor("out", [nseq, SEQ, D], F32, kind="ExternalOutput").ap()
    w1t = nc.dram_tensor("w1t", [2, 8, 128, 8 * 512], BF16, kind="Internal").ap()
    w2t = nc.dram_tensor("w2t", [2, 8, 128, 32 * 128], BF16, kind="Internal").ap()

    es = contextlib.ExitStack()
    S = Sched(nc)
    with es:
        sb = lambda name, shape, dt: es.enter_context(nc.sbuf_tensor(name, shape, dt))
        xhi = sb("xhi", [128, 8, SEQ], BF16)
        xlo = sb("xlo", [128, 8, SEQ], BF16)
        obuf = sb("obuf", [128, 8, SEQ], BF16)
        identb = sb("identb", [128, 128], BF16)
        onesb = sb("onesb", [128, 128], BF16)
        alphaI = sb("alphaI", [128, 128], BF16)
        epsc = sb("epsc", [128, 2], F32)
        lnp = sb("lnp", [128, 4, 2, 8], F32)
        b1c = sb("b1c", [128, 2, 32], F32)
        b2c = sb("b2c", [128, 2, 8], F32)
        qng = sb("qng", [128, 3], F32)
        kvng = sb("kvng", [128, 2], F32)
        dng = sb("dng", [128, 1], F32)
        lamt = sb("lamt", [128, 4, 64], F32)
        lamw = sb("lamw", [128, 8], F32)
        sinkt = sb("sinkt", [128, 16], F32)
        arena = Arena(nc, es, "arena", 106 * 1024)
        ps = [es.enter_context(nc.psum_tensor("ps%d" % i, [128, 512], F32)) for i in range(8)]
        PS = lambda i: ("ps", i)

        def dma(eng, out, in_, reads=(), writes=()):
            return S.add(eng, lambda e: e.dma_start(out=out, in_=in_), reads=reads, writes=writes, dma=True)

        dma("pool", identb[:], c_ident, writes=["identb"])
        S.add("dve", lambda e: e.memset(onesb[:], 1.0), writes=["onesb"])
        S.add("act", lambda e: e.activation(out=alphaI[:], in_=identb[:], func=AF.Copy, scale=ALPHA), reads=["identb"], writes=["alphaI"])
        S.add("dve", lambda e: e.memset(epsc[:, 0:1], LN_EPS), writes=["epsc"])
        S.add("dve", lambda e: e.memset(epsc[:, 1:2], RMS_EPS), writes=["epsc"])
        dma("sp", lnp[:].rearrange("p l t c -> p (l t c)"), ln_p, writes=["lnp"])
        dma("sp", b1c[:].rearrange("p l c -> p (l c)"), ffn_b1, writes=["b1c"])
        dma("sp", b2c[:].rearrange("p l c -> p (l c)"), ffn_b2, writes=["b2c"])
        dma("sp", qng[:], ev_q_norm, writes=["qng"])
        dma("sp", kvng[:], ev_kv_norm, writes=["kvng"])
        dma("sp", dng[:], ev_diff_norm, writes=["dng"])
        dma("sp", lamt[:].rearrange("p a b -> p (a b)"), lamv, writes=["lamt"])
        dma("sp", sinkt[:], od_sink, writes=["sinkt"])
        S.add("dve", lambda e: e.tensor_tensor(out=lamt[:, 0, :], in0=lamt[:, 0, :], in1=lamt[:, 1, :], op=ALU.mult), writes=["lamt"])
        S.add("dve", lambda e: e.tensor_tensor(out=lamt[:, 2, :], in0=lamt[:, 2, :], in1=lamt[:, 3, :], op=ALU.mult), writes=["lamt"])
        S.add("dve", lambda e: e.reduce_sum(out=lamw[:, 0:1], in_=lamt[:, 0, :], axis=mybir.AxisListType.X), reads=["lamt"], writes=["lamw"])
        S.add("dve", lambda e: e.reduce_sum(out=lamw[:, 1:2], in_=lamt[:, 2, :], axis=mybir.AxisListType.X), reads=["lamt"], writes=["lamw"])
        S.add("act", lambda e: e.activation(out=lamw[:, 2:4], in_=lamw[:, 0:2], func=AF.Exp), writes=["lamw"])
        S.add("dve", lambda e: e.tensor_tensor(out=lamw[:, 4:5], in0=lamw[:, 3:4], in1=lamw[:, 2:3], op=ALU.subtract), writes=["lamw"])
        S.add("dve", lambda e: e.tensor_scalar(out=lamw[:, 4:5], in0=lamw[:, 4:5], scalar1=-LAMBDA_INIT0, scalar2=None, op0=ALU.add), writes=["lamw"])
        S.add("dve", lambda e: e.tensor_scalar(out=lamw[:, 5:6], in0=dng[:, 0:1], scalar1=1.0 - LAMBDA_INIT0, scalar2=None, op0=ALU.mult), reads=["dng"], writes=["lamw"])
        S.add("act", lambda e: e.activation(out=sinkt[:], in_=sinkt[:], func=AF.Exp), writes=["sinkt"])

        def mm(out, lhsT, rhs, start, stop, reads, bank):
            return S.add("pe", lambda e: e.matmul(out, lhsT=lhsT, rhs=rhs, start=start, stop=stop), reads=reads, writes=[PS(bank)])

        def tg_sl(tg):
            return slice(tg * 512, (tg + 1) * 512)

        def load_w(eng_out, src, writes):
            return dma("pool", eng_out, src, writes=writes)

        conv_done = set()
        conv_cnt = [0]

        def ensure_conv(l):
            if l in conv_done:
                return
            conv_done.add(l)
            def chain():
                i = conv_cnt[0]
                conv_cnt[0] += 1
                return ([("conv", i - 6)] if i >= 6 else []), ("conv", i)
            for hb in range(8):
                rd, wr = chain()
                S.add("pool", lambda e, hb=hb: e.dma_start(out=w1t[l, hb].rearrange("p (c n) -> p c n", c=8),
                                                            in_=ffn_w1[l, :, hb * 512:(hb + 1) * 512].rearrange("(c p) n -> p c n", p=128)),
                      reads=rd, writes=[("w1t", l, hb), wr], dma=True, persist=True)
            for m in range(8):
                for part in range(4):
                    rd, wr = chain()
                    S.add("pool", lambda e, m=m, part=part: e.dma_start(
                        out=w2t[l, m].rearrange("p (k n) -> p k n", k=32)[:, part * 8:(part + 1) * 8, :],
                        in_=ffn_w2[l, part * 1024:(part + 1) * 1024, m * 128:(m + 1) * 128].rearrange("(k p) n -> p k n", p=128)),
                        reads=rd, writes=[("w2t", l, m, part), wr], dma=True, persist=True)

        def load_x(s):
            arena.reset()
            xin = [arena.take([D], F32) for _ in range(4)]
            hit = [arena.take([D], BF16) for _ in range(4)]
            lot = [arena.take([D], BF16) for _ in range(4)]
            for tt in range(16):
                b = tt % 4
                tg = tt // 4
                dma("sp" if tt % 2 == 0 else "pool", xin[b], x_d[s, tt * 128:(tt + 1) * 128, :], writes=[("xin", b)])
                S.add("act", lambda e, b=b: e.activation(out=hit[b], in_=xin[b], func=AF.Copy), reads=[("xin", b)], writes=[("hit", b)])
                S.add("dve", lambda e, b=b: e.tensor_tensor(out=lot[b], in0=xin[b], in1=hit[b], op=ALU.subtract),
                      reads=[("xin", b), ("hit", b)], writes=[("lot", b)])
                for (src, dst, nm, bank) in ((hit, xhi, "xh", 0 + 2 * b), (lot, xlo, "xl", 1 + 2 * b)):
                    pb = ps[bank][:].bitcast(BF16)
                    for c in range(8):
                        S.add("pe", lambda e, c=c, pb=pb, src=src, b=b: e.transpose(pb[:, c * 128:(c + 1) * 128], src[b][:, c * 128:(c + 1) * 128], identb[:]),
                              reads=[(nm[1] == "h" and ("hit", b) or ("lot", b)), "identb"], writes=[PS(bank)])
                    S.add("dve" if nm == "xh" else "act",
                          (lambda e, pb=pb, dst=dst, tt=tt: e.tensor_copy(out=dst[:, :, tt * 128:(tt + 1) * 128], in_=pb.rearrange("p (c t) -> p c t", c=8)))
                          if nm == "xh" else
                          (lambda e, pb=pb, dst=dst, tt=tt: e.activation(out=dst[:, :, tt * 128:(tt + 1) * 128], in_=pb.rearrange("p (c t) -> p c t", c=8), func=AF.Copy)),
                          writes=[PS(bank)] + [(nm, c, tg) for c in range(8)])

        def store_x(s):
            arena.reset()
            ta = [arena.take([D], F32) for _ in range(4)]
            to = [arena.take([D], F32) for _ in range(4)]
            outs = []
            for tt in range(16):
                b = tt % 4
                tg = tt // 4
                for (src, nm, bank) in ((xhi, "xh", 0 + 2 * b), (xlo, "xl", 1 + 2 * b)):
                    pb = ps[bank][:].bitcast(BF16)
                    for c in range(8):
                        S.add("pe", lambda e, c=c, pb=pb, src=src, tt=tt: e.transpose(pb[:, c * 128:(c + 1) * 128], src[:, c, tt * 128:(tt + 1) * 128], identb[:]),
                              reads=[(nm, c, tg), "identb"], writes=[PS(bank)])
                pa = ps[0 + 2 * b][:].bitcast(BF16)
                pl = ps[1 + 2 * b][:].bitcast(BF16)
                S.add("act", lambda e, b=b, pa=pa: e.activation(out=ta[b], in_=pa, func=AF.Copy), writes=[PS(0 + 2 * b), ("ta", b)])
                S.add("dve", lambda e, b=b, pl=pl: e.tensor_tensor(out=to[b], in0=pl, in1=ta[b], op=ALU.add),
                      reads=[("ta", b)], writes=[PS(1 + 2 * b), ("to", b)])
                outs.append(dma("sp" if tt % 2 == 0 else "pool", out_d[s, tt * 128:(tt + 1) * 128, :], to[b], reads=[("to", b)]))
            return outs

        def store_load_x(s_out, s_in):
            arena.reset()
            ta = [arena.take([D], F32) for _ in range(2)]
            to = [arena.take([D], F32) for _ in range(2)]
            xin = [arena.take([D], F32) for _ in range(2)]
            hit = [arena.take([D], BF16) for _ in range(2)]
            lot = [arena.take([D], BF16) for _ in range(2)]
            outs = []
            for tt in range(16):
                b = tt % 2
                tg = tt // 4
                cols = slice(tt * 128, (tt + 1) * 128)
                dma("pool", xin[b], x_d[s_in, cols, :], writes=[("xin", b)])
                for (src, nm, bank) in ((xhi, "xh", 0 + 2 * b), (xlo, "xl", 1 + 2 * b)):
                    pb = ps[bank][:].bitcast(BF16)
                    for c in range(8):
                        S.add("pe", lambda e, c=c, pb=pb, src=src, cols=cols: e.transpose(pb[:, c * 128:(c + 1) * 128], src[:, c, cols], identb[:]),
                              reads=[(nm, "t", tt), "identb"], writes=[PS(bank)])
                pa = ps[0 + 2 * b][:].bitcast(BF16)
                pl = ps[1 + 2 * b][:].bitcast(BF16)
                S.add("act", lambda e, b=b, pa=pa: e.activation(out=ta[b], in_=pa, func=AF.Copy), writes=[PS(0 + 2 * b), ("ta", b)])
                S.add("dve", lambda e, b=b, pl=pl: e.tensor_tensor(out=to[b], in0=pl, in1=ta[b], op=ALU.add),
                      reads=[("ta", b)], writes=[PS(1 + 2 * b), ("to", b)])
                outs.append(dma("sp", out_d[s_out, cols, :], to[b], reads=[("to", b)]))
                S.add("act", lambda e, b=b: e.activation(out=hit[b], in_=xin[b], func=AF.Copy), reads=[("xin", b)], writes=[("hit", b)])
                S.add("dve", lambda e, b=b: e.tensor_tensor(out=lot[b], in0=xin[b], in1=hit[b], op=ALU.subtract),
                      reads=[("xin", b), ("hit", b)], writes=[("lot", b)])
                for (src, dst, nm, bank, eng) in ((hit, xhi, "xh", 4 + 2 * b, "dve"), (lot, xlo, "xl", 5 + 2 * b, "act")):
                    pb = ps[bank][:].bitcast(BF16)
                    for c in range(8):
                        S.add("pe", lambda e, c=c, pb=pb, src=src, b=b: e.transpose(pb[:, c * 128:(c + 1) * 128], src[b][:, c * 128:(c + 1) * 128], identb[:]),
                              reads=[(("hit", b) if nm == "xh" else ("lot", b)), "identb"], writes=[PS(bank)])
                    pv = pb.rearrange("p (c t) -> p c t", c=8)
                    if eng == "dve":
                        S.add("dve", lambda e, pv=pv, dst=dst, cols=cols: e.tensor_copy(out=dst[:, :, cols], in_=pv),
                              writes=[PS(bank), (nm, "t", tt)] + [(nm, c, tg) for c in range(8)])
                    else:
                        S.add("act", lambda e, pv=pv, dst=dst, cols=cols: e.activation(out=dst[:, :, cols], in_=pv, func=AF.Copy),
                              writes=[PS(bank), (nm, "t", tt)] + [(nm, c, tg) for c in range(8)])
            return outs

        class LNState:
            pass

        def ln_alloc():
            st = LNState()
            st.r = arena.take([8, 512], F32)
            st.rb = [arena.take([512], BF16) for _ in range(3)]
            st.rsq = [arena.take([512], BF16) for _ in range(3)]
            st.mean = arena.take([512], F32)
            st.var = arena.take([512], F32)
            st.rstd = arena.take([512], F32)
            st.nmr = arena.take([512], F32)
            st.v = [arena.take([512], F32) for _ in range(2)]
            st.nb = [arena.take([512], F32) for _ in range(2)]
            st.cnt = 0
            return st

        def ln_resid_chunk(st, m, tg, y_bank, bias_col, b_s1, b_s2):
            sl = tg_sl(tg)
            i = st.cnt % 3
            st.cnt += 1
            rm = st.r[:, m, :]
            if bias_col is not None:
                S.add("act", lambda e: e.activation(out=rm, in_=ps[y_bank][:], func=AF.Identity, bias=bias_col, scale=1.0),
                      reads=["b2c"], writes=[PS(y_bank), ("r", m)])
                S.add("dve", lambda e: e.scalar_tensor_tensor(out=rm, in0=xhi[:, m, sl], scalar=ALPHA, in1=rm, op0=ALU.mult, op1=ALU.add),
                      reads=[("xh", m, tg)], writes=[("r", m)])
                S.add("dve", lambda e: e.scalar_tensor_tensor(out=rm, in0=xlo[:, m, sl], scalar=ALPHA, in1=rm, op0=ALU.mult, op1=ALU.add),
                      reads=[("xl", m, tg)], writes=[("r", m)])
            else:
                S.add("dve", lambda e: e.scalar_tensor_tensor(out=rm, in0=xhi[:, m, sl], scalar=ALPHA, in1=ps[y_bank][:], op0=ALU.mult, op1=ALU.add),
                      reads=[("xh", m, tg)], writes=[PS(y_bank), ("r", m)])
            S.add("act", lambda e: e.activation(out=st.rb[i], in_=rm, func=AF.Copy), reads=[("r", m)], writes=[("rb", i)])
            S.add("act", lambda e: e.activation(out=st.rsq[i], in_=rm, func=AF.Square), reads=[("r", m)], writes=[("rsq", i)])
            def emit_stats():
                mm(ps[b_s1][:], onesb[:], st.rb[i], m == 0, m == 7, ["onesb", ("rb", i)], b_s1)
                mm(ps[b_s2][:], onesb[:], st.rsq[i], m == 0, m == 7, ["onesb", ("rsq", i)], b_s2)
            return emit_stats

        def ln_finish(st, ln_idx, tg, b_s1, b_s2):
            sl = tg_sl(tg)

            def stats():
                S.add("dve", lambda e: e.tensor_scalar(out=st.mean, in0=ps[b_s1][:], scalar1=1.0 / D, scalar2=None, op0=ALU.mult),
                      writes=[PS(b_s1), "mean"])
                S.add("act", lambda e: e.activation(out=st.var, in_=st.mean, func=AF.Square), reads=["mean"], writes=["var"])
                S.add("dve", lambda e: e.scalar_tensor_tensor(out=st.var, in0=ps[b_s2][:], scalar=1.0 / D, in1=st.var, op0=ALU.mult, op1=ALU.subtract),
                      writes=[PS(b_s2), "var"])
                S.add("act", lambda e: e.activation(out=st.var, in_=st.var, func=AF.Sqrt, bias=epsc[:, 0:1], scale=1.0), reads=["epsc"], writes=["var"])
                S.add("dve", lambda e: e.reciprocal(out=st.rstd, in_=st.var), reads=["var"], writes=["rstd"])
                S.add("dve", lambda e: e.scalar_tensor_tensor(out=st.nmr, in0=st.mean, scalar=-1.0, in1=st.rstd, op0=ALU.mult, op1=ALU.mult),
                      reads=["mean", "rstd"], writes=["nmr"])

            def chunk(m):
                i = m % 2
                g = lnp[:, ln_idx, 0, m:m + 1]
                bcol = lnp[:, ln_idx, 1, m:m + 1]
                rm = st.r[:, m, :]
                S.add("dve", lambda e: e.scalar_tensor_tensor(out=st.v[i], in0=rm, scalar=g, in1=st.rstd, op0=ALU.mult, op1=ALU.mult),
                      reads=[("r", m), "rstd", "lnp"], writes=[("v", i)])
                S.add("act", lambda e: e.activation(out=st.nb[i], in_=st.nmr, func=AF.Identity, bias=bcol, scale=g),
                      reads=["nmr", "lnp"], writes=[("nb", i)])
                S.add("dve", lambda e: e.tensor_tensor(out=st.v[i], in0=st.v[i], in1=st.nb[i], op=ALU.add),
                      reads=[("nb", i)], writes=[("v", i)])
                S.add("act", lambda e: e.activation(out=xhi[:, m, sl], in_=st.v[i], func=AF.Copy), reads=[("v", i)], writes=[("xh", m, tg)])
                S.add("dve", lambda e: e.tensor_tensor(out=xlo[:, m, sl], in0=st.v[i], in1=xhi[:, m, sl], op=ALU.subtract),
                      reads=[("v", i), ("xh", m, tg)], writes=[("xl", m, tg)])

            steps = [stats]
            for m in range(8):
                steps.append(lambda m=m: chunk(m))
            return steps

        def out_proj_ln(w_out_d, ln_idx):
            S.barrier()
            arena.reset()
            st = ln_alloc()
            st.end = arena.off
            wo = arena.take([8, D], BF16)
            for c in range(8):
                load_w(wo[:, c, :], w_out_d[c * 128:(c + 1) * 128, :], [("wo", c)])
            fsteps = []
            for tg in range(4):
                sl = tg_sl(tg)
                pend = None
                for m in range(8):
                    bank = m % 4
                    for c in range(8):
                        mm(ps[bank][:], wo[:, c, m * 128:(m + 1) * 128], obuf[:, c, sl], c == 0, False, [("wo", c), ("ob", c, tg)], bank)
                    mm(ps[bank][:], alphaI[:], xlo[:, m, sl], False, True, ["alphaI", ("xl", m, tg)], bank)
                    if pend is not None:
                        pend()
                    if fsteps:
                        if m == 0:
                            fsteps.pop(0)()
                        fsteps.pop(0)()
                    pend = ln_resid_chunk(st, m, tg, bank, None, 4, 5)
                pend()
                assert not fsteps
                fsteps = ln_finish(st, ln_idx, tg, 4, 5)
            return st, fsteps

        def ffn_ln(layer, ln_idx, st, pending):
            ensure_conv(layer)
            arena.off = st.end
            w1b = [arena.take([8, 512], BF16) for _ in range(2)]
            h1 = arena.take([32, 512], BF16)
            w2b = [arena.take([32, 128], BF16) for _ in range(2)]
            tmp = [arena.take([512], F32) for _ in range(2)]
            tmp += [obuf[:, 7, i * 1024:(i + 1) * 1024].bitcast(F32) for i in range(2)]
            tmp_first = set()
            nw1 = 0
            nw2 = 0
            fsteps = list(pending)
            for tg in range(4):
                sl = tg_sl(tg)
                for hb in range(8):
                    wb_i = nw1 % 2
                    nw1 += 1
                    dma("sp", w1b[wb_i], w1t[layer, hb].rearrange("p (c n) -> p c n", c=8), reads=[("w1t", layer, hb)],
                        writes=[("w1b", wb_i)] + ([("wo", c) for c in range(8)] if nw1 <= 2 else []))
                    for mc in range(4):
                        ch = hb * 4 + mc
                        bank = (0, 1, 6, 7)[ch % 4]
                        for k in range(8):
                            mm(ps[bank][:], w1b[wb_i][:, k, mc * 128:(mc + 1) * 128], xhi[:, k, sl], k == 0, k == 7,
                               [("w1b", wb_i), ("xh", k, tg)], bank)
                        ti = ch % 4
                        extra = []
                        if ti >= 2 and ti not in tmp_first:
                            tmp_first.add(ti)
                            extra = [("ob", 7, tg_) for tg_ in range(4)]
                        S.add("act", lambda e, bank=bank, ti=ti, ch=ch: e.activation(out=tmp[ti], in_=ps[bank][:], func=AF.Identity,
                                                                                    bias=b1c[:, layer, ch:ch + 1], scale=1.0),
                              reads=["b1c"], writes=[PS(bank), ("tmp", ti)] + extra)
                        S.add("dve", lambda e, ti=ti, ch=ch: e.scalar_tensor_tensor(out=h1[:, ch, :], in0=tmp[ti], scalar=0.0, in1=tmp[ti],
                                                                                   op0=ALU.max, op1=ALU.mult),
                              reads=[("tmp", ti)], writes=[("h1", ch)])
                        if fsteps and ch % 3 == 2:
                            fsteps.pop(0)()
                while fsteps:
                    fsteps.pop(0)()
                pend = None
                for m in range(8):
                    wb_i = nw2 % 2
                    nw2 += 1
                    dma("pool", w2b[wb_i], w2t[layer, m].rearrange("p (k n) -> p k n", k=32),
                        reads=[("w2t", layer, m, part) for part in range(4)], writes=[("w2b", wb_i)])
                    bank = 2 + m % 2
                    for k in range(32):
                        mm(ps[bank][:], w2b[wb_i][:, k, :], h1[:, k, :], k == 0, k == 31, [("w2b", wb_i), ("h1", k)], bank)
                    if pend is not None:
                        pend()
                    pend = ln_resid_chunk(st, m, tg, bank, b2c[:, layer, m:m + 1], 4, 5)
                pend()
                assert not fsteps
                fsteps = ln_finish(st, ln_idx, tg, 4, 5)
            for f in fsteps:
                f()

        class AttnBufs:
            pass

        def attn_alloc(n_pt=6, need_tmp=False):
            ab = AttnBufs()
            ab.pt = [arena.take([512], BF16) for _ in range(n_pt)]
            ab.tmp = [arena.take([512], F32) for _ in range(2)] if need_tmp else None
            ab.rz = [arena.take([512], F32) for _ in range(2)]
            ab.oc = [arena.take([512], F32) for _ in range(2)]
            ab.zc = [arena.take([512], F32) for _ in range(2)]
            ab.n = 0
            ab.nt = 0
            return ab

        def attn_tiles(ab, tiles, o_bank, z_bank):
            attn_run(ab, [(tiles, o_bank, z_bank)])

        def attn_run(ab, streams, hooks=None, SB=(0, 1, 6)):
            merged = []
            mx = max(len(t) for t, _, _ in streams)
            for i in range(mx):
                for (tl, ob, zb) in streams:
                    if i < len(tl):
                        t = dict(tl[i])
                        t["ob"], t["zb"] = ob, zb
                        t["first"] = (i == 0)
                        t["last"] = (i == len(tl) - 1)
                        merged.append(t)
            tiles = merged
            nt = len(tiles)
            NSB = len(SB)
            LA = NSB - 1

            def score(i):
                t = tiles[i]
                bank = SB[(ab.n + i) % NSB]
                mm(ps[bank][:, 0:t["n1"] - t["n0"]], t["kT"], t["q"], True, True, t["kreads"], bank)

            for i in range(min(LA, nt)):
                score(i)
            hooks = list(hooks or [])
            for i, t in enumerate(tiles):
                if i + LA < nt:
                    score(i + LA)
                if hooks and i >= 3 and (i - 3) % 4 == 0:
                    hooks.pop(0)()
                bank = SB[(ab.n + i) % NSB]
                n = t["n1"] - t["n0"]
                pi = (ab.n + i) % len(ab.pt)
                pt = ab.pt[pi][:, 0:n]
                if t.get("dtab") is not None:
                    ti = ab.nt % 2
                    ab.nt += 1
                    tmp = ab.tmp[ti][:, 0:n]
                    S.add("dve", lambda e, t=t, tmp=tmp, bank=bank, n=n: e.scalar_tensor_tensor(
                        out=tmp, in0=t["dtab"], scalar=t["dscale"], in1=ps[bank][:, 0:n], op0=ALU.mult, op1=ALU.add),
                        reads=["dtab"], writes=[PS(bank), ("atmp", ti)])
                    S.add("act", lambda e, t=t, tmp=tmp, pt=pt: e.activation(out=pt, in_=tmp, func=AF.Exp, scale=t["scale"], bias=t.get("cbias", 0.0)),
                          reads=[("atmp", ti)], writes=[("pt", pi)])
                else:
                    S.add("act", lambda e, t=t, pt=pt, bank=bank, n=n: e.activation(out=pt, in_=ps[bank][:, 0:n], func=AF.Exp, scale=t["scale"]),
                          writes=[PS(bank), ("pt", pi)])
                mm(ps[t["ob"]][:, t["n0"]:t["n1"]], t["v"], pt, t["first"], t["last"], t["vreads"] + [("pt", pi)], t["ob"])
                mm(ps[t["zb"]][:, t["n0"]:t["n1"]], onesb[:], pt, t["first"], t["last"], ["onesb", ("pt", pi)], t["zb"])
            for hk in hooks:
                hk()
            ab.n += nt

        def layer0_attention():
            S.barrier()
            arena.reset()
            dtab = arena.take([5, 512], F32)
            dma("sp", dtab, c_dtab.rearrange("v p n -> p v n"), writes=["dtab"])
            wd = [arena.take([8, 384], BF16) for _ in range(2)]
            dkT = [arena.take([SEQ], BF16) for _ in range(2)]
            dv = [arena.take([16, 128], BF16) for _ in range(2)]
            dqz = [[arena.take([512], BF16) for _ in range(2)] for _ in range(2)]
            for mp_ in range(2):
                for b_ in range(2):
                    S.add("dve", lambda e, mp_=mp_, b_=b_: e.memset(dqz[mp_][b_], 0.0), writes=[("dqz", mp_, b_)])
            a32 = [arena.take([512], F32) for _ in range(3)]
            osq = arena.take([512], BF16)
            rsd = arena.take([512], F32)
            ab = attn_alloc(6, need_tmp=True)
            dscale = 64.0 ** -0.5
            it = 0
            tail = []
            for h in range(4):
                slope = 2.0 ** (-8.0 * (h + 1) / 4)
                wi = h % 2
                w4 = wd[wi].rearrange("p c (a n) -> p c a n", a=3)
                def load_wd(hh):
                    w4n = wd[hh % 2].rearrange("p c (a n) -> p c a n", a=3)
                    for a in range(3):
                        c0 = 672 + a * 512 + hh * 128
                        load_w(w4n[:, :, a, :], ev_w_in[:, c0:c0 + 128].rearrange("(c p) n -> p c n", p=128), [("wd", hh % 2, a)])
                if h == 0:
                    load_wd(0)
                if h + 1 < 4:
                    load_wd(h + 1)
                if h == 1:
                    ensure_conv(0)
                def head_prep(hh, tg):
                    wj = hh % 2
                    w4_ = wd[wj].rearrange("p c (a n) -> p c a n", a=3)
                    sl_ = tg_sl(tg)
                    for k in range(8):
                        mm(ps[6][:], w4_[:, k, 1, :], xhi[:, k, sl_], k == 0, k == 7, [("wd", wj, 1), ("xh", k, tg)], 6)
                    S.add("act", lambda e: e.activation(out=dkT[wj][:, sl_], in_=ps[6][:], func=AF.Copy), writes=[PS(6), ("dkT", wj)])
                    for t4 in range(4):
                        tt = tg * 4 + t4
                        for k in range(8):
                            mm(ps[7][:, t4 * 128:(t4 + 1) * 128], xhi[:, k, tt * 128:(tt + 1) * 128], w4_[:, k, 2, :], k == 0, k == 7,
                               [("wd", wj, 2), ("xh", k, tg)], 7)
                    S.add("dve", lambda e: e.tensor_copy(out=dv[wj][:, tg * 4:(tg + 1) * 4, :], in_=ps[7][:].rearrange("p (t n) -> p t n", t=4)),
                          writes=[PS(7), ("dv", wj)])

                if h == 0:
                    for tg in range(4):
                        head_prep(0, tg)
                for qg in range(4):
                    sl = tg_sl(qg)
                    qb = it % 2

                    def dq_proj(hh, qg_, qb_):
                        w4_ = wd[hh % 2].rearrange("p c (a n) -> p c a n", a=3)
                        sl_ = tg_sl(qg_)
                        for k in range(8):
                            mm(ps[6][:], w4_[:, k, 0, :], xhi[:, k, sl_], k == 0, k == 7, [("wd", hh % 2, 0), ("xh", k, qg_)], 6)
                        S.add("act", lambda e: e.activation(out=dqz[0][qb_][0:64, :], in_=ps[6][0:64, :], func=AF.Copy),
                              writes=[PS(6), ("dqz", 0, qb_)])
                        S.add("dve", lambda e: e.tensor_copy(out=dqz[1][qb_][64:128, :], in_=ps[6][64:128, :]),
                              writes=[PS(6), ("dqz", 1, qb_)])

                    if h + 1 < 4:
                        head_prep(h + 1, qg)
                    if it == 0:
                        dq_proj(0, 0, 0)
                    nxt = (h, qg + 1) if qg < 3 else ((h + 1, 0) if h < 3 else None)
                    if nxt is not None:
                        dq_proj(nxt[0], nxt[1], (it + 1) % 2)
                    streams = []
                    for mp in range(2):
                        rows = slice(mp * 64, mp * 64 + 64)
                        ob, zb = 2 + mp, 4 + mp
                        tiles = []
                        for kc in range(16):
                            delta = qg * 512 - kc * 128
                            if delta >= 128:
                                dt_ap, dsc, cb = dtab[:, 0, :], slope / dscale, -slope * delta
                            elif delta <= -512:
                                dt_ap, dsc, cb = dtab[:, 0, :], -slope / dscale, slope * delta
                            else:
                                dt_ap, dsc, cb = dtab[:, 1 + (-delta) // 128, :], slope / dscale, 0.0
                            tiles.append(dict(kT=dkT[wi][:, kc * 128:(kc + 1) * 128], q=dqz[mp][qb][:, :], kreads=[("dkT", wi), ("dqz", mp, qb)],
                                              v=dv[wi][:, kc, :], vreads=[("dv", wi)], n0=0, n1=512, scale=dscale,
                                              dtab=dt_ap, dscale=dsc, cbias=cb * 1.0))
                        streams.append((tiles, ob, zb))
                    attn_run(ab, streams, hooks=tail)
                    for mp in range(2):
                        ob, zb = 2 + mp, 4 + mp
                        S.add("act", lambda e, zb=zb, mp=mp: e.activation(out=ab.zc[mp][:], in_=ps[zb][:], func=AF.Copy), writes=[PS(zb), ("zc", mp)])
                        S.add("act", lambda e, ob=ob, mp=mp: e.activation(out=ab.oc[mp][:], in_=ps[ob][:], func=AF.Copy), writes=[PS(ob), ("oc", mp)])

                    def tailA():
                        S.add("dve", lambda e: e.tensor_tensor(out=a32[0][:], in0=ab.oc[0][:], in1=ab.zc[1][:], op=ALU.mult),
                              reads=[("oc", 0), ("zc", 1)], writes=[("a32", 0)])
                        S.add("dve", lambda e: e.tensor_tensor(out=a32[1][:], in0=ab.oc[1][:], in1=ab.zc[0][:], op=ALU.mult),
                              reads=[("oc", 1), ("zc", 0)], writes=[("a32", 1)])
                        S.add("dve", lambda e: e.scalar_tensor_tensor(out=a32[2][:], in0=a32[1][:], scalar=lamw[:, 4:5], in1=a32[0][:], op0=ALU.mult, op1=ALU.add),
                              reads=[("a32", 0), ("a32", 1), "lamw"], writes=[("a32", 2)])
                        S.add("act", lambda e: e.activation(out=osq[:], in_=a32[2][:], func=AF.Square), reads=[("a32", 2)], writes=["osq"])

                    def tailB():
                        S.add("dve", lambda e: e.tensor_tensor(out=ab.rz[0][:], in0=ab.zc[0][:], in1=ab.zc[1][:], op=ALU.mult),
                              reads=[("zc", 0), ("zc", 1)], writes=[("rz", 0)])
                        S.add("dve", lambda e: e.scalar_tensor_tensor(out=ab.rz[1][:], in0=ab.rz[0][:], scalar=RMS_EPS, in1=ab.rz[0][:], op0=ALU.mult, op1=ALU.mult),
                              reads=[("rz", 0)], writes=[("rz", 1)])
                        mm(ps[7][:], onesb[:], osq[:], True, True, ["onesb", "osq"], 7)
                        S.add("dve", lambda e: e.scalar_tensor_tensor(out=rsd[:], in0=ps[7][:], scalar=1.0 / 128, in1=ab.rz[1][:], op0=ALU.mult, op1=ALU.add),
                              reads=[("rz", 1)], writes=[PS(7), "rsd"])
                        S.add("act", lambda e: e.activation(out=rsd[:], in_=rsd[:], func=AF.Sqrt), writes=["rsd"])

                    def tailC(h=h, sl=sl, qg=qg):
                        S.add("dve", lambda e: e.reciprocal(out=rsd[:], in_=rsd[:]), writes=["rsd"])
                        S.add("dve", lambda e: e.scalar_tensor_tensor(out=obuf[:, 4 + h, sl], in0=a32[2][:], scalar=lamw[:, 5:6], in1=rsd[:], op0=ALU.mult, op1=ALU.mult),
                              reads=[("a32", 2), "rsd", "lamw"], writes=[("ob", 4 + h, qg)])

                    tail = [tailA, tailB, tailC]
                    it += 1

            for f in tail:
                f()
            S.barrier()
            arena.reset()
            ctab = arena.take([SEQ], F32)
            stab = arena.take([SEQ], F32)
            dma("sp", ctab[0:96, :], c_rope[0], writes=["ctab"])
            dma("sp", stab[0:96, :], c_rope[1], writes=["stab"])
            cqn = arena.take([3, SEQ], BF16)
            ckvn = arena.take([2, SEQ], BF16)
            kr = arena.take([SEQ], BF16)
            vall = arena.take([16, 512], BF16)
            wuq = arena.take([3, 800], BF16)
            wuqr = arena.take([3, 800], BF16)
            wukv = arena.take([2, 1024], BF16)
            mark = arena.off
            wlat = arena.take([8, 672], BF16)
            wkrot = arena.take([8, 96], BF16)
            cq32 = arena.take([3, 512], F32)
            sq = [arena.take([512], BF16) for _ in range(2)]
            t32 = [arena.take([512], F32) for _ in range(2)]
            rst = arena.take([512], F32)
            wvc = arena.take([2, 512], BF16)
            load_w(wlat, ev_w_in[:, 0:672].rearrange("(c p) n -> p c n", p=128), ["wlat"])
            S.add("dve", lambda e: e.memset(wuq[:, :, 768:800], 0.0), writes=["wuqpad"])
            load_w(wuq[:, :, 0:768], ev_w_uq.rearrange("(c p) n -> p c n", p=128), ["wuq"])
            load_w(wukv, ev_w_ukv.rearrange("(c p) n -> p c n", p=128), ["wukv"])
            ensure_conv(1)
            S.add("dve", lambda e: e.memset(wkrot[:], 0.0), writes=["wkrot"])
            S.add("dve", lambda e: e.tensor_scalar(out=wkrot[:, :, 64:80], in0=wlat[:, :, 656:672], scalar1=-1.0, scalar2=None, op0=ALU.mult),
                  reads=["wlat"], writes=["wkrot"])
            S.add("dve", lambda e: e.tensor_copy(out=wkrot[:, :, 80:96], in_=wlat[:, :, 640:656]), reads=["wlat"], writes=["wkrot"])
            S.add("dve", lambda e: e.tensor_copy(out=wvc.rearrange("p c (h j) -> p c h j", h=8),
                                                  in_=wukv.rearrange("p c (h j) -> p c h j", h=8)[:, :, :, 64:128]), reads=["wukv"], writes=["wvc"])
            S.add("dve", lambda e: e.memset(wuqr[:], 0.0), writes=["wuqr"])
            wuq4 = wuq[:, :, 0:768].rearrange("p c (h j) -> p c h j", h=8)
            wuqr4 = wuqr[:, :, 0:768].rearrange("p c (h j) -> p c h j", h=8)
            for c in range(3):
                S.add("dve", lambda e, c=c: e.tensor_scalar(out=wuqr4[:, c, :, 64:80], in0=wuq4[:, c, :, 80:96], scalar1=-1.0, scalar2=None, op0=ALU.mult),
                      reads=["wuq"], writes=["wuqr"])
                S.add("dve", lambda e, c=c: e.tensor_copy(out=wuqr4[:, c, :, 80:96], in_=wuq4[:, c, :, 64:80]), reads=["wuq"], writes=["wuqr"])

            def rms_block(srcbanks, ng, gcols, nfeat, dst, tg):
                sl = tg_sl(tg)
                for j, bank in enumerate(srcbanks):
                    S.add("act", lambda e, j=j, bank=bank: e.activation(out=cq32[:, j, :], in_=ps[bank][:], func=AF.Copy),
                          writes=[PS(bank), ("cq32", j)])
                    S.add("act", lambda e, j=j: e.activation(out=sq[j % 2], in_=cq32[:, j, :], func=AF.Square),
                          reads=[("cq32", j)], writes=[("sq", j % 2)])
                    mm(ps[7][:], onesb[:], sq[j % 2], j == 0, j == ng - 1, ["onesb", ("sq", j % 2)], 7)
                S.add("act", lambda e: e.activation(out=rst, in_=ps[7][:], func=AF.Sqrt, bias=epsc[:, 1:2], scale=1.0 / nfeat),
                      reads=["epsc"], writes=[PS(7), "rst"])
                S.add("dve", lambda e: e.reciprocal(out=rst, in_=rst), writes=["rst"])
                for j in range(ng):
                    S.add("dve", lambda e, j=j: e.scalar_tensor_tensor(out=dst[:, j, sl], in0=cq32[:, j, :], scalar=gcols[:, j:j + 1], in1=rst,
                                                                      op0=ALU.mult, op1=ALU.mult),
                          reads=[("cq32", j), "rst", "qng", "kvng"], writes=[(id(dst), j, tg)])

            for tg in range(4):
                sl = tg_sl(tg)
                for j in range(3):
                    for k in range(8):
                        mm(ps[j][:], wlat[:, k, j * 128:(j + 1) * 128], xhi[:, k, sl], k == 0, k == 7, ["wlat", ("xh", k, tg)], j)
                rms_block([0, 1, 2], 3, qng, 384.0, cqn, tg)
                for j in range(2):
                    for k in range(8):
                        mm(ps[3 + j][:], wlat[:, k, 384 + j * 128:384 + (j + 1) * 128], xhi[:, k, sl], k == 0, k == 7, ["wlat", ("xh", k, tg)], 3 + j)
                rms_block([3, 4], 2, kvng, 256.0, ckvn, tg)
                for k in range(8):
                    mm(ps[5][0:96, :], wlat[:, k, 576:672], xhi[:, k, sl], k == 0, k == 7, ["wlat", ("xh", k, tg)], 5)
                for k in range(8):
                    mm(ps[6][0:96, :], wkrot[:, k, :], xhi[:, k, sl], k == 0, k == 7, ["wkrot", ("xh", k, tg)], 6)
                S.add("dve", lambda e, sl=sl: e.tensor_tensor(out=t32[0][64:96, :], in0=ps[5][64:96, :], in1=ctab[64:96, sl], op=ALU.mult),
                      reads=["ctab"], writes=[PS(5), ("t32", 0)])
                S.add("dve", lambda e, sl=sl: e.tensor_tensor(out=t32[1][64:96, :], in0=ps[6][64:96, :], in1=stab[64:96, sl], op=ALU.mult),
                      reads=["stab"], writes=[PS(6), ("t32", 1)])
                S.add("dve", lambda e, sl=sl: e.tensor_tensor(out=kr[64:96, sl], in0=t32[0][64:96, :], in1=t32[1][64:96, :], op=ALU.add),
                      reads=[("t32", 0), ("t32", 1)], writes=[("kr", tg)])
                wv = wukv.rearrange("p c (h j) -> p c h j", h=8)
                for t4 in range(4):
                    tt = tg * 4 + t4
                    bank = 5 + t4 % 2
                    for c in range(2):
                        mm(ps[bank][:], ckvn[:, c, tt * 128:(tt + 1) * 128], wvc[:, c, :], c == 0, c == 1,
                           ["wvc", (id(ckvn), 0, tg), (id(ckvn), 1, tg)], bank)
                    S.add("act", lambda e, bank=bank, tt=tt: e.activation(out=vall[:, tt, :], in_=ps[bank][:], func=AF.Copy),
                          writes=[PS(bank), ("vall", tt)])

            S.barrier()
            arena.off = mark
            kh = [arena.take([SEQ], BF16) for _ in range(2)]
            qh = [arena.take([512], BF16) for _ in range(2)]
            qt = [arena.take([512], F32) for _ in range(2)]
            ab = attn_alloc(6)
            scale = 96.0 ** -0.5
            it = 0
            mtail = []
            for h in range(8):
                kb = h % 2
                hp, e_ = h // 2, h % 2
                def k_prep(hh):
                    kb_ = hh % 2
                    for tg in range(4):
                        sl_ = tg_sl(tg)
                        for c in range(2):
                            mm(ps[6][:], wukv[:, c, hh * 128:(hh + 1) * 128], ckvn[:, c, sl_], c == 0, c == 1, ["wukv", "ckvn_all"], 6)
                        S.add("dve", lambda e, sl_=sl_: e.tensor_copy(out=kh[kb_][0:64, sl_], in_=ps[6][0:64, :]), writes=[PS(6), ("kh", kb_)])
                    S.add("dve", lambda e: e.tensor_copy(out=kh[kb_][64:96, :], in_=kr[64:96, :]), reads=["kr_all"], writes=[("kh", kb_)])

                if h == 0:
                    k_prep(0)
                for qpair in range(2):
                    if qpair == 1 and h + 1 < 8:
                        k_prep(h + 1)
                    streams = []
                    for j in range(2):
                        qg = qpair * 2 + j
                        sl = tg_sl(qg)
                        qb = j
                        for c in range(3):
                            mm(ps[6][:], wuq[:, c, h * 96:h * 96 + 128], cqn[:, c, sl], c == 0, c == 2, ["wuq", "cqn_all"], 6)
                        S.add("dve", lambda e, sl=sl: e.tensor_tensor(out=qt[0][0:96, :], in0=ps[6][0:96, :], in1=ctab[0:96, sl], op=ALU.mult),
                              reads=["ctab"], writes=[PS(6), ("qt", 0)])
                        for c in range(3):
                            mm(ps[6][:], wuqr[:, c, h * 96:h * 96 + 128], cqn[:, c, sl], c == 0, c == 2, ["wuqr", "cqn_all"], 6)
                        S.add("dve", lambda e, sl=sl: e.tensor_tensor(out=qt[1][0:96, :], in0=ps[6][0:96, :], in1=stab[0:96, sl], op=ALU.mult),
                              reads=["stab"], writes=[PS(6), ("qt", 1)])
                        S.add("dve", lambda e, qb=qb: e.tensor_tensor(out=qh[qb][0:96, :], in0=qt[0][0:96, :], in1=qt[1][0:96, :], op=ALU.add),
                              reads=[("qt", 0), ("qt", 1)], writes=[("qh", qb)])
                        tiles = []
                        for kc in range(16):
                            tiles.append(dict(kT=kh[kb][0:96, kc * 128:(kc + 1) * 128], q=qh[qb][0:96, :], kreads=[("kh", kb), ("qh", qb)],
                                              v=vall[:, kc, hp * 128:(hp + 1) * 128], vreads=[("vall", kc)], n0=0, n1=512, scale=scale))
                        streams.append((tiles, 2 + j, 4 + j))
                    attn_run(ab, streams, hooks=mtail, SB=(0, 1, 6, 7))
                    rows = slice(e_ * 64, e_ * 64 + 64)
                    for j in range(2):
                        qg = qpair * 2 + j
                        sl = tg_sl(qg)
                        ob, zb, ri = 2 + j, 4 + j, j
                        S.add("act", lambda e, zb=zb, ri=ri, rows=rows: e.activation(out=ab.zc[ri][rows, :], in_=ps[zb][rows, :], func=AF.Copy), writes=[PS(zb), ("zc", ri)])
                        S.add("act", lambda e, ob=ob, ri=ri, rows=rows: e.activation(out=ab.oc[ri][rows, :], in_=ps[ob][rows, :], func=AF.Copy), writes=[PS(ob), ("oc", ri)])
                    mtail = []
                    for j in range(2):
                        def mt(ri=j, rows=rows, hp=hp, sl=tg_sl(qpair * 2 + j), qg=qpair * 2 + j):
                            S.add("dve", lambda e: e.reciprocal(out=ab.rz[ri][rows, :], in_=ab.zc[ri][rows, :]), reads=[("zc", ri)], writes=[("rz", ri)])
                            S.add("dve", lambda e: e.tensor_tensor(out=obuf[rows, hp, sl], in0=ab.oc[ri][rows, :], in1=ab.rz[ri][rows, :], op=ALU.mult),
                                  reads=[("rz", ri), ("oc", ri)], writes=[("ob", hp, qg)])
                        mtail.append(mt)
            for f in mtail:
                f()

        def layer1_attention():
            S.barrier()
            arena.reset()
            swat = arena.take([384], F32)
            dma("sp", swat, c_swa, writes=["dtab"])
            wk = arena.take([8, 512], BF16)
            for c in range(8):
                load_w(wk[:, c, :], od_w_in[c * 128:(c + 1) * 128, 1024:1536], ["wk"])
            kT = arena.take([4, SEQ], BF16)
            wkd = arena.take([8, 4, 128], BF16)
            vdup = arena.take([16, 4, 128], BF16)
            wk4 = wk[:, :, 0:256].rearrange("p c (g j) -> p c g j", g=4)
            S.add("dve", lambda e: e.tensor_copy(out=wkd[:, :, :, 0:64], in_=wk4), reads=["wk"], writes=["wkd"])
            S.add("dve", lambda e: e.tensor_copy(out=wkd[:, :, :, 64:128], in_=wk4), reads=["wk"], writes=["wkd"])
            wq = [arena.take([8, 128], BF16) for _ in range(2)]
            qz = [[arena.take([SEQ], BF16) for _ in range(2)] for _ in range(2)]
            for e0 in range(2):
                for b_ in range(2):
                    S.add("dve", lambda e, e0=e0, b_=b_: e.memset(qz[e0][b_], 0.0), writes=[("qz", e0, b_)])
            ab = attn_alloc(6, need_tmp=True)
            sscale = 64.0 ** -0.5
            for tg in range(4):
                sl = tg_sl(tg)
                for j in range(4):
                    for k in range(8):
                        mm(ps[6][:], wkd[:, k, j, :], xhi[:, k, sl], k == 0, k == 7, ["wkd", ("xh", k, tg)], 6)
                    S.add("act", lambda e, j=j, sl=sl: e.activation(out=kT[:, j, sl], in_=ps[6][:], func=AF.Copy), writes=[PS(6), "kT"])
                for t4 in range(4):
                    tt = tg * 4 + t4
                    for k in range(8):
                        mm(ps[6][:, 0:256], xhi[:, k, tt * 128:(tt + 1) * 128], wk[:, k, 256:512], k == 0, k == 7, ["wk", ("xh", k, tg)], 6)
                    pv = ps[6][:, 0:256].rearrange("p (g j) -> p g j", g=4)
                    S.add("dve", lambda e, tt=tt, pv=pv: e.tensor_copy(out=vdup[:, tt, :, 0:64], in_=pv), writes=[PS(6), "vdup"])
                    S.add("act", lambda e, tt=tt, pv=pv: e.activation(out=vdup[:, tt, :, 64:128], in_=pv, func=AF.Copy), writes=[PS(6), "vdup"])
            it = 0
            stail = []
            for hp in range(8):
                wi = hp % 2
                if hp == 0:
                    load_w(wq[0], od_w_in[:, 0:128].rearrange("(c p) n -> p c n", p=128), [("wq", 0)])
                if hp + 1 < 8:
                    load_w(wq[(hp + 1) % 2], od_w_in[:, (hp + 1) * 128:(hp + 2) * 128].rearrange("(c p) n -> p c n", p=128), [("wq", (hp + 1) % 2)])
                def q_proj(hpp, tg):
                    wj = hpp % 2
                    sl_ = tg_sl(tg)
                    for k in range(8):
                        mm(ps[6][:], wq[wj][:, k, :], xhi[:, k, sl_], k == 0, k == 7, [("wq", wj), ("xh", k, tg)], 6)
                    S.add("act", lambda e: e.activation(out=qz[0][wj][0:64, sl_], in_=ps[6][0:64, :], func=AF.Copy), writes=[PS(6), ("qz", 0, wj)])
                    S.add("dve", lambda e: e.tensor_copy(out=qz[1][wj][64:128, sl_], in_=ps[6][64:128, :]), writes=[PS(6), ("qz", 1, wj)])

                if hp == 0:
                    for tg in range(4):
                        q_proj(0, tg)
                for qg in range(4):
                    sl = tg_sl(qg)
                    qb0 = qg * 4
                    if hp + 1 < 8:
                        q_proj(hp + 1, qg)
                    streams = []
                    for e_ in range(2):
                        h = hp * 2 + e_
                        g = h // 4
                        slope = 2.0 ** (-8.0 * (h + 1) / 16)
                        rows = slice(e_ * 64, e_ * 64 + 64)
                        tiles = []
                        for kc in range(qb0 - 1, qb0 + 5):
                            if kc < 0 or kc > 15:
                                continue
                            qa = max(qb0, kc - 1)
                            qz_ = min(qb0 + 3, kc + 1)
                            n0, n1 = (qa - qb0) * 128, (qz_ - qb0 + 1) * 128
                            toff = (qa - (kc - 1)) * 128
                            tiles.append(dict(kT=kT[:, g, kc * 128:(kc + 1) * 128], q=qz[e_][wi][:, qa * 128:(qz_ + 1) * 128],
                                              kreads=["kT", ("qz", e_, wi)], v=vdup[:, kc, g, :], vreads=["vdup"], n0=n0, n1=n1, scale=sscale,
                                              dtab=swat[:, toff:toff + (n1 - n0)], dscale=-slope / sscale, cbias=0.0))
                        streams.append((tiles, 2 + e_, 4 + e_))
                    attn_run(ab, streams, hooks=stail, SB=(0, 1, 6, 7))
                    rb = it % 2
                    it += 1
                    for e_ in range(2):
                        h = hp * 2 + e_
                        rows = slice(e_ * 64, e_ * 64 + 64)
                        ob, zb = 2 + e_, 4 + e_
                        S.add("act", lambda e, zb=zb, rb=rb, rows=rows, h=h: e.activation(out=ab.zc[rb][rows, :], in_=ps[zb][rows, :], func=AF.Identity,
                                                                                          bias=sinkt[rows, h:h + 1], scale=1.0),
                              reads=["sinkt"], writes=[PS(zb), ("zc", rb)])
                        S.add("act", lambda e, ob=ob, rb=rb, rows=rows: e.activation(out=ab.oc[rb][rows, :], in_=ps[ob][rows, :], func=AF.Copy), writes=[PS(ob), ("oc", rb)])

                    def stl1(rb=rb):
                        S.add("dve", lambda e: e.reciprocal(out=ab.rz[rb][:, 0:256], in_=ab.zc[rb][:, 0:256]), reads=[("zc", rb)], writes=[("rz", rb, 0)])

                    def stl2(rb=rb, hp=hp, sl=sl, qg=qg):
                        S.add("dve", lambda e: e.reciprocal(out=ab.rz[rb][:, 256:512], in_=ab.zc[rb][:, 256:512]), reads=[("zc", rb)], writes=[("rz", rb, 1)])
                        S.add("dve", lambda e: e.tensor_tensor(out=obuf[:, hp, sl], in0=ab.oc[rb][:], in1=ab.rz[rb][:], op=ALU.mult),
                              reads=[("rz", rb, 0), ("rz", rb, 1), ("oc", rb)], writes=[("ob", hp, qg)])
                    stail = [stl1, stl2]
            for f in stail:
                f()

        final = []
        for s in range(nseq):
            S.barrier()
            if s == 0:
                load_x(s)
            for layer in layers:
                if layer in ("a0", "a1"):
                    (layer0_attention if layer == "a0" else layer1_attention)()
                    S.barrier()
                    for c in range(8):
                        S.add("dve", lambda e, c=c: e.tensor_copy(out=xhi[:, c, :], in_=obuf[:, c, :]))
                    S.add("dve", lambda e: e.memset(xlo[:], 0.0))
                    S.barrier()
                    continue
                if layer == 0:
                    layer0_attention()
                    st_, pend_ = out_proj_ln(ev_w_out, 0)
                    ffn_ln(0, 1, st_, pend_)
                else:
                    layer1_attention()
                    st_, pend_ = out_proj_ln(od_w_out, 2)
                    ffn_ln(1, 3, st_, pend_)
            S.barrier()
            if s + 1 < nseq:
                final += store_load_x(s, s + 1)
            else:
                final += store_x(s)
        S.emit(es, final_wait_ops=final)
    return nc


def _const_tables():
    half = 16
    inv = (10000.0 ** (-np.arange(half, dtype=np.float32) / half)).astype(np.float32)
    pos = np.arange(SEQ, dtype=np.float32)
    ang = pos[None, :] * inv[:, None]
    cos = np.cos(ang).astype(np.float32)
    sin = np.sin(ang).astype(np.float32)
    rope = np.zeros((2, 96, SEQ), np.float32)
    rope[0, 0:64] = 1.0
    rope[0, 64:80] = cos
    rope[0, 80:96] = cos
    rope[1, 64:80] = sin
    rope[1, 80:96] = sin
    koff = np.arange(128, dtype=np.float32)[:, None]
    qoff = np.arange(512, dtype=np.float32)[None, :]
    dtab = np.zeros((5, 128, 512), np.float32)
    dtab[0] = -(qoff - koff)
    for v in range(4):
        dtab[1 + v] = -np.abs(qoff - koff - 128.0 * v)
    c = np.arange(384, dtype=np.float32)[None, :]
    dd = np.abs((c - 128.0) - koff)
    swa = np.where(dd <= 128.0, dd, 1.0e6).astype(np.float32)
    return rope, dtab, swa


_CACHE = {}


def _get_nc(layers):
    key = tuple(layers)
    if key not in _CACHE:
        _CACHE[key] = build_program(layers=layers)
    return _CACHE[key]


def _run(x, weights, layers):
    nc = _get_nc(layers)
    rope, dtab, swa = _const_tables()
    common = dict(weights)
    common.update(c_ident=np.eye(128, dtype=np.float32), c_rope=rope, c_dtab=dtab, c_swa=swa)
    in_maps = []
    for c in range(8):
        m = dict(common)
        m["x"] = np.ascontiguousarray(x[c * NSEQ:(c + 1) * NSEQ])
        in_maps.append(m)
    res = run_bass_kernel_spmd(nc, in_maps, core_ids=list(range(8)))
    return np.concatenate([r["out"] for r in res.results], axis=0)


def _pack_weights(ev_w_in, ev_q_norm, ev_kv_norm, ev_w_uq, ev_w_ukv, ev_lam_q1, ev_lam_k1, ev_lam_q2, ev_lam_k2,
                  ev_diff_norm, ev_w_out, od_w_in, od_sink, od_w_out, ln1_g, ln1_b, ln2_g, ln2_b,
                  ffn_w1, ffn_b1, ffn_w2, ffn_b2):
    f = lambda a: np.ascontiguousarray(np.asarray(a, dtype=np.float32))
    col = lambda v: np.ascontiguousarray(f(v).reshape(-1, 128).T)
    ln_p = np.stack([np.stack([f(ln1_g)[0], f(ln1_b)[0]]), np.stack([f(ln2_g)[0], f(ln2_b)[0]]),
                     np.stack([f(ln1_g)[1], f(ln1_b)[1]]), np.stack([f(ln2_g)[1], f(ln2_b)[1]])])
    ln_p = np.ascontiguousarray(ln_p.reshape(4, 2, 8, 128).transpose(3, 0, 1, 2).reshape(128, 64))
    b1 = np.ascontiguousarray(f(ffn_b1).reshape(2, 32, 128).transpose(2, 0, 1).reshape(128, 64))
    b2 = np.ascontiguousarray(f(ffn_b2).reshape(2, 8, 128).transpose(2, 0, 1).reshape(128, 16))
    lam = np.concatenate([f(ev_lam_q1)[0], f(ev_lam_k1)[0], f(ev_lam_q2)[0], f(ev_lam_k2)[0]])
    rep = lambda v: np.ascontiguousarray(np.broadcast_to(v[None, :], (128, v.shape[0])))
    return dict(
        ev_w_in=f(ev_w_in)[0], ev_q_norm=col(f(ev_q_norm)[0]), ev_kv_norm=col(f(ev_kv_norm)[0]), ev_w_uq=f(ev_w_uq)[0],
        ev_w_ukv=f(ev_w_ukv)[0], ev_lam=rep(lam),
        ev_diff_norm=col(f(ev_diff_norm)[0]), ev_w_out=f(ev_w_out)[0], od_w_in=f(od_w_in)[0], od_sink=rep(f(od_sink)[0]),
        od_w_out=f(od_w_out)[0], ln_p=ln_p, ffn_w1=f(ffn_w1), ffn_b1=b1, ffn_w2=f(ffn_w2),
        ffn_b2=b2)


LAUNCH_PLAN = [(0, 1)]


def kernel(x, **w):
    weights = _pack_weights(**w)
    cur = np.asarray(x, dtype=np.float32)
    for layers in LAUNCH_PLAN:
        cur = _run(cur, weights, layers)
    return cur.astype(np.float32)
```

```python
import contextlib
import math
import numpy as np
import concourse.bass as bass
import concourse.mybir as mybir
from concourse.bass_utils import run_bass_kernel_spmd

F32 = mybir.dt.float32
BF16 = mybir.dt.bfloat16
AF = mybir.ActivationFunctionType
ALU = mybir.AluOpType

D = 1024
SEQ = 2048
NSEQ = 2
DFF = 4096
ALPHA = 4 ** 0.25
LN_EPS = 1e-5
RMS_EPS = 1e-6
EVEN_IN = 2208
ODD_IN = 1536
LAMBDA_INIT0 = 0.8 - 0.6 * math.exp(0.0)


class _Op:
    __slots__ = ("eng", "fn", "deps", "dma", "sig", "needed", "idx")

    def __init__(self, eng, fn, deps, dma, idx):
        self.eng, self.fn, self.deps, self.dma, self.idx = eng, fn, deps, dma, idx
        self.sig = None
        self.needed = False


class Sched:
    ENGS = ("pe", "act", "dve", "pool", "sp")

    def __init__(self, nc, n_dma_sems=32):
        self.nc = nc
        self.ops = []
        self.last_w = {}
        self.readers = {}
        self.n_dma_sems = n_dma_sems
        self.last_barrier = 0
        self.persist_ops = set()
        self.persist_w = {}

    def add(self, eng, fn, reads=(), writes=(), dma=False, persist=False):
        idx = len(self.ops)
        if persist:
            self.persist_ops.add(idx)
            for r in writes:
                self.persist_w[r] = idx
        deps = set()
        for r in reads:
            w = self.last_w.get(r)
            if w is not None:
                deps.add(w)
        for r in writes:
            w = self.last_w.get(r)
            if w is not None:
                deps.add(w)
            deps.update(self.readers.get(r, ()))
        for r in reads:
            self.readers.setdefault(r, []).append(idx)
        for r in writes:
            self.last_w[r] = idx
            self.readers[r] = []
        deps.discard(idx)
        self.ops.append(_Op(eng, fn, deps, dma, idx))
        return idx

    def barrier(self):
        last = {}
        for op in self.ops:
            if not op.dma and op.fn is not None:
                last[op.eng] = op.idx
        deps = set(last.values())
        deps.update(op.idx for op in self.ops[self.last_barrier:] if op.dma and op.idx not in self.persist_ops)
        for e in self.ENGS:
            idx = len(self.ops)
            self.ops.append(_Op(e, None, set(deps), False, idx))
        self.last_barrier = len(self.ops)
        self.last_w = dict(self.persist_w)
        self.readers = {}

    @staticmethod
    def _skip(dop, op):
        return dop.eng == "pe" and op.eng == "pe" and not dop.dma and not op.dma

    def emit(self, es, final_wait_ops=()):
        nc = self.nc
        ops = self.ops
        for op in ops:
            for d in op.deps:
                if not self._skip(ops[d], op):
                    ops[d].needed = True
        for i in final_wait_ops:
            ops[i].needed = True
        eng_sem = {e: es.enter_context(nc.semaphore("s_" + e)) for e in self.ENGS}
        dma_sems = [es.enter_context(nc.semaphore("d%d" % i)) for i in range(self.n_dma_sems)]
        eng_cnt = {e: 0 for e in self.ENGS}
        dma_cnt = [0] * self.n_dma_sems
        rr = 0
        for op in ops:
            if op.dma:
                s = rr % self.n_dma_sems
                rr += 1
                prev = dma_cnt[s]
                dma_cnt[s] += 16
                op.sig = (dma_sems[s], dma_cnt[s], prev)
            elif op.needed:
                eng_cnt[op.eng] += 1
                op.sig = (eng_sem[op.eng], eng_cnt[op.eng], None)
        self.counts = dict(eng_cnt)
        by_eng = {e: [op for op in ops if op.eng == e] for e in self.ENGS}
        block = es.enter_context(nc.Block())

        def replay(ename, e):
            known = {}

            def wait(sem, val):
                if val <= 0 or known.get(sem.num, 0) >= val:
                    return
                e.wait_ge(sem, val)
                known[sem.num] = val

            for op in by_eng[ename]:
                for d in sorted(op.deps):
                    dop = ops[d]
                    if self._skip(dop, op):
                        continue
                    wait(dop.sig[0], dop.sig[1])
                if op.fn is None:
                    continue
                if op.dma:
                    wait(op.sig[0], op.sig[2])
                ins = op.fn(e)
                if op.dma:
                    ins.then_inc(op.sig[0], 16)
                elif op.sig is not None:
                    ins.then_inc(op.sig[0], 1)
            if ename == "sp":
                for i in final_wait_ops:
                    wait(ops[i].sig[0], ops[i].sig[1])

        @block.sync
        def _(e):
            replay("sp", e)

        @block.tensor
        def _(e):
            replay("pe", e)

        @block.scalar
        def _(e):
            replay("act", e)

        @block.vector
        def _(e):
            replay("dve", e)

        @block.gpsimd
        def _(e):
            replay("pool", e)


class Arena:
    def __init__(self, nc, es, name, nbytes):
        self.nbytes = nbytes
        self.t = es.enter_context(nc.sbuf_tensor(name, [128, nbytes // 2], BF16))
        self.off = 0

    def reset(self):
        self.off = 0

    def take(self, shape, dtype):
        n = 1
        for s in shape:
            n *= s
        nb = n * (4 if dtype == F32 else 2)
        nb = (nb + 31) // 32 * 32
        assert self.off + nb <= self.nbytes, ("arena overflow", self.off, nb, self.nbytes)
        ap = self.t[:, self.off // 2:(self.off + nb) // 2]
        self.off += nb
        if dtype == F32:
            ap = ap.bitcast(F32)
        ap = ap[:, 0:n]
        if len(shape) == 2:
            ap = ap.rearrange("p (a b) -> p a b", a=shape[0])
        elif len(shape) == 3:
            ap = ap.rearrange("p (a b c) -> p a b c", a=shape[0], b=shape[1])
        return ap


def build_program(layers=(0, 1), nseq=NSEQ):
    nc = bass.Bass("TRN2", target_bir_lowering=False)

    def din(name, shape):
        return nc.dram_tensor(name, list(shape), F32, kind="ExternalInput").ap()

    x_d = din("x", [nseq, SEQ, D])
    ev_w_in = din("ev_w_in", [D, EVEN_IN])
    ev_q_norm = din("ev_q_norm", [128, 3])
    ev_kv_norm = din("ev_kv_norm", [128, 2])
    ev_w_uq = din("ev_w_uq", [384, 768])
    ev_w_ukv = din("ev_w_ukv", [256, 1024])
    lamv = din("ev_lam", [128, 256])
    ev_diff_norm = din("ev_diff_norm", [128, 1])
    ev_w_out = din("ev_w_out", [D, D])
    od_w_in = din("od_w_in", [D, ODD_IN])
    od_sink = din("od_sink", [128, 16])
    od_w_out = din("od_w_out", [D, D])
    ln_p = din("ln_p", [128, 64])
    ffn_w1 = din("ffn_w1", [2, D, DFF])
    ffn_b1 = din("ffn_b1", [128, 64])
    ffn_w2 = din("ffn_w2", [2, DFF, D])
    ffn_b2 = din("ffn_b2", [128, 16])
    c_ident = din("c_ident", [128, 128])
    c_rope = din("c_rope", [2, 96, SEQ])
    c_dtab = din("c_dtab", [5, 128, 512])
    c_swa = din("c_swa", [128, 384])
    out_d = nc.dram_tensor("out", [nseq, SEQ, D], F32, kind="ExternalOutput").ap()
    w1t = nc.dram_tensor("w1t", [2, 8, 128, 8 * 512], BF16, kind="Internal").ap()
    w2t = nc.dram_tensor("w2t", [2, 8, 128, 32 * 128], BF16, kind="Internal").ap()

    es = contextlib.ExitStack()
    S = Sched(nc)
    with es:
        sb = lambda name, shape, dt: es.enter_context(nc.sbuf_tensor(name, shape, dt))
        xhi = sb("xhi", [128, 8, SEQ], BF16)
        xlo = sb("xlo", [128, 8, SEQ], BF16)
        obuf = sb("obuf", [128, 8, SEQ], BF16)
        identb = sb("identb", [128, 128], BF16)
        onesb = sb("onesb", [128, 128], BF16)
        alphaI = sb("alphaI", [128, 128], BF16)
        epsc = sb("epsc", [128, 2], F32)
        lnp = sb("lnp", [128, 4, 2, 8], F32)
        b1c = sb("b1c", [128, 2, 32], F32)
        b2c = sb("b2c", [128, 2, 8], F32)
        qng = sb("qng", [128, 3], F32)
        kvng = sb("kvng", [128, 2], F32)
        dng = sb("dng", [128, 1], F32)
        lamt = sb("lamt", [128, 4, 64], F32)
        lamw = sb("lamw", [128, 8], F32)
        sinkt = sb("sinkt", [128, 16], F32)
        arena = Arena(nc, es, "arena", 106 * 1024)
        ps = [es.enter_context(nc.psum_tensor("ps%d" % i, [128, 512], F32)) for i in range(8)]
        PS = lambda i: ("ps", i)

        def dma(eng, out, in_, reads=(), writes=()):
            return S.add(eng, lambda e: e.dma_start(out=out, in_=in_), reads=reads, writes=writes, dma=True)

        dma("pool", identb[:], c_ident, writes=["identb"])
        S.add("dve", lambda e: e.memset(onesb[:], 1.0), writes=["onesb"])
        S.add("act", lambda e: e.activation(out=alphaI[:], in_=identb[:], func=AF.Copy, scale=ALPHA), reads=["identb"], writes=["alphaI"])
        S.add("dve", lambda e: e.memset(epsc[:, 0:1], LN_EPS), writes=["epsc"])
        S.add("dve", lambda e: e.memset(epsc[:, 1:2], RMS_EPS), writes=["epsc"])
        dma("sp", lnp[:].rearrange("p l t c -> p (l t c)"), ln_p, writes=["lnp"])
        dma("sp", b1c[:].rearrange("p l c -> p (l c)"), ffn_b1, writes=["b1c"])
        dma("sp", b2c[:].rearrange("p l c -> p (l c)"), ffn_b2, writes=["b2c"])
        dma("sp", qng[:], ev_q_norm, writes=["qng"])
        dma("sp", kvng[:], ev_kv_norm, writes=["kvng"])
        dma("sp", dng[:], ev_diff_norm, writes=["dng"])
        dma("sp", lamt[:].rearrange("p a b -> p (a b)"), lamv, writes=["lamt"])
        dma("sp", sinkt[:], od_sink, writes=["sinkt"])
        S.add("dve", lambda e: e.tensor_tensor(out=lamt[:, 0, :], in0=lamt[:, 0, :], in1=lamt[:, 1, :], op=ALU.mult), writes=["lamt"])
        S.add("dve", lambda e: e.tensor_tensor(out=lamt[:, 2, :], in0=lamt[:, 2, :], in1=lamt[:, 3, :], op=ALU.mult), writes=["lamt"])
        S.add("dve", lambda e: e.reduce_sum(out=lamw[:, 0:1], in_=lamt[:, 0, :], axis=mybir.AxisListType.X), reads=["lamt"], writes=["lamw"])
        S.add("dve", lambda e: e.reduce_sum(out=lamw[:, 1:2], in_=lamt[:, 2, :], axis=mybir.AxisListType.X), reads=["lamt"], writes=["lamw"])
        S.add("act", lambda e: e.activation(out=lamw[:, 2:4], in_=lamw[:, 0:2], func=AF.Exp), writes=["lamw"])
        S.add("dve", lambda e: e.tensor_tensor(out=lamw[:, 4:5], in0=lamw[:, 3:4], in1=lamw[:, 2:3], op=ALU.subtract), writes=["lamw"])
        S.add("dve", lambda e: e.tensor_scalar(out=lamw[:, 4:5], in0=lamw[:, 4:5], scalar1=-LAMBDA_INIT0, scalar2=None, op0=ALU.add), writes=["lamw"])
        S.add("dve", lambda e: e.tensor_scalar(out=lamw[:, 5:6], in0=dng[:, 0:1], scalar1=1.0 - LAMBDA_INIT0, scalar2=None, op0=ALU.mult), reads=["dng"], writes=["lamw"])
        S.add("act", lambda e: e.activation(out=sinkt[:], in_=sinkt[:], func=AF.Exp), writes=["sinkt"])

        def mm(out, lhsT, rhs, start, stop, reads, bank):
            return S.add("pe", lambda e: e.matmul(out, lhsT=lhsT, rhs=rhs, start=start, stop=stop), reads=reads, writes=[PS(bank)])

        def tg_sl(tg):
            return slice(tg * 512, (tg + 1) * 512)

        def load_w(eng_out, src, writes):
            return dma("pool", eng_out, src, writes=writes)

        conv_done = set()
        conv_cnt = [0]

        def ensure_conv(l):
            if l in conv_done:
                return
            conv_done.add(l)
            def chain():
                i = conv_cnt[0]
                conv_cnt[0] += 1
                return ([("conv", i - 6)] if i >= 6 else []), ("conv", i)
            for hb in range(8):
                rd, wr = chain()
                S.add("pool", lambda e, hb=hb: e.dma_start(out=w1t[l, hb].rearrange("p (c n) -> p c n", c=8),
                                                            in_=ffn_w1[l, :, hb * 512:(hb + 1) * 512].rearrange("(c p) n -> p c n", p=128)),
                      reads=rd, writes=[("w1t", l, hb), wr], dma=True, persist=True)
            for m in range(8):
                for part in range(4):
                    rd, wr = chain()
                    S.add("pool", lambda e, m=m, part=part: e.dma_start(
                        out=w2t[l, m].rearrange("p (k n) -> p k n", k=32)[:, part * 8:(part + 1) * 8, :],
                        in_=ffn_w2[l, part * 1024:(part + 1) * 1024, m * 128:(m + 1) * 128].rearrange("(k p) n -> p k n", p=128)),
                        reads=rd, writes=[("w2t", l, m, part), wr], dma=True, persist=True)

        def load_x(s):
            arena.reset()
            xin = [arena.take([D], F32) for _ in range(4)]
            hit = [arena.take([D], BF16) for _ in range(4)]
            lot = [arena.take([D], BF16) for _ in range(4)]
            for tt in range(16):
                b = tt % 4
                tg = tt // 4
                dma("sp" if tt % 2 == 0 else "pool", xin[b], x_d[s, tt * 128:(tt + 1) * 128, :], writes=[("xin", b)])
                S.add("act", lambda e, b=b: e.activation(out=hit[b], in_=xin[b], func=AF.Copy), reads=[("xin", b)], writes=[("hit", b)])
                S.add("dve", lambda e, b=b: e.tensor_tensor(out=lot[b], in0=xin[b], in1=hit[b], op=ALU.subtract),
                      reads=[("xin", b), ("hit", b)], writes=[("lot", b)])
                for (src, dst, nm, bank) in ((hit, xhi, "xh", 0 + 2 * b), (lot, xlo, "xl", 1 + 2 * b)):
                    pb = ps[bank][:].bitcast(BF16)
                    for c in range(8):
                        S.add("pe", lambda e, c=c, pb=pb, src=src, b=b: e.transpose(pb[:, c * 128:(c + 1) * 128], src[b][:, c * 128:(c + 1) * 128], identb[:]),
                              reads=[(nm[1] == "h" and ("hit", b) or ("lot", b)), "identb"], writes=[PS(bank)])
                    S.add("dve" if nm == "xh" else "act",
                          (lambda e, pb=pb, dst=dst, tt=tt: e.tensor_copy(out=dst[:, :, tt * 128:(tt + 1) * 128], in_=pb.rearrange("p (c t) -> p c t", c=8)))
                          if nm == "xh" else
                          (lambda e, pb=pb, dst=dst, tt=tt: e.activation(out=dst[:, :, tt * 128:(tt + 1) * 128], in_=pb.rearrange("p (c t) -> p c t", c=8), func=AF.Copy)),
                          writes=[PS(bank)] + [(nm, c, tg) for c in range(8)])

        def store_x(s):
            arena.reset()
            ta = [arena.take([D], F32) for _ in range(4)]
            to = [arena.take([D], F32) for _ in range(4)]
            outs = []
            for tt in range(16):
                b = tt % 4
                tg = tt // 4
                for (src, nm, bank) in ((xhi, "xh", 0 + 2 * b), (xlo, "xl", 1 + 2 * b)):
                    pb = ps[bank][:].bitcast(BF16)
                    for c in range(8):
                        S.add("pe", lambda e, c=c, pb=pb, src=src, tt=tt: e.transpose(pb[:, c * 128:(c + 1) * 128], src[:, c, tt * 128:(tt + 1) * 128], identb[:]),
                              reads=[(nm, c, tg), "identb"], writes=[PS(bank)])
                pa = ps[0 + 2 * b][:].bitcast(BF16)
                pl = ps[1 + 2 * b][:].bitcast(BF16)
                S.add("act", lambda e, b=b, pa=pa: e.activation(out=ta[b], in_=pa, func=AF.Copy), writes=[PS(0 + 2 * b), ("ta", b)])
                S.add("dve", lambda e, b=b, pl=pl: e.tensor_tensor(out=to[b], in0=pl, in1=ta[b], op=ALU.add),
                      reads=[("ta", b)], writes=[PS(1 + 2 * b), ("to", b)])
                outs.append(dma("sp" if tt % 2 == 0 else "pool", out_d[s, tt * 128:(tt + 1) * 128, :], to[b], reads=[("to", b)]))
            return outs

        def store_load_x(s_out, s_in):
            arena.reset()
            ta = [arena.take([D], F32) for _ in range(2)]
            to = [arena.take([D], F32) for _ in range(2)]
            xin = [arena.take([D], F32) for _ in range(2)]
            hit = [arena.take([D], BF16) for _ in range(2)]
            lot = [arena.take([D], BF16) for _ in range(2)]
            outs = []
            for tt in range(16):
                b = tt % 2
                tg = tt // 4
                cols = slice(tt * 128, (tt + 1) * 128)
                dma("pool", xin[b], x_d[s_in, cols, :], writes=[("xin", b)])
                for (src, nm, bank) in ((xhi, "xh", 0 + 2 * b), (xlo, "xl", 1 + 2 * b)):
                    pb = ps[bank][:].bitcast(BF16)
                    for c in range(8):
                        S.add("pe", lambda e, c=c, pb=pb, src=src, cols=cols: e.transpose(pb[:, c * 128:(c + 1) * 128], src[:, c, cols], identb[:]),
                              reads=[(nm, "t", tt), "identb"], writes=[PS(bank)])
                pa = ps[0 + 2 * b][:].bitcast(BF16)
                pl = ps[1 + 2 * b][:].bitcast(BF16)
                S.add("act", lambda e, b=b, pa=pa: e.activation(out=ta[b], in_=pa, func=AF.Copy), writes=[PS(0 + 2 * b), ("ta", b)])
                S.add("dve", lambda e, b=b, pl=pl: e.tensor_tensor(out=to[b], in0=pl, in1=ta[b], op=ALU.add),
                      reads=[("ta", b)], writes=[PS(1 + 2 * b), ("to", b)])
                outs.append(dma("sp", out_d[s_out, cols, :], to[b], reads=[("to", b)]))
                S.add("act", lambda e, b=b: e.activation(out=hit[b], in_=xin[b], func=AF.Copy), reads=[("xin", b)], writes=[("hit", b)])
                S.add("dve", lambda e, b=b: e.tensor_tensor(out=lot[b], in0=xin[b], in1=hit[b], op=ALU.subtract),
                      reads=[("xin", b), ("hit", b)], writes=[("lot", b)])
                for (src, dst, nm, bank, eng) in ((hit, xhi, "xh", 4 + 2 * b, "dve"), (lot, xlo, "xl", 5 + 2 * b, "act")):
                    pb = ps[bank][:].bitcast(BF16)
                    for c in range(8):
                        S.add("pe", lambda e, c=c, pb=pb, src=src, b=b: e.transpose(pb[:, c * 128:(c + 1) * 128], src[b][:, c * 128:(c + 1) * 128], identb[:]),
                              reads=[(("hit", b) if nm == "xh" else ("lot", b)), "identb"], writes=[PS(bank)])
                    pv = pb.rearrange("p (c t) -> p c t", c=8)
                    if eng == "dve":
                        S.add("dve", lambda e, pv=pv, dst=dst, cols=cols: e.tensor_copy(out=dst[:, :, cols], in_=pv),
                              writes=[PS(bank), (nm, "t", tt)] + [(nm, c, tg) for c in range(8)])
                    else:
                        S.add("act", lambda e, pv=pv, dst=dst, cols=cols: e.activation(out=dst[:, :, cols], in_=pv, func=AF.Copy),
                              writes=[PS(bank), (nm, "t", tt)] + [(nm, c, tg) for c in range(8)])
            return outs

        class LNState:
            pass

        def ln_alloc():
            st = LNState()
            st.r = arena.take([8, 512], F32)
            st.rb = [arena.take([512], BF16) for _ in range(3)]
            st.rsq = [arena.take([512], BF16) for _ in range(3)]
            st.mean = arena.take([512], F32)
            st.var = arena.take([512], F32)
            st.rstd = arena.take([512], F32)
            st.nmr = arena.take([512], F32)
            st.v = [arena.take([512], F32) for _ in range(2)]
            st.nb = [arena.take([512], F32) for _ in range(2)]
            st.cnt = 0
            return st

        def ln_resid_chunk(st, m, tg, y_bank, bias_col, b_s1, b_s2):
            sl = tg_sl(tg)
            i = st.cnt % 3
            st.cnt += 1
            rm = st.r[:, m, :]
            if bias_col is not None:
                S.add("act", lambda e: e.activation(out=rm, in_=ps[y_bank][:], func=AF.Identity, bias=bias_col, scale=1.0),
                      reads=["b2c"], writes=[PS(y_bank), ("r", m)])
                S.add("dve", lambda e: e.scalar_tensor_tensor(out=rm, in0=xhi[:, m, sl], scalar=ALPHA, in1=rm, op0=ALU.mult, op1=ALU.add),
                      reads=[("xh", m, tg)], writes=[("r", m)])
                S.add("dve", lambda e: e.scalar_tensor_tensor(out=rm, in0=xlo[:, m, sl], scalar=ALPHA, in1=rm, op0=ALU.mult, op1=ALU.add),
                      reads=[("xl", m, tg)], writes=[("r", m)])
            else:
                S.add("dve", lambda e: e.scalar_tensor_tensor(out=rm, in0=xhi[:, m, sl], scalar=ALPHA, in1=ps[y_bank][:], op0=ALU.mult, op1=ALU.add),
                      reads=[("xh", m, tg)], writes=[PS(y_bank), ("r", m)])
            S.add("act", lambda e: e.activation(out=st.rb[i], in_=rm, func=AF.Copy), reads=[("r", m)], writes=[("rb", i)])
            S.add("act", lambda e: e.activation(out=st.rsq[i], in_=rm, func=AF.Square), reads=[("r", m)], writes=[("rsq", i)])
            def emit_stats():
                mm(ps[b_s1][:], onesb[:], st.rb[i], m == 0, m == 7, ["onesb", ("rb", i)], b_s1)
                mm(ps[b_s2][:], onesb[:], st.rsq[i], m == 0, m == 7, ["onesb", ("rsq", i)], b_s2)
            return emit_stats

        def ln_finish(st, ln_idx, tg, b_s1, b_s2):
            sl = tg_sl(tg)

            def stats():
                S.add("dve", lambda e: e.tensor_scalar(out=st.mean, in0=ps[b_s1][:], scalar1=1.0 / D, scalar2=None, op0=ALU.mult),
                      writes=[PS(b_s1), "mean"])
                S.add("act", lambda e: e.activation(out=st.var, in_=st.mean, func=AF.Square), reads=["mean"], writes=["var"])
                S.add("dve", lambda e: e.scalar_tensor_tensor(out=st.var, in0=ps[b_s2][:], scalar=1.0 / D, in1=st.var, op0=ALU.mult, op1=ALU.subtract),
                      writes=[PS(b_s2), "var"])
                S.add("act", lambda e: e.activation(out=st.var, in_=st.var, func=AF.Sqrt, bias=epsc[:, 0:1], scale=1.0), reads=["epsc"], writes=["var"])
                S.add("dve", lambda e: e.reciprocal(out=st.rstd, in_=st.var), reads=["var"], writes=["rstd"])
                S.add("dve", lambda e: e.scalar_tensor_tensor(out=st.nmr, in0=st.mean, scalar=-1.0, in1=st.rstd, op0=ALU.mult, op1=ALU.mult),
                      reads=["mean", "rstd"], writes=["nmr"])

            def chunk(m):
                i = m % 2
                g = lnp[:, ln_idx, 0, m:m + 1]
                bcol = lnp[:, ln_idx, 1, m:m + 1]
                rm = st.r[:, m, :]
                S.add("dve", lambda e: e.scalar_tensor_tensor(out=st.v[i], in0=rm, scalar=g, in1=st.rstd, op0=ALU.mult, op1=ALU.mult),
                      reads=[("r", m), "rstd", "lnp"], writes=[("v", i)])
                S.add("act", lambda e: e.activation(out=st.nb[i], in_=st.nmr, func=AF.Identity, bias=bcol, scale=g),
                      reads=["nmr", "lnp"], writes=[("nb", i)])
                S.add("dve", lambda e: e.tensor_tensor(out=st.v[i], in0=st.v[i], in1=st.nb[i], op=ALU.add),
                      reads=[("nb", i)], writes=[("v", i)])
                S.add("act", lambda e: e.activation(out=xhi[:, m, sl], in_=st.v[i], func=AF.Copy), reads=[("v", i)], writes=[("xh", m, tg)])
                S.add("dve", lambda e: e.tensor_tensor(out=xlo[:, m, sl], in0=st.v[i], in1=xhi[:, m, sl], op=ALU.subtract),
                      reads=[("v", i), ("xh", m, tg)], writes=[("xl", m, tg)])

            steps = [stats]
            for m in range(8):
                steps.append(lambda m=m: chunk(m))
            return steps

        def out_proj_ln(w_out_d, ln_idx):
            S.barrier()
            arena.reset()
            st = ln_alloc()
            st.end = arena.off
            wo = arena.take([8, D], BF16)
            for c in range(8):
                load_w(wo[:, c, :], w_out_d[c * 128:(c + 1) * 128, :], [("wo", c)])
            fsteps = []
            for tg in range(4):
                sl = tg_sl(tg)
                pend = None
                for m in range(8):
                    bank = m % 4
                    for c in range(8):
                        mm(ps[bank][:], wo[:, c, m * 128:(m + 1) * 128], obuf[:, c, sl], c == 0, False, [("wo", c), ("ob", c, tg)], bank)
                    mm(ps[bank][:], alphaI[:], xlo[:, m, sl], False, True, ["alphaI", ("xl", m, tg)], bank)
                    if pend is not None:
                        pend()
                    if fsteps:
                        if m == 0:
                            fsteps.pop(0)()
                        fsteps.pop(0)()
                    pend = ln_resid_chunk(st, m, tg, bank, None, 4, 5)
                pend()
                assert not fsteps
                fsteps = ln_finish(st, ln_idx, tg, 4, 5)
            return st, fsteps

        def ffn_ln(layer, ln_idx, st, pending):
            ensure_conv(layer)
            arena.off = st.end
            w1b = [arena.take([8, 512], BF16) for _ in range(2)]
            h1 = arena.take([32, 512], BF16)
            w2b = [arena.take([32, 128], BF16) for _ in range(2)]
            tmp = [arena.take([512], F32) for _ in range(2)]
            tmp += [obuf[:, 7 - i // 2, (i % 2) * 1024:(i % 2 + 1) * 1024].bitcast(F32) for i in range(4)]
            tmp_first = set()
            nw1 = 0
            nw2 = 0
            fsteps = list(pending)
            for tg in range(4):
                sl = tg_sl(tg)
                for hb in range(8):
                    wb_i = nw1 % 2
                    nw1 += 1
                    dma("sp", w1b[wb_i], w1t[layer, hb].rearrange("p (c n) -> p c n", c=8), reads=[("w1t", layer, hb)],
                        writes=[("w1b", wb_i)] + ([("wo", c) for c in range(8)] if nw1 <= 2 else []))
                    for mc in range(4):
                        ch = hb * 4 + mc
                        bank = (0, 1, 6, 7)[ch % 4]
                        for k in range(8):
                            mm(ps[bank][:], w1b[wb_i][:, k, mc * 128:(mc + 1) * 128], xhi[:, k, sl], k == 0, k == 7,
                               [("w1b", wb_i), ("xh", k, tg)], bank)
                        ti = ch % 6
                        extra = []
                        if ti >= 2 and ti not in tmp_first:
                            tmp_first.add(ti)
                            extra = [("ob", 7 - (ti - 2) // 2, tg_) for tg_ in range(4)]
                        S.add("act", lambda e, bank=bank, ti=ti, ch=ch: e.activation(out=tmp[ti], in_=ps[bank][:], func=AF.Identity,
                                                                                    bias=b1c[:, layer, ch:ch + 1], scale=1.0),
                              reads=["b1c"], writes=[PS(bank), ("tmp", ti)] + extra)
                        S.add("dve", lambda e, ti=ti, ch=ch: e.scalar_tensor_tensor(out=h1[:, ch, :], in0=tmp[ti], scalar=0.0, in1=tmp[ti],
                                                                                   op0=ALU.max, op1=ALU.mult),
                              reads=[("tmp", ti)], writes=[("h1", ch)])
                        if fsteps and ch % 3 == 2:
                            fsteps.pop(0)()
                while fsteps:
                    fsteps.pop(0)()
                pend = None
                for m in range(8):
                    wb_i = nw2 % 2
                    nw2 += 1
                    dma("pool", w2b[wb_i], w2t[layer, m].rearrange("p (k n) -> p k n", k=32),
                        reads=[("w2t", layer, m, part) for part in range(4)], writes=[("w2b", wb_i)])
                    bank = 2 + m % 2
                    for k in range(32):
                        mm(ps[bank][:], w2b[wb_i][:, k, :], h1[:, k, :], k == 0, k == 31, [("w2b", wb_i), ("h1", k)], bank)
                    if pend is not None:
                        pend()
                    pend = ln_resid_chunk(st, m, tg, bank, b2c[:, layer, m:m + 1], 4, 5)
                pend()
                assert not fsteps
                fsteps = ln_finish(st, ln_idx, tg, 4, 5)
            for f in fsteps:
                f()

        class AttnBufs:
            pass

        def attn_alloc(n_pt=6, need_tmp=False):
            ab = AttnBufs()
            ab.pt = [arena.take([512], BF16) for _ in range(n_pt)]
            ab.tmp = [arena.take([512], F32) for _ in range(2)] if need_tmp else None
            ab.rz = [arena.take([512], F32) for _ in range(2)]
            ab.oc = [arena.take([512], F32) for _ in range(2)]
            ab.zc = [arena.take([512], F32) for _ in range(2)]
            ab.n = 0
            ab.nt = 0
            return ab

        def attn_tiles(ab, tiles, o_bank, z_bank):
            attn_run(ab, [(tiles, o_bank, z_bank)])

        def attn_run(ab, streams, hooks=None, SB=(0, 1, 6)):
            merged = []
            mx = max(len(t) for t, _, _ in streams)
            for i in range(mx):
                for (tl, ob, zb) in streams:
                    if i < len(tl):
                        t = dict(tl[i])
                        t["ob"], t["zb"] = ob, zb
                        t["first"] = (i == 0)
                        t["last"] = (i == len(tl) - 1)
                        merged.append(t)
            tiles = merged
            nt = len(tiles)
            NSB = len(SB)
            LA = NSB - 1

            def score(i):
                t = tiles[i]
                bank = SB[(ab.n + i) % NSB]
                mm(ps[bank][:, 0:t["n1"] - t["n0"]], t["kT"], t["q"], True, True, t["kreads"], bank)

            for i in range(min(LA, nt)):
                score(i)
            hooks = list(hooks or [])
            for i, t in enumerate(tiles):
                if i + LA < nt:
                    score(i + LA)
                if hooks and i >= 3 and (i - 3) % 4 == 0:
                    hooks.pop(0)()
                bank = SB[(ab.n + i) % NSB]
                n = t["n1"] - t["n0"]
                pi = (ab.n + i) % len(ab.pt)
                pt = ab.pt[pi][:, 0:n]
                if t.get("dtab") is not None:
                    ti = ab.nt % 2
                    ab.nt += 1
                    tmp = ab.tmp[ti][:, 0:n]
                    S.add("dve", lambda e, t=t, tmp=tmp, bank=bank, n=n: e.scalar_tensor_tensor(
                        out=tmp, in0=t["dtab"], scalar=t["dscale"], in1=ps[bank][:, 0:n], op0=ALU.mult, op1=ALU.add),
                        reads=["dtab"], writes=[PS(bank), ("atmp", ti)])
                    S.add("act", lambda e, t=t, tmp=tmp, pt=pt: e.activation(out=pt, in_=tmp, func=AF.Exp, scale=t["scale"], bias=t.get("cbias", 0.0)),
                          reads=[("atmp", ti)], writes=[("pt", pi)])
                else:
                    S.add("act", lambda e, t=t, pt=pt, bank=bank, n=n: e.activation(out=pt, in_=ps[bank][:, 0:n], func=AF.Exp, scale=t["scale"]),
                          writes=[PS(bank), ("pt", pi)])
                mm(ps[t["ob"]][:, t["n0"]:t["n1"]], t["v"], pt, t["first"], t["last"], t["vreads"] + [("pt", pi)], t["ob"])
                mm(ps[t["zb"]][:, t["n0"]:t["n1"]], onesb[:], pt, t["first"], t["last"], ["onesb", ("pt", pi)], t["zb"])
            for hk in hooks:
                hk()
            ab.n += nt

        def layer0_attention():
            S.barrier()
            arena.reset()
            dtab = arena.take([5, 512], F32)
            dma("sp", dtab, c_dtab.rearrange("v p n -> p v n"), writes=["dtab"])
            wd = [arena.take([8, 384], BF16) for _ in range(2)]
            dkT = [arena.take([SEQ], BF16) for _ in range(2)]
            dv = [arena.take([16, 128], BF16) for _ in range(2)]
            dqz = [[arena.take([512], BF16) for _ in range(2)] for _ in range(2)]
            for mp_ in range(2):
                for b_ in range(2):
                    S.add("dve", lambda e, mp_=mp_, b_=b_: e.memset(dqz[mp_][b_], 0.0), writes=[("dqz", mp_, b_)])
            a32 = [arena.take([512], F32) for _ in range(3)]
            osq = arena.take([512], BF16)
            rsd = arena.take([512], F32)
            ab = attn_alloc(6, need_tmp=True)
            dscale = 64.0 ** -0.5
            it = 0
            tail = []
            for h in range(4):
                slope = 2.0 ** (-8.0 * (h + 1) / 4)
                wi = h % 2
                w4 = wd[wi].rearrange("p c (a n) -> p c a n", a=3)
                def load_wd(hh):
                    w4n = wd[hh % 2].rearrange("p c (a n) -> p c a n", a=3)
                    for a in range(3):
                        c0 = 672 + a * 512 + hh * 128
                        load_w(w4n[:, :, a, :], ev_w_in[:, c0:c0 + 128].rearrange("(c p) n -> p c n", p=128), [("wd", hh % 2, a)])
                if h == 0:
                    load_wd(0)
                if h + 1 < 4:
                    load_wd(h + 1)
                if h == 1:
                    ensure_conv(0)
                def head_prep(hh, tg):
                    wj = hh % 2
                    w4_ = wd[wj].rearrange("p c (a n) -> p c a n", a=3)
                    sl_ = tg_sl(tg)
                    for k in range(8):
                        mm(ps[6][:], w4_[:, k, 1, :], xhi[:, k, sl_], k == 0, k == 7, [("wd", wj, 1), ("xh", k, tg)], 6)
                    S.add("act", lambda e: e.activation(out=dkT[wj][:, sl_], in_=ps[6][:], func=AF.Copy), writes=[PS(6), ("dkT", wj)])
                    for t4 in range(4):
                        tt = tg * 4 + t4
                        for k in range(8):
                            mm(ps[7][:, t4 * 128:(t4 + 1) * 128], xhi[:, k, tt * 128:(tt + 1) * 128], w4_[:, k, 2, :], k == 0, k == 7,
                               [("wd", wj, 2), ("xh", k, tg)], 7)
                    S.add("dve", lambda e: e.tensor_copy(out=dv[wj][:, tg * 4:(tg + 1) * 4, :], in_=ps[7][:].rearrange("p (t n) -> p t n", t=4)),
                          writes=[PS(7), ("dv", wj)])

                if h == 0:
                    for tg in range(4):
                        head_prep(0, tg)
                for qg in range(4):
                    sl = tg_sl(qg)
                    qb = it % 2

                    def dq_proj(hh, qg_, qb_):
                        w4_ = wd[hh % 2].rearrange("p c (a n) -> p c a n", a=3)
                        sl_ = tg_sl(qg_)
                        for k in range(8):
                            mm(ps[6][:], w4_[:, k, 0, :], xhi[:, k, sl_], k == 0, k == 7, [("wd", hh % 2, 0), ("xh", k, qg_)], 6)
                        S.add("act", lambda e: e.activation(out=dqz[0][qb_][0:64, :], in_=ps[6][0:64, :], func=AF.Copy),
                              writes=[PS(6), ("dqz", 0, qb_)])
                        S.add("dve", lambda e: e.tensor_copy(out=dqz[1][qb_][64:128, :], in_=ps[6][64:128, :]),
                              writes=[PS(6), ("dqz", 1, qb_)])

                    if h + 1 < 4:
                        head_prep(h + 1, qg)
                    if it == 0:
                        dq_proj(0, 0, 0)
                    nxt = (h, qg + 1) if qg < 3 else ((h + 1, 0) if h < 3 else None)
                    if nxt is not None:
                        dq_proj(nxt[0], nxt[1], (it + 1) % 2)
                    streams = []
                    for mp in range(2):
                        rows = slice(mp * 64, mp * 64 + 64)
                        ob, zb = 2 + mp, 4 + mp
                        tiles = []
                        for kc in range(16):
                            delta = qg * 512 - kc * 128
                            if delta >= 128:
                                dt_ap, dsc, cb = dtab[:, 0, :], slope / dscale, -slope * delta
                            elif delta <= -512:
                                dt_ap, dsc, cb = dtab[:, 0, :], -slope / dscale, slope * delta
                            else:
                                dt_ap, dsc, cb = dtab[:, 1 + (-delta) // 128, :], slope / dscale, 0.0
                            tiles.append(dict(kT=dkT[wi][:, kc * 128:(kc + 1) * 128], q=dqz[mp][qb][:, :], kreads=[("dkT", wi), ("dqz", mp, qb)],
                                              v=dv[wi][:, kc, :], vreads=[("dv", wi)], n0=0, n1=512, scale=dscale,
                                              dtab=dt_ap, dscale=dsc, cbias=cb * 1.0))
                        streams.append((tiles, ob, zb))
                    attn_run(ab, streams, hooks=tail)
                    for mp in range(2):
                        ob, zb = 2 + mp, 4 + mp
                        S.add("act", lambda e, zb=zb, mp=mp: e.activation(out=ab.zc[mp][:], in_=ps[zb][:], func=AF.Copy), writes=[PS(zb), ("zc", mp)])
                        S.add("act", lambda e, ob=ob, mp=mp: e.activation(out=ab.oc[mp][:], in_=ps[ob][:], func=AF.Copy), writes=[PS(ob), ("oc", mp)])

                    def tailA():
                        S.add("dve", lambda e: e.tensor_tensor(out=a32[0][:], in0=ab.oc[0][:], in1=ab.zc[1][:], op=ALU.mult),
                              reads=[("oc", 0), ("zc", 1)], writes=[("a32", 0)])
                        S.add("dve", lambda e: e.tensor_tensor(out=a32[1][:], in0=ab.oc[1][:], in1=ab.zc[0][:], op=ALU.mult),
                              reads=[("oc", 1), ("zc", 0)], writes=[("a32", 1)])
                        S.add("dve", lambda e: e.scalar_tensor_tensor(out=a32[2][:], in0=a32[1][:], scalar=lamw[:, 4:5], in1=a32[0][:], op0=ALU.mult, op1=ALU.add),
                              reads=[("a32", 0), ("a32", 1), "lamw"], writes=[("a32", 2)])
                        S.add("act", lambda e: e.activation(out=osq[:], in_=a32[2][:], func=AF.Square), reads=[("a32", 2)], writes=["osq"])

                    def tailB():
                        S.add("dve", lambda e: e.tensor_tensor(out=ab.rz[0][:], in0=ab.zc[0][:], in1=ab.zc[1][:], op=ALU.mult),
                              reads=[("zc", 0), ("zc", 1)], writes=[("rz", 0)])
                        S.add("dve", lambda e: e.scalar_tensor_tensor(out=ab.rz[1][:], in0=ab.rz[0][:], scalar=RMS_EPS, in1=ab.rz[0][:], op0=ALU.mult, op1=ALU.mult),
                              reads=[("rz", 0)], writes=[("rz", 1)])
                        mm(ps[7][:], onesb[:], osq[:], True, True, ["onesb", "osq"], 7)
                        S.add("dve", lambda e: e.scalar_tensor_tensor(out=rsd[:], in0=ps[7][:], scalar=1.0 / 128, in1=ab.rz[1][:], op0=ALU.mult, op1=ALU.add),
                              reads=[("rz", 1)], writes=[PS(7), "rsd"])
                        S.add("act", lambda e: e.activation(out=rsd[:], in_=rsd[:], func=AF.Sqrt), writes=["rsd"])

                    def tailC(h=h, sl=sl, qg=qg):
                        S.add("dve", lambda e: e.reciprocal(out=rsd[:], in_=rsd[:]), writes=["rsd"])
                        S.add("dve", lambda e: e.scalar_tensor_tensor(out=obuf[:, 4 + h, sl], in0=a32[2][:], scalar=lamw[:, 5:6], in1=rsd[:], op0=ALU.mult, op1=ALU.mult),
                              reads=[("a32", 2), "rsd", "lamw"], writes=[("ob", 4 + h, qg)])

                    tail = [tailA, tailB, tailC]
                    it += 1

            for f in tail:
                f()
            S.barrier()
            arena.reset()
            ctab = arena.take([SEQ], F32)
            stab = arena.take([SEQ], F32)
            dma("sp", ctab[0:96, :], c_rope[0], writes=["ctab"])
            dma("sp", stab[0:96, :], c_rope[1], writes=["stab"])
            cqn = arena.take([3, SEQ], BF16)
            ckvn = arena.take([2, SEQ], BF16)
            kr = arena.take([SEQ], BF16)
            vall = arena.take([16, 512], BF16)
            wuq = arena.take([3, 800], BF16)
            wuqr = arena.take([3, 800], BF16)
            wukv = arena.take([2, 1024], BF16)
            mark = arena.off
            wlat = arena.take([8, 672], BF16)
            wkrot = arena.take([8, 96], BF16)
            cq32 = arena.take([3, 512], F32)
            sq = [arena.take([512], BF16) for _ in range(2)]
            t32 = [arena.take([512], F32) for _ in range(2)]
            rst = arena.take([512], F32)
            wvc = arena.take([2, 512], BF16)
            load_w(wlat, ev_w_in[:, 0:672].rearrange("(c p) n -> p c n", p=128), ["wlat"])
            S.add("dve", lambda e: e.memset(wuq[:, :, 768:800], 0.0), writes=["wuqpad"])
            load_w(wuq[:, :, 0:768], ev_w_uq.rearrange("(c p) n -> p c n", p=128), ["wuq"])
            load_w(wukv, ev_w_ukv.rearrange("(c p) n -> p c n", p=128), ["wukv"])
            ensure_conv(1)
            S.add("dve", lambda e: e.memset(wkrot[:], 0.0), writes=["wkrot"])
            S.add("dve", lambda e: e.tensor_scalar(out=wkrot[:, :, 64:80], in0=wlat[:, :, 656:672], scalar1=-1.0, scalar2=None, op0=ALU.mult),
                  reads=["wlat"], writes=["wkrot"])
            S.add("dve", lambda e: e.tensor_copy(out=wkrot[:, :, 80:96], in_=wlat[:, :, 640:656]), reads=["wlat"], writes=["wkrot"])
            S.add("dve", lambda e: e.tensor_copy(out=wvc.rearrange("p c (h j) -> p c h j", h=8),
                                                  in_=wukv.rearrange("p c (h j) -> p c h j", h=8)[:, :, :, 64:128]), reads=["wukv"], writes=["wvc"])
            S.add("dve", lambda e: e.memset(wuqr[:], 0.0), writes=["wuqr"])
            wuq4 = wuq[:, :, 0:768].rearrange("p c (h j) -> p c h j", h=8)
            wuqr4 = wuqr[:, :, 0:768].rearrange("p c (h j) -> p c h j", h=8)
            for c in range(3):
                S.add("dve", lambda e, c=c: e.tensor_scalar(out=wuqr4[:, c, :, 64:80], in0=wuq4[:, c, :, 80:96], scalar1=-1.0, scalar2=None, op0=ALU.mult),
                      reads=["wuq"], writes=["wuqr"])
                S.add("dve", lambda e, c=c: e.tensor_copy(out=wuqr4[:, c, :, 80:96], in_=wuq4[:, c, :, 64:80]), reads=["wuq"], writes=["wuqr"])

            def rms_block(srcbanks, ng, gcols, nfeat, dst, tg):
                sl = tg_sl(tg)
                for j, bank in enumerate(srcbanks):
                    S.add("act", lambda e, j=j, bank=bank: e.activation(out=cq32[:, j, :], in_=ps[bank][:], func=AF.Copy),
                          writes=[PS(bank), ("cq32", j)])
                    S.add("act", lambda e, j=j: e.activation(out=sq[j % 2], in_=cq32[:, j, :], func=AF.Square),
                          reads=[("cq32", j)], writes=[("sq", j % 2)])
                    mm(ps[7][:], onesb[:], sq[j % 2], j == 0, j == ng - 1, ["onesb", ("sq", j % 2)], 7)
                S.add("act", lambda e: e.activation(out=rst, in_=ps[7][:], func=AF.Sqrt, bias=epsc[:, 1:2], scale=1.0 / nfeat),
                      reads=["epsc"], writes=[PS(7), "rst"])
                S.add("dve", lambda e: e.reciprocal(out=rst, in_=rst), writes=["rst"])
                for j in range(ng):
                    S.add("dve", lambda e, j=j: e.scalar_tensor_tensor(out=dst[:, j, sl], in0=cq32[:, j, :], scalar=gcols[:, j:j + 1], in1=rst,
                                                                      op0=ALU.mult, op1=ALU.mult),
                          reads=[("cq32", j), "rst", "qng", "kvng"], writes=[(id(dst), j, tg)])

            for tg in range(4):
                sl = tg_sl(tg)
                for j in range(3):
                    for k in range(8):
                        mm(ps[j][:], wlat[:, k, j * 128:(j + 1) * 128], xhi[:, k, sl], k == 0, k == 7, ["wlat", ("xh", k, tg)], j)
                rms_block([0, 1, 2], 3, qng, 384.0, cqn, tg)
                for j in range(2):
                    for k in range(8):
                        mm(ps[3 + j][:], wlat[:, k, 384 + j * 128:384 + (j + 1) * 128], xhi[:, k, sl], k == 0, k == 7, ["wlat", ("xh", k, tg)], 3 + j)
                rms_block([3, 4], 2, kvng, 256.0, ckvn, tg)
                for k in range(8):
                    mm(ps[5][0:96, :], wlat[:, k, 576:672], xhi[:, k, sl], k == 0, k == 7, ["wlat", ("xh", k, tg)], 5)
                for k in range(8):
                    mm(ps[6][0:96, :], wkrot[:, k, :], xhi[:, k, sl], k == 0, k == 7, ["wkrot", ("xh", k, tg)], 6)
                S.add("dve", lambda e, sl=sl: e.tensor_tensor(out=t32[0][64:96, :], in0=ps[5][64:96, :], in1=ctab[64:96, sl], op=ALU.mult),
                      reads=["ctab"], writes=[PS(5), ("t32", 0)])
                S.add("dve", lambda e, sl=sl: e.tensor_tensor(out=t32[1][64:96, :], in0=ps[6][64:96, :], in1=stab[64:96, sl], op=ALU.mult),
                      reads=["stab"], writes=[PS(6), ("t32", 1)])
                S.add("dve", lambda e, sl=sl: e.tensor_tensor(out=kr[64:96, sl], in0=t32[0][64:96, :], in1=t32[1][64:96, :], op=ALU.add),
                      reads=[("t32", 0), ("t32", 1)], writes=[("kr", tg)])
                wv = wukv.rearrange("p c (h j) -> p c h j", h=8)
                for t4 in range(4):
                    tt = tg * 4 + t4
                    bank = 5 + t4 % 2
                    for c in range(2):
                        mm(ps[bank][:], ckvn[:, c, tt * 128:(tt + 1) * 128], wvc[:, c, :], c == 0, c == 1,
                           ["wvc", (id(ckvn), 0, tg), (id(ckvn), 1, tg)], bank)
                    S.add("act", lambda e, bank=bank, tt=tt: e.activation(out=vall[:, tt, :], in_=ps[bank][:], func=AF.Copy),
                          writes=[PS(bank), ("vall", tt)])

            S.barrier()
            arena.off = mark
            kh = [arena.take([SEQ], BF16) for _ in range(2)]
            qh = [arena.take([512], BF16) for _ in range(2)]
            qt = [arena.take([512], F32) for _ in range(2)]
            ab = attn_alloc(6)
            scale = 96.0 ** -0.5
            it = 0
            mtail = []
            for h in range(8):
                kb = h % 2
                hp, e_ = h // 2, h % 2
                def k_prep(hh):
                    kb_ = hh % 2
                    for tg in range(4):
                        sl_ = tg_sl(tg)
                        for c in range(2):
                            mm(ps[6][:], wukv[:, c, hh * 128:(hh + 1) * 128], ckvn[:, c, sl_], c == 0, c == 1, ["wukv", "ckvn_all"], 6)
                        S.add("dve", lambda e, sl_=sl_: e.tensor_copy(out=kh[kb_][0:64, sl_], in_=ps[6][0:64, :]), writes=[PS(6), ("kh", kb_)])
                    S.add("dve", lambda e: e.tensor_copy(out=kh[kb_][64:96, :], in_=kr[64:96, :]), reads=["kr_all"], writes=[("kh", kb_)])

                if h == 0:
                    k_prep(0)
                for qpair in range(2):
                    if qpair == 1 and h + 1 < 8:
                        k_prep(h + 1)
                    streams = []
                    for j in range(2):
                        qg = qpair * 2 + j
                        sl = tg_sl(qg)
                        qb = j
                        for c in range(3):
                            mm(ps[6][:], wuq[:, c, h * 96:h * 96 + 128], cqn[:, c, sl], c == 0, c == 2, ["wuq", "cqn_all"], 6)
                        S.add("dve", lambda e, sl=sl: e.tensor_tensor(out=qt[0][0:96, :], in0=ps[6][0:96, :], in1=ctab[0:96, sl], op=ALU.mult),
                              reads=["ctab"], writes=[PS(6), ("qt", 0)])
                        for c in range(3):
                            mm(ps[6][:], wuqr[:, c, h * 96:h * 96 + 128], cqn[:, c, sl], c == 0, c == 2, ["wuqr", "cqn_all"], 6)
                        S.add("dve", lambda e, sl=sl: e.tensor_tensor(out=qt[1][0:96, :], in0=ps[6][0:96, :], in1=stab[0:96, sl], op=ALU.mult),
                              reads=["stab"], writes=[PS(6), ("qt", 1)])
                        S.add("dve", lambda e, qb=qb: e.tensor_tensor(out=qh[qb][0:96, :], in0=qt[0][0:96, :], in1=qt[1][0:96, :], op=ALU.add),
                              reads=[("qt", 0), ("qt", 1)], writes=[("qh", qb)])
                        tiles = []
                        for kc in range(16):
                            tiles.append(dict(kT=kh[kb][0:96, kc * 128:(kc + 1) * 128], q=qh[qb][0:96, :], kreads=[("kh", kb), ("qh", qb)],
                                              v=vall[:, kc, hp * 128:(hp + 1) * 128], vreads=[("vall", kc)], n0=0, n1=512, scale=scale))
                        streams.append((tiles, 2 + j, 4 + j))
                    attn_run(ab, streams, hooks=mtail, SB=(0, 1, 6, 7))
                    rows = slice(e_ * 64, e_ * 64 + 64)
                    for j in range(2):
                        qg = qpair * 2 + j
                        sl = tg_sl(qg)
                        ob, zb, ri = 2 + j, 4 + j, j
                        S.add("act", lambda e, zb=zb, ri=ri, rows=rows: e.activation(out=ab.zc[ri][rows, :], in_=ps[zb][rows, :], func=AF.Copy), writes=[PS(zb), ("zc", ri)])
                        S.add("act", lambda e, ob=ob, ri=ri, rows=rows: e.activation(out=ab.oc[ri][rows, :], in_=ps[ob][rows, :], func=AF.Copy), writes=[PS(ob), ("oc", ri)])
                    mtail = []
                    for j in range(2):
                        def mt(ri=j, rows=rows, hp=hp, sl=tg_sl(qpair * 2 + j), qg=qpair * 2 + j):
                            S.add("dve", lambda e: e.reciprocal(out=ab.rz[ri][rows, :], in_=ab.zc[ri][rows, :]), reads=[("zc", ri)], writes=[("rz", ri)])
                            S.add("dve", lambda e: e.tensor_tensor(out=obuf[rows, hp, sl], in0=ab.oc[ri][rows, :], in1=ab.rz[ri][rows, :], op=ALU.mult),
                                  reads=[("rz", ri), ("oc", ri)], writes=[("ob", hp, qg)])
                        mtail.append(mt)
            for f in mtail:
                f()

        def layer1_attention():
            S.barrier()
            arena.reset()
            swat = arena.take([384], F32)
            dma("sp", swat, c_swa, writes=["dtab"])
            wk = arena.take([8, 512], BF16)
            for c in range(8):
                load_w(wk[:, c, :], od_w_in[c * 128:(c + 1) * 128, 1024:1536], ["wk"])
            kT = arena.take([4, SEQ], BF16)
            wkd = arena.take([8, 4, 128], BF16)
            vdup = arena.take([16, 4, 128], BF16)
            wk4 = wk[:, :, 0:256].rearrange("p c (g j) -> p c g j", g=4)
            S.add("dve", lambda e: e.tensor_copy(out=wkd[:, :, :, 0:64], in_=wk4), reads=["wk"], writes=["wkd"])
            S.add("dve", lambda e: e.tensor_copy(out=wkd[:, :, :, 64:128], in_=wk4), reads=["wk"], writes=["wkd"])
            wq = [arena.take([8, 128], BF16) for _ in range(2)]
            qz = [[arena.take([SEQ], BF16) for _ in range(2)] for _ in range(2)]
            for e0 in range(2):
                for b_ in range(2):
                    S.add("dve", lambda e, e0=e0, b_=b_: e.memset(qz[e0][b_], 0.0), writes=[("qz", e0, b_)])
            ab = attn_alloc(6, need_tmp=True)
            sscale = 64.0 ** -0.5
            for tg in range(4):
                sl = tg_sl(tg)
                for j in range(4):
                    for k in range(8):
                        mm(ps[6][:], wkd[:, k, j, :], xhi[:, k, sl], k == 0, k == 7, ["wkd", ("xh", k, tg)], 6)
                    S.add("act", lambda e, j=j, sl=sl: e.activation(out=kT[:, j, sl], in_=ps[6][:], func=AF.Copy), writes=[PS(6), "kT"])
                for t4 in range(4):
                    tt = tg * 4 + t4
                    for k in range(8):
                        mm(ps[6][:, 0:256], xhi[:, k, tt * 128:(tt + 1) * 128], wk[:, k, 256:512], k == 0, k == 7, ["wk", ("xh", k, tg)], 6)
                    pv = ps[6][:, 0:256].rearrange("p (g j) -> p g j", g=4)
                    S.add("dve", lambda e, tt=tt, pv=pv: e.tensor_copy(out=vdup[:, tt, :, 0:64], in_=pv), writes=[PS(6), "vdup"])
                    S.add("act", lambda e, tt=tt, pv=pv: e.activation(out=vdup[:, tt, :, 64:128], in_=pv, func=AF.Copy), writes=[PS(6), "vdup"])
            it = 0
            stail = []
            for hp in range(8):
                wi = hp % 2
                if hp == 0:
                    load_w(wq[0], od_w_in[:, 0:128].rearrange("(c p) n -> p c n", p=128), [("wq", 0)])
                if hp + 1 < 8:
                    load_w(wq[(hp + 1) % 2], od_w_in[:, (hp + 1) * 128:(hp + 2) * 128].rearrange("(c p) n -> p c n", p=128), [("wq", (hp + 1) % 2)])
                def q_proj(hpp, tg):
                    wj = hpp % 2
                    sl_ = tg_sl(tg)
                    for k in range(8):
                        mm(ps[6][:], wq[wj][:, k, :], xhi[:, k, sl_], k == 0, k == 7, [("wq", wj), ("xh", k, tg)], 6)
                    S.add("act", lambda e: e.activation(out=qz[0][wj][0:64, sl_], in_=ps[6][0:64, :], func=AF.Copy), writes=[PS(6), ("qz", 0, wj)])
                    S.add("dve", lambda e: e.tensor_copy(out=qz[1][wj][64:128, sl_], in_=ps[6][64:128, :]), writes=[PS(6), ("qz", 1, wj)])

                if hp == 0:
                    for tg in range(4):
                        q_proj(0, tg)
                for qg in range(4):
                    sl = tg_sl(qg)
                    qb0 = qg * 4
                    if hp + 1 < 8:
                        q_proj(hp + 1, qg)
                    streams = []
                    for e_ in range(2):
                        h = hp * 2 + e_
                        g = h // 4
                        slope = 2.0 ** (-8.0 * (h + 1) / 16)
                        rows = slice(e_ * 64, e_ * 64 + 64)
                        tiles = []
                        for kc in range(qb0 - 1, qb0 + 5):
                            if kc < 0 or kc > 15:
                                continue
                            qa = max(qb0, kc - 1)
                            qz_ = min(qb0 + 3, kc + 1)
                            n0, n1 = (qa - qb0) * 128, (qz_ - qb0 + 1) * 128
                            toff = (qa - (kc - 1)) * 128
                            tiles.append(dict(kT=kT[:, g, kc * 128:(kc + 1) * 128], q=qz[e_][wi][:, qa * 128:(qz_ + 1) * 128],
                                              kreads=["kT", ("qz", e_, wi)], v=vdup[:, kc, g, :], vreads=["vdup"], n0=n0, n1=n1, scale=sscale,
                                              dtab=swat[:, toff:toff + (n1 - n0)], dscale=-slope / sscale, cbias=0.0))
                        streams.append((tiles, 2 + e_, 4 + e_))
                    attn_run(ab, streams, hooks=stail, SB=(0, 1, 6, 7))
                    rb = it % 2
                    it += 1
                    for e_ in range(2):
                        h = hp * 2 + e_
                        rows = slice(e_ * 64, e_ * 64 + 64)
                        ob, zb = 2 + e_, 4 + e_
                        S.add("act", lambda e, zb=zb, rb=rb, rows=rows, h=h: e.activation(out=ab.zc[rb][rows, :], in_=ps[zb][rows, :], func=AF.Identity,
                                                                                          bias=sinkt[rows, h:h + 1], scale=1.0),
                              reads=["sinkt"], writes=[PS(zb), ("zc", rb)])
                        S.add("act", lambda e, ob=ob, rb=rb, rows=rows: e.activation(out=ab.oc[rb][rows, :], in_=ps[ob][rows, :], func=AF.Copy), writes=[PS(ob), ("oc", rb)])

                    def stl1(rb=rb):
                        S.add("dve", lambda e: e.reciprocal(out=ab.rz[rb][:, 0:256], in_=ab.zc[rb][:, 0:256]), reads=[("zc", rb)], writes=[("rz", rb, 0)])

                    def stl2(rb=rb, hp=hp, sl=sl, qg=qg):
                        S.add("dve", lambda e: e.reciprocal(out=ab.rz[rb][:, 256:512], in_=ab.zc[rb][:, 256:512]), reads=[("zc", rb)], writes=[("rz", rb, 1)])
                        S.add("dve", lambda e: e.tensor_tensor(out=obuf[:, hp, sl], in0=ab.oc[rb][:], in1=ab.rz[rb][:], op=ALU.mult),
                              reads=[("rz", rb, 0), ("rz", rb, 1), ("oc", rb)], writes=[("ob", hp, qg)])
                    stail = [stl1, stl2]
            for f in stail:
                f()

        final = []
        for s in range(nseq):
            S.barrier()
            if s == 0:
                load_x(s)
            for layer in layers:
                if layer in ("a0", "a1"):
                    (layer0_attention if layer == "a0" else layer1_attention)()
                    S.barrier()
                    for c in range(8):
                        S.add("dve", lambda e, c=c: e.tensor_copy(out=xhi[:, c, :], in_=obuf[:, c, :]))
                    S.add("dve", lambda e: e.memset(xlo[:], 0.0))
                    S.barrier()
                    continue
                if layer == 0:
                    layer0_attention()
                    st_, pend_ = out_proj_ln(ev_w_out, 0)
                    ffn_ln(0, 1, st_, pend_)
                else:
                    layer1_attention()
                    st_, pend_ = out_proj_ln(od_w_out, 2)
                    ffn_ln(1, 3, st_, pend_)
            S.barrier()
            if s + 1 < nseq:
                final += store_load_x(s, s + 1)
            else:
                final += store_x(s)
        S.emit(es, final_wait_ops=final)
    return nc


def _const_tables():
    half = 16
    inv = (10000.0 ** (-np.arange(half, dtype=np.float32) / half)).astype(np.float32)
    pos = np.arange(SEQ, dtype=np.float32)
    ang = pos[None, :] * inv[:, None]
    cos = np.cos(ang).astype(np.float32)
    sin = np.sin(ang).astype(np.float32)
    rope = np.zeros((2, 96, SEQ), np.float32)
    rope[0, 0:64] = 1.0
    rope[0, 64:80] = cos
    rope[0, 80:96] = cos
    rope[1, 64:80] = sin
    rope[1, 80:96] = sin
    koff = np.arange(128, dtype=np.float32)[:, None]
    qoff = np.arange(512, dtype=np.float32)[None, :]
    dtab = np.zeros((5, 128, 512), np.float32)
    dtab[0] = -(qoff - koff)
    for v in range(4):
        dtab[1 + v] = -np.abs(qoff - koff - 128.0 * v)
    c = np.arange(384, dtype=np.float32)[None, :]
    dd = np.abs((c - 128.0) - koff)
    swa = np.where(dd <= 128.0, dd, 1.0e6).astype(np.float32)
    return rope, dtab, swa


_CACHE = {}


def _get_nc(layers):
    key = tuple(layers)
    if key not in _CACHE:
        _CACHE[key] = build_program(layers=layers)
    return _CACHE[key]


def _run(x, weights, layers):
    nc = _get_nc(layers)
    rope, dtab, swa = _const_tables()
    common = dict(weights)
    common.update(c_ident=np.eye(128, dtype=np.float32), c_rope=rope, c_dtab=dtab, c_swa=swa)
    in_maps = []
    for c in range(8):
        m = dict(common)
        m["x"] = np.ascontiguousarray(x[c * NSEQ:(c + 1) * NSEQ])
        in_maps.append(m)
    res = run_bass_kernel_spmd(nc, in_maps, core_ids=list(range(8)))
    return np.concatenate([r["out"] for r in res.results], axis=0)


def _pack_weights(ev_w_in, ev_q_norm, ev_kv_norm, ev_w_uq, ev_w_ukv, ev_lam_q1, ev_lam_k1, ev_lam_q2, ev_lam_k2,
                  ev_diff_norm, ev_w_out, od_w_in, od_sink, od_w_out, ln1_g, ln1_b, ln2_g, ln2_b,
                  ffn_w1, ffn_b1, ffn_w2, ffn_b2):
    f = lambda a: np.ascontiguousarray(np.asarray(a, dtype=np.float32))
    col = lambda v: np.ascontiguousarray(f(v).reshape(-1, 128).T)
    ln_p = np.stack([np.stack([f(ln1_g)[0], f(ln1_b)[0]]), np.stack([f(ln2_g)[0], f(ln2_b)[0]]),
                     np.stack([f(ln1_g)[1], f(ln1_b)[1]]), np.stack([f(ln2_g)[1], f(ln2_b)[1]])])
    ln_p = np.ascontiguousarray(ln_p.reshape(4, 2, 8, 128).transpose(3, 0, 1, 2).reshape(128, 64))
    b1 = np.ascontiguousarray(f(ffn_b1).reshape(2, 32, 128).transpose(2, 0, 1).reshape(128, 64))
    b2 = np.ascontiguousarray(f(ffn_b2).reshape(2, 8, 128).transpose(2, 0, 1).reshape(128, 16))
    lam = np.concatenate([f(ev_lam_q1)[0], f(ev_lam_k1)[0], f(ev_lam_q2)[0], f(ev_lam_k2)[0]])
    rep = lambda v: np.ascontiguousarray(np.broadcast_to(v[None, :], (128, v.shape[0])))
    return dict(
        ev_w_in=f(ev_w_in)[0], ev_q_norm=col(f(ev_q_norm)[0]), ev_kv_norm=col(f(ev_kv_norm)[0]), ev_w_uq=f(ev_w_uq)[0],
        ev_w_ukv=f(ev_w_ukv)[0], ev_lam=rep(lam),
        ev_diff_norm=col(f(ev_diff_norm)[0]), ev_w_out=f(ev_w_out)[0], od_w_in=f(od_w_in)[0], od_sink=rep(f(od_sink)[0]),
        od_w_out=f(od_w_out)[0], ln_p=ln_p, ffn_w1=f(ffn_w1), ffn_b1=b1, ffn_w2=f(ffn_w2),
        ffn_b2=b2)


LAUNCH_PLAN = [(0, 1)]


def kernel(x, **w):
    weights = _pack_weights(**w)
    cur = np.asarray(x, dtype=np.float32)
    for layers in LAUNCH_PLAN:
        cur = _run(cur, weights, layers)
    return cur.astype(np.float32)
```

```python
import contextlib
import math
import numpy as np
import concourse.bass as bass
import concourse.mybir as mybir
from concourse.bass_utils import run_bass_kernel_spmd

F32 = mybir.dt.float32
BF16 = mybir.dt.bfloat16
AF = mybir.ActivationFunctionType
ALU = mybir.AluOpType

D = 1024
SEQ = 2048
NSEQ = 2
DFF = 4096
ALPHA = 4 ** 0.25
LN_EPS = 1e-5
RMS_EPS = 1e-6
EVEN_IN = 2208
ODD_IN = 1536
LAMBDA_INIT0 = 0.8 - 0.6 * math.exp(0.0)


class _Op:
    __slots__ = ("eng", "fn", "deps", "dma", "sig", "needed", "idx")

    def __init__(self, eng, fn, deps, dma, idx):
        self.eng, self.fn, self.deps, self.dma, self.idx = eng, fn, deps, dma, idx
        self.sig = None
        self.needed = False


class Sched:
    ENGS = ("pe", "act", "dve", "pool", "sp")

    def __init__(self, nc, n_dma_sems=32):
        self.nc = nc
        self.ops = []
        self.last_w = {}
        self.readers = {}
        self.n_dma_sems = n_dma_sems
        self.last_barrier = 0
        self.persist_ops = set()
        self.persist_w = {}

    def add(self, eng, fn, reads=(), writes=(), dma=False, persist=False):
        idx = len(self.ops)
        if persist:
            self.persist_ops.add(idx)
            for r in writes:
                self.persist_w[r] = idx
        deps = set()
        for r in reads:
            w = self.last_w.get(r)
            if w is not None:
                deps.add(w)
        for r in writes:
            w = self.last_w.get(r)
            if w is not None:
                deps.add(w)
            deps.update(self.readers.get(r, ()))
        for r in reads:
            self.readers.setdefault(r, []).append(idx)
        for r in writes:
            self.last_w[r] = idx
            self.readers[r] = []
        deps.discard(idx)
        self.ops.append(_Op(eng, fn, deps, dma, idx))
        return idx

    def barrier(self):
        last = {}
        for op in self.ops:
            if not op.dma and op.fn is not None:
                last[op.eng] = op.idx
        deps = set(last.values())
        deps.update(op.idx for op in self.ops[self.last_barrier:] if op.dma and op.idx not in self.persist_ops)
        for e in self.ENGS:
            idx = len(self.ops)
            self.ops.append(_Op(e, None, set(deps), False, idx))
        self.last_barrier = len(self.ops)
        self.last_w = dict(self.persist_w)
        self.readers = {}

    @staticmethod
    def _skip(dop, op):
        return dop.eng == "pe" and op.eng == "pe" and not dop.dma and not op.dma

    def emit(self, es, final_wait_ops=()):
        nc = self.nc
        ops = self.ops
        for op in ops:
            for d in op.deps:
                if not self._skip(ops[d], op):
                    ops[d].needed = True
        for i in final_wait_ops:
            ops[i].needed = True
        eng_sem = {e: es.enter_context(nc.semaphore("s_" + e)) for e in self.ENGS}
        dma_sems = [es.enter_context(nc.semaphore("d%d" % i)) for i in range(self.n_dma_sems)]
        eng_cnt = {e: 0 for e in self.ENGS}
        dma_cnt = [0] * self.n_dma_sems
        rr = 0
        for op in ops:
            if op.dma:
                s = rr % self.n_dma_sems
                rr += 1
                prev = dma_cnt[s]
                dma_cnt[s] += 16
                op.sig = (dma_sems[s], dma_cnt[s], prev)
            elif op.needed:
                eng_cnt[op.eng] += 1
                op.sig = (eng_sem[op.eng], eng_cnt[op.eng], None)
        self.counts = dict(eng_cnt)
        by_eng = {e: [op for op in ops if op.eng == e] for e in self.ENGS}
        block = es.enter_context(nc.Block())

        def replay(ename, e):
            known = {}

            def wait(sem, val):
                if val <= 0 or known.get(sem.num, 0) >= val:
                    return
                e.wait_ge(sem, val)
                known[sem.num] = val

            for op in by_eng[ename]:
                for d in sorted(op.deps):
                    dop = ops[d]
                    if self._skip(dop, op):
                        continue
                    wait(dop.sig[0], dop.sig[1])
                if op.fn is None:
                    continue
                if op.dma:
                    wait(op.sig[0], op.sig[2])
                ins = op.fn(e)
                if op.dma:
                    ins.then_inc(op.sig[0], 16)
                elif op.sig is not None:
                    ins.then_inc(op.sig[0], 1)
            if ename == "sp":
                for i in final_wait_ops:
                    wait(ops[i].sig[0], ops[i].sig[1])

        @block.sync
        def _(e):
            replay("sp", e)

        @block.tensor
        def _(e):
            replay("pe", e)

        @block.scalar
        def _(e):
            replay("act", e)

        @block.vector
        def _(e):
            replay("dve", e)

        @block.gpsimd
        def _(e):
            replay("pool", e)


class Arena:
    def __init__(self, nc, es, name, nbytes):
        self.nbytes = nbytes
        self.t = es.enter_context(nc.sbuf_tensor(name, [128, nbytes // 2], BF16))
        self.off = 0

    def reset(self):
        self.off = 0

    def take(self, shape, dtype):
        n = 1
        for s in shape:
            n *= s
        nb = n * (4 if dtype == F32 else 2)
        nb = (nb + 31) // 32 * 32
        assert self.off + nb <= self.nbytes, ("arena overflow", self.off, nb, self.nbytes)
        ap = self.t[:, self.off // 2:(self.off + nb) // 2]
        self.off += nb
        if dtype == F32:
            ap = ap.bitcast(F32)
        ap = ap[:, 0:n]
        if len(shape) == 2:
            ap = ap.rearrange("p (a b) -> p a b", a=shape[0])
        elif len(shape) == 3:
            ap = ap.rearrange("p (a b c) -> p a b c", a=shape[0], b=shape[1])
        return ap


def build_program(layers=(0, 1), nseq=NSEQ):
    nc = bass.Bass("TRN2", target_bir_lowering=False)

    def din(name, shape):
        return nc.dram_tensor(name, list(shape), F32, kind="ExternalInput").ap()

    x_d = din("x", [nseq, SEQ, D])
    ev_w_in = din("ev_w_in", [D, EVEN_IN])
    ev_q_norm = din("ev_q_norm", [128, 3])
    ev_kv_norm = din("ev_kv_norm", [128, 2])
    ev_w_uq = din("ev_w_uq", [384, 768])
    ev_w_ukv = din("ev_w_ukv", [256, 1024])
    lamv = din("ev_lam", [128, 256])
    ev_diff_norm = din("ev_diff_norm", [128, 1])
    ev_w_out = din("ev_w_out", [D, D])
    od_w_in = din("od_w_in", [D, ODD_IN])
    od_sink = din("od_sink", [128, 16])
    od_w_out = din("od_w_out", [D, D])
    ln_p = din("ln_p", [128, 64])
    ffn_w1 = din("ffn_w1", [2, D, DFF])
    ffn_b1 = din("ffn_b1", [128, 64])
    ffn_w2 = din("ffn_w2", [2, DFF, D])
    ffn_b2 = din("ffn_b2", [128, 16])
    c_ident = din("c_ident", [128, 128])
    c_rope = din("c_rope", [2, 96, SEQ])
    c_dtab = din("c_dtab", [5, 128, 512])
    c_swa = din("c_swa", [128, 384])
    out_d = nc.dram_tensor("out", [nseq, SEQ, D], F32, kind="ExternalOutput").ap()
    w1t = nc.dram_tensor("w1t", [2, 8, 128, 8 * 512], BF16, kind="Internal").ap()
    w2t = nc.dram_tensor("w2t", [2, 8, 128, 32 * 128], BF16, kind="Internal").ap()

    es = contextlib.ExitStack()
    S = Sched(nc)
    with es:
        sb = lambda name, shape, dt: es.enter_context(nc.sbuf_tensor(name, shape, dt))
        xhi = sb("xhi", [128, 8, SEQ], BF16)
        xlo = sb("xlo", [128, 8, SEQ], BF16)
        obuf = sb("obuf", [128, 8, SEQ], BF16)
        identb = sb("identb", [128, 128], BF16)
        onesb = sb("onesb", [128, 128], BF16)
        alphaI = sb("alphaI", [128, 128], BF16)
        epsc = sb("epsc", [128, 2], F32)
        lnp = sb("lnp", [128, 4, 2, 8], F32)
        b1c = sb("b1c", [128, 2, 32], F32)
        b2c = sb("b2c", [128, 2, 8], F32)
        qng = sb("qng", [128, 3], F32)
        kvng = sb("kvng", [128, 2], F32)
        dng = sb("dng", [128, 1], F32)
        lamt = sb("lamt", [128, 4, 64], F32)
        lamw = sb("lamw", [128, 8], F32)
        sinkt = sb("sinkt", [128, 16], F32)
        arena = Arena(nc, es, "arena", 106 * 1024)
        ps = [es.enter_context(nc.psum_tensor("ps%d" % i, [128, 512], F32)) for i in range(8)]
        PS = lambda i: ("ps", i)

        def dma(eng, out, in_, reads=(), writes=()):
            return S.add(eng, lambda e: e.dma_start(out=out, in_=in_), reads=reads, writes=writes, dma=True)

        dma("pool", identb[:], c_ident, writes=["identb"])
        S.add("dve", lambda e: e.memset(onesb[:], 1.0), writes=["onesb"])
        S.add("act", lambda e: e.activation(out=alphaI[:], in_=identb[:], func=AF.Copy, scale=ALPHA), reads=["identb"], writes=["alphaI"])
        S.add("dve", lambda e: e.memset(epsc[:, 0:1], LN_EPS), writes=["epsc"])
        S.add("dve", lambda e: e.memset(epsc[:, 1:2], RMS_EPS), writes=["epsc"])
        dma("sp", lnp[:].rearrange("p l t c -> p (l t c)"), ln_p, writes=["lnp"])
        dma("sp", b1c[:].rearrange("p l c -> p (l c)"), ffn_b1, writes=["b1c"])
        dma("sp", b2c[:].rearrange("p l c -> p (l c)"), ffn_b2, writes=["b2c"])
        dma("sp", qng[:], ev_q_norm, writes=["qng"])
        dma("sp", kvng[:], ev_kv_norm, writes=["kvng"])
        dma("sp", dng[:], ev_diff_norm, writes=["dng"])
        dma("sp", lamt[:].rearrange("p a b -> p (a b)"), lamv, writes=["lamt"])
        dma("sp", sinkt[:], od_sink, writes=["sinkt"])
        S.add("dve", lambda e: e.tensor_tensor(out=lamt[:, 0, :], in0=lamt[:, 0, :], in1=lamt[:, 1, :], op=ALU.mult), writes=["lamt"])
        S.add("dve", lambda e: e.tensor_tensor(out=lamt[:, 2, :], in0=lamt[:, 2, :], in1=lamt[:, 3, :], op=ALU.mult), writes=["lamt"])
        S.add("dve", lambda e: e.reduce_sum(out=lamw[:, 0:1], in_=lamt[:, 0, :], axis=mybir.AxisListType.X), reads=["lamt"], writes=["lamw"])
        S.add("dve", lambda e: e.reduce_sum(out=lamw[:, 1:2], in_=lamt[:, 2, :], axis=mybir.AxisListType.X), reads=["lamt"], writes=["lamw"])
        S.add("act", lambda e: e.activation(out=lamw[:, 2:4], in_=lamw[:, 0:2], func=AF.Exp), writes=["lamw"])
        S.add("dve", lambda e: e.tensor_tensor(out=lamw[:, 4:5], in0=lamw[:, 3:4], in1=lamw[:, 2:3], op=ALU.subtract), writes=["lamw"])
        S.add("dve", lambda e: e.tensor_scalar(out=lamw[:, 4:5], in0=lamw[:, 4:5], scalar1=-LAMBDA_INIT0, scalar2=None, op0=ALU.add), writes=["lamw"])
        S.add("dve", lambda e: e.tensor_scalar(out=lamw[:, 5:6], in0=dng[:, 0:1], scalar1=1.0 - LAMBDA_INIT0, scalar2=None, op0=ALU.mult), reads=["dng"], writes=["lamw"])
        S.add("act", lambda e: e.activation(out=sinkt[:], in_=sinkt[:], func=AF.Exp), writes=["sinkt"])

        def mm(out, lhsT, rhs, start, stop, reads, bank):
            return S.add("pe", lambda e: e.matmul(out, lhsT=lhsT, rhs=rhs, start=start, stop=stop), reads=reads, writes=[PS(bank)])

        def tg_sl(tg):
            return slice(tg * 512, (tg + 1) * 512)

        def load_w(eng_out, src, writes):
            return dma("pool", eng_out, src, writes=writes)

        conv_done = set()
        conv_cnt = [0]

        def ensure_conv(l):
            if l in conv_done:
                return
            conv_done.add(l)
            def chain():
                i = conv_cnt[0]
                conv_cnt[0] += 1
                return ([("conv", i - 6)] if i >= 6 else []), ("conv", i)
            for hb in range(8):
                rd, wr = chain()
                S.add("pool", lambda e, hb=hb: e.dma_start(out=w1t[l, hb].rearrange("p (c n) -> p c n", c=8),
                                                            in_=ffn_w1[l, :, hb * 512:(hb + 1) * 512].rearrange("(c p) n -> p c n", p=128)),
                      reads=rd, writes=[("w1t", l, hb), wr], dma=True, persist=True)
            for m in range(8):
                for part in range(4):
                    rd, wr = chain()
                    S.add("pool", lambda e, m=m, part=part: e.dma_start(
                        out=w2t[l, m].rearrange("p (k n) -> p k n", k=32)[:, part * 8:(part + 1) * 8, :],
                        in_=ffn_w2[l, part * 1024:(part + 1) * 1024, m * 128:(m + 1) * 128].rearrange("(k p) n -> p k n", p=128)),
                        reads=rd, writes=[("w2t", l, m, part), wr], dma=True, persist=True)

        def load_x(s):
            arena.reset()
            xin = [arena.take([D], F32) for _ in range(4)]
            hit = [arena.take([D], BF16) for _ in range(4)]
            lot = [arena.take([D], BF16) for _ in range(4)]
            for tt in range(16):
                b = tt % 4
                tg = tt // 4
                dma("sp" if tt % 2 == 0 else "pool", xin[b], x_d[s, tt * 128:(tt + 1) * 128, :], writes=[("xin", b)])
                S.add("act", lambda e, b=b: e.activation(out=hit[b], in_=xin[b], func=AF.Copy), reads=[("xin", b)], writes=[("hit", b)])
                S.add("dve", lambda e, b=b: e.tensor_tensor(out=lot[b], in0=xin[b], in1=hit[b], op=ALU.subtract),
                      reads=[("xin", b), ("hit", b)], writes=[("lot", b)])
                for (src, dst, nm, bank) in ((hit, xhi, "xh", 0 + 2 * b), (lot, xlo, "xl", 1 + 2 * b)):
                    pb = ps[bank][:].bitcast(BF16)
                    for c in range(8):
                        S.add("pe", lambda e, c=c, pb=pb, src=src, b=b: e.transpose(pb[:, c * 128:(c + 1) * 128], src[b][:, c * 128:(c + 1) * 128], identb[:]),
                              reads=[(nm[1] == "h" and ("hit", b) or ("lot", b)), "identb"], writes=[PS(bank)])
                    S.add("dve" if nm == "xh" else "act",
                          (lambda e, pb=pb, dst=dst, tt=tt: e.tensor_copy(out=dst[:, :, tt * 128:(tt + 1) * 128], in_=pb.rearrange("p (c t) -> p c t", c=8)))
                          if nm == "xh" else
                          (lambda e, pb=pb, dst=dst, tt=tt: e.activation(out=dst[:, :, tt * 128:(tt + 1) * 128], in_=pb.rearrange("p (c t) -> p c t", c=8), func=AF.Copy)),
                          writes=[PS(bank)] + [(nm, c, tg) for c in range(8)])

        def store_x(s):
            arena.reset()
            ta = [arena.take([D], F32) for _ in range(4)]
            to = [arena.take([D], F32) for _ in range(4)]
            outs = []
            for tt in range(16):
                b = tt % 4
                tg = tt // 4
                for (src, nm, bank) in ((xhi, "xh", 0 + 2 * b), (xlo, "xl", 1 + 2 * b)):
                    pb = ps[bank][:].bitcast(BF16)
                    for c in range(8):
                        S.add("pe", lambda e, c=c, pb=pb, src=src, tt=tt: e.transpose(pb[:, c * 128:(c + 1) * 128], src[:, c, tt * 128:(tt + 1) * 128], identb[:]),
                              reads=[(nm, c, tg), "identb"], writes=[PS(bank)])
                pa = ps[0 + 2 * b][:].bitcast(BF16)
                pl = ps[1 + 2 * b][:].bitcast(BF16)
                S.add("act", lambda e, b=b, pa=pa: e.activation(out=ta[b], in_=pa, func=AF.Copy), writes=[PS(0 + 2 * b), ("ta", b)])
                S.add("dve", lambda e, b=b, pl=pl: e.tensor_tensor(out=to[b], in0=pl, in1=ta[b], op=ALU.add),
                      reads=[("ta", b)], writes=[PS(1 + 2 * b), ("to", b)])
                outs.append(dma("sp" if tt % 2 == 0 else "pool", out_d[s, tt * 128:(tt + 1) * 128, :], to[b], reads=[("to", b)]))
            return outs

        def store_load_x(s_out, s_in):
            arena.reset()
            ta = [arena.take([D], F32) for _ in range(2)]
            to = [arena.take([D], F32) for _ in range(2)]
            xin = [arena.take([D], F32) for _ in range(2)]
            hit = [arena.take([D], BF16) for _ in range(2)]
            lot = [arena.take([D], BF16) for _ in range(2)]
            outs = []
            for tt in range(16):
                b = tt % 2
                tg = tt // 4
                cols = slice(tt * 128, (tt + 1) * 128)
                dma("pool", xin[b], x_d[s_in, cols, :], writes=[("xin", b)])
                for (src, nm, bank) in ((xhi, "xh", 0 + 2 * b), (xlo, "xl", 1 + 2 * b)):
                    pb = ps[bank][:].bitcast(BF16)
                    for c in range(8):
                        S.add("pe", lambda e, c=c, pb=pb, src=src, cols=cols: e.transpose(pb[:, c * 128:(c + 1) * 128], src[:, c, cols], identb[:]),
                              reads=[(nm, "t", tt), "identb"], writes=[PS(bank)])
                pa = ps[0 + 2 * b][:].bitcast(BF16)
                pl = ps[1 + 2 * b][:].bitcast(BF16)
                S.add("act", lambda e, b=b, pa=pa: e.activation(out=ta[b], in_=pa, func=AF.Copy), writes=[PS(0 + 2 * b), ("ta", b)])
                S.add("dve", lambda e, b=b, pl=pl: e.tensor_tensor(out=to[b], in0=pl, in1=ta[b], op=ALU.add),
                      reads=[("ta", b)], writes=[PS(1 + 2 * b), ("to", b)])
                outs.append(dma("sp", out_d[s_out, cols, :], to[b], reads=[("to", b)]))
                S.add("act", lambda e, b=b: e.activation(out=hit[b], in_=xin[b], func=AF.Copy), reads=[("xin", b)], writes=[("hit", b)])
                S.add("dve", lambda e, b=b: e.tensor_tensor(out=lot[b], in0=xin[b], in1=hit[b], op=ALU.subtract),
                      reads=[("xin", b), ("hit", b)], writes=[("lot", b)])
                for (src, dst, nm, bank, eng) in ((hit, xhi, "xh", 4 + 2 * b, "dve"), (lot, xlo, "xl", 5 + 2 * b, "act")):
                    pb = ps[bank][:].bitcast(BF16)
                    for c in range(8):
                        S.add("pe", lambda e, c=c, pb=pb, src=src, b=b: e.transpose(pb[:, c * 128:(c + 1) * 128], src[b][:, c * 128:(c + 1) * 128], identb[:]),
                              reads=[(("hit", b) if nm == "xh" else ("lot", b)), "identb"], writes=[PS(bank)])
                    pv = pb.rearrange("p (c t) -> p c t", c=8)
                    if eng == "dve":
                        S.add("dve", lambda e, pv=pv, dst=dst, cols=cols: e.tensor_copy(out=dst[:, :, cols], in_=pv),
                              writes=[PS(bank), (nm, "t", tt)] + [(nm, c, tg) for c in range(8)])
                    else:
                        S.add("act", lambda e, pv=pv, dst=dst, cols=cols: e.activation(out=dst[:, :, cols], in_=pv, func=AF.Copy),
                              writes=[PS(bank), (nm, "t", tt)] + [(nm, c, tg) for c in range(8)])
            return outs

        class LNState:
            pass

        def ln_alloc():
            st = LNState()
            st.r = arena.take([8, 512], F32)
            st.rb = [arena.take([512], BF16) for _ in range(3)]
            st.rsq = [arena.take([512], BF16) for _ in range(3)]
            st.mean = arena.take([512], F32)
            st.var = arena.take([512], F32)
            st.rstd = arena.take([512], F32)
            st.nmr = arena.take([512], F32)
            st.v = [arena.take([512], F32) for _ in range(2)]
            st.nb = [arena.take([512], F32) for _ in range(2)]
            st.cnt = 0
            return st

        def ln_resid_chunk(st, m, tg, y_bank, bias_col, b_s1, b_s2):
            sl = tg_sl(tg)
            i = st.cnt % 3
            st.cnt += 1
            rm = st.r[:, m, :]
            if bias_col is not None:
                S.add("act", lambda e: e.activation(out=rm, in_=ps[y_bank][:], func=AF.Identity, bias=bias_col, scale=1.0),
                      reads=["b2c"], writes=[PS(y_bank), ("r", m)])
                S.add("dve", lambda e: e.scalar_tensor_tensor(out=rm, in0=xhi[:, m, sl], scalar=ALPHA, in1=rm, op0=ALU.mult, op1=ALU.add),
                      reads=[("xh", m, tg)], writes=[("r", m)])
                S.add("dve", lambda e: e.scalar_tensor_tensor(out=rm, in0=xlo[:, m, sl], scalar=ALPHA, in1=rm, op0=ALU.mult, op1=ALU.add),
                      reads=[("xl", m, tg)], writes=[("r", m)])
            else:
                S.add("dve", lambda e: e.scalar_tensor_tensor(out=rm, in0=xhi[:, m, sl], scalar=ALPHA, in1=ps[y_bank][:], op0=ALU.mult, op1=ALU.add),
                      reads=[("xh", m, tg)], writes=[PS(y_bank), ("r", m)])
            S.add("act", lambda e: e.activation(out=st.rb[i], in_=rm, func=AF.Copy), reads=[("r", m)], writes=[("rb", i)])
            S.add("act", lambda e: e.activation(out=st.rsq[i], in_=rm, func=AF.Square), reads=[("r", m)], writes=[("rsq", i)])
            def emit_stats():
                mm(ps[b_s1][:], onesb[:], st.rb[i], m == 0, m == 7, ["onesb", ("rb", i)], b_s1)
                mm(ps[b_s2][:], onesb[:], st.rsq[i], m == 0, m == 7, ["onesb", ("rsq", i)], b_s2)
            return emit_stats

        def ln_finish(st, ln_idx, tg, b_s1, b_s2):
            sl = tg_sl(tg)

            def stats():
                S.add("dve", lambda e: e.tensor_scalar(out=st.mean, in0=ps[b_s1][:], scalar1=1.0 / D, scalar2=None, op0=ALU.mult),
                      writes=[PS(b_s1), "mean"])
                S.add("act", lambda e: e.activation(out=st.var, in_=st.mean, func=AF.Square), reads=["mean"], writes=["var"])
                S.add("dve", lambda e: e.scalar_tensor_tensor(out=st.var, in0=ps[b_s2][:], scalar=1.0 / D, in1=st.var, op0=ALU.mult, op1=ALU.subtract),
                      writes=[PS(b_s2), "var"])
                S.add("act", lambda e: e.activation(out=st.var, in_=st.var, func=AF.Sqrt, bias=epsc[:, 0:1], scale=1.0), reads=["epsc"], writes=["var"])
                S.add("dve", lambda e: e.reciprocal(out=st.rstd, in_=st.var), reads=["var"], writes=["rstd"])
                S.add("dve", lambda e: e.scalar_tensor_tensor(out=st.nmr, in0=st.mean, scalar=-1.0, in1=st.rstd, op0=ALU.mult, op1=ALU.mult),
                      reads=["mean", "rstd"], writes=["nmr"])

            def chunk(m):
                i = m % 2
                g = lnp[:, ln_idx, 0, m:m + 1]
                bcol = lnp[:, ln_idx, 1, m:m + 1]
                rm = st.r[:, m, :]
                S.add("dve", lambda e: e.scalar_tensor_tensor(out=st.v[i], in0=rm, scalar=g, in1=st.rstd, op0=ALU.mult, op1=ALU.mult),
                      reads=[("r", m), "rstd", "lnp"], writes=[("v", i)])
                S.add("act", lambda e: e.activation(out=st.nb[i], in_=st.nmr, func=AF.Identity, bias=bcol, scale=g),
                      reads=["nmr", "lnp"], writes=[("nb", i)])
                S.add("dve", lambda e: e.tensor_tensor(out=st.v[i], in0=st.v[i], in1=st.nb[i], op=ALU.add),
                      reads=[("nb", i)], writes=[("v", i)])
                S.add("act", lambda e: e.activation(out=xhi[:, m, sl], in_=st.v[i], func=AF.Copy), reads=[("v", i)], writes=[("xh", m, tg)])
                S.add("dve", lambda e: e.tensor_tensor(out=xlo[:, m, sl], in0=st.v[i], in1=xhi[:, m, sl], op=ALU.subtract),
                      reads=[("v", i), ("xh", m, tg)], writes=[("xl", m, tg)])

            steps = [stats]
            for m in range(8):
                steps.append(lambda m=m: chunk(m))
            return steps

        def out_proj_ln(w_out_d, ln_idx):
            S.barrier()
            arena.reset()
            st = ln_alloc()
            st.end = arena.off
            wo = arena.take([8, D], BF16)
            for c in range(8):
                load_w(wo[:, c, :], w_out_d[c * 128:(c + 1) * 128, :], [("wo", c)])
            fsteps = []
            for tg in range(4):
                sl = tg_sl(tg)
                pend = None
                for m in range(8):
                    bank = m % 4
                    for c in range(8):
                        mm(ps[bank][:], wo[:, c, m * 128:(m + 1) * 128], obuf[:, c, sl], c == 0, False, [("wo", c), ("ob", c, tg)], bank)
                    mm(ps[bank][:], alphaI[:], xlo[:, m, sl], False, True, ["alphaI", ("xl", m, tg)], bank)
                    if pend is not None:
                        pend()
                    if fsteps:
                        if m == 0:
                            fsteps.pop(0)()
                        fsteps.pop(0)()
                    pend = ln_resid_chunk(st, m, tg, bank, None, 4, 5)
                pend()
                assert not fsteps
                fsteps = ln_finish(st, ln_idx, tg, 4, 5)
            return st, fsteps

        def ffn_ln(layer, ln_idx, st, pending):
            ensure_conv(layer)
            arena.off = st.end
            w1b = [arena.take([8, 512], BF16) for _ in range(2)]
            h1 = arena.take([32, 512], BF16)
            w2b = [arena.take([32, 128], BF16) for _ in range(2)]
            tmp = [arena.take([512], F32) for _ in range(2)]
            tmp += [obuf[:, 7, i * 1024:(i + 1) * 1024].bitcast(F32) for i in range(2)]
            tmp_first = set()
            nw1 = 0
            nw2 = 0
            fsteps = list(pending)
            for tg in range(4):
                sl = tg_sl(tg)
                for hb in range(8):
                    wb_i = nw1 % 2
                    nw1 += 1
                    dma("sp", w1b[wb_i], w1t[layer, hb].rearrange("p (c n) -> p c n", c=8), reads=[("w1t", layer, hb)],
                        writes=[("w1b", wb_i)] + ([("wo", c) for c in range(8)] if nw1 <= 2 else []))
                    for mc in range(4):
                        ch = hb * 4 + mc
                        bank = (0, 1, 6, 7)[ch % 4]
                        for k in range(8):
                            mm(ps[bank][:], w1b[wb_i][:, k, mc * 128:(mc + 1) * 128], xhi[:, k, sl], k == 0, k == 7,
                               [("w1b", wb_i), ("xh", k, tg)], bank)
                        ti = ch % 4
                        extra = []
                        if ti >= 2 and ti not in tmp_first:
                            tmp_first.add(ti)
                            extra = [("ob", 7, tg_) for tg_ in range(4)]
                        S.add("act", lambda e, bank=bank, ti=ti, ch=ch: e.activation(out=tmp[ti], in_=ps[bank][:], func=AF.Identity,
                                                                                    bias=b1c[:, layer, ch:ch + 1], scale=1.0),
                              reads=["b1c"], writes=[PS(bank), ("tmp", ti)] + extra)
                        S.add("dve", lambda e, ti=ti, ch=ch: e.scalar_tensor_tensor(out=h1[:, ch, :], in0=tmp[ti], scalar=0.0, in1=tmp[ti],
                                                                                   op0=ALU.max, op1=ALU.mult),
                              reads=[("tmp", ti)], writes=[("h1", ch)])
                        if fsteps and ch % 3 == 2:
                            fsteps.pop(0)()
                while fsteps:
                    fsteps.pop(0)()
                pend = None
                for m in range(8):
                    wb_i = nw2 % 2
                    nw2 += 1
                    dma("pool", w2b[wb_i], w2t[layer, m].rearrange("p (k n) -> p k n", k=32),
                        reads=[("w2t", layer, m, part) for part in range(4)], writes=[("w2b", wb_i)])
                    bank = 2 + m % 2
                    for k in range(32):
                        mm(ps[bank][:], w2b[wb_i][:, k, :], h1[:, k, :], k == 0, k == 31, [("w2b", wb_i), ("h1", k)], bank)
                    if pend is not None:
                        pend()
                    pend = ln_resid_chunk(st, m, tg, bank, b2c[:, layer, m:m + 1], 4, 5)
                pend()
                assert not fsteps
                fsteps = ln_finish(st, ln_idx, tg, 4, 5)
            for f in fsteps:
                f()

        class AttnBufs:
            pass

        def attn_alloc(n_pt=6, need_tmp=False):
            ab = AttnBufs()
            ab.pt = [arena.take([512], BF16) for _ in range(n_pt)]
            ab.tmp = [arena.take([512], F32) for _ in range(4)] if need_tmp else None
            ab.rz = [arena.take([512], F32) for _ in range(2)]
            ab.oc = [arena.take([512], F32) for _ in range(2)]
            ab.zc = [arena.take([512], F32) for _ in range(2)]
            ab.n = 0
            ab.nt = 0
            return ab

        def attn_tiles(ab, tiles, o_bank, z_bank):
            attn_run(ab, [(tiles, o_bank, z_bank)])

        def attn_run(ab, streams, hooks=None, SB=(0, 1, 6)):
            merged = []
            mx = max(len(t) for t, _, _ in streams)
            for i in range(mx):
                for (tl, ob, zb) in streams:
                    if i < len(tl):
                        t = dict(tl[i])
                        t["ob"], t["zb"] = ob, zb
                        t["first"] = (i == 0)
                        t["last"] = (i == len(tl) - 1)
                        merged.append(t)
            tiles = merged
            nt = len(tiles)
            NSB = len(SB)
            LA = NSB - 1

            def score(i):
                t = tiles[i]
                bank = SB[(ab.n + i) % NSB]
                mm(ps[bank][:, 0:t["n1"] - t["n0"]], t["kT"], t["q"], True, True, t["kreads"], bank)

            for i in range(min(LA, nt)):
                score(i)
            hooks = list(hooks or [])
            for i, t in enumerate(tiles):
                if i + LA < nt:
                    score(i + LA)
                if hooks and i >= 3 and (i - 3) % 4 == 0:
                    hooks.pop(0)()
                bank = SB[(ab.n + i) % NSB]
                n = t["n1"] - t["n0"]
                pi = (ab.n + i) % len(ab.pt)
                pt = ab.pt[pi][:, 0:n]
                if t.get("dtab") is not None:
                    ti = ab.nt % 4
                    ab.nt += 1
                    tmp = ab.tmp[ti][:, 0:n]
                    S.add("dve", lambda e, t=t, tmp=tmp, bank=bank, n=n: e.scalar_tensor_tensor(
                        out=tmp, in0=t["dtab"], scalar=t["dscale"], in1=ps[bank][:, 0:n], op0=ALU.mult, op1=ALU.add),
                        reads=["dtab"], writes=[PS(bank), ("atmp", ti)])
                    S.add("act", lambda e, t=t, tmp=tmp, pt=pt: e.activation(out=pt, in_=tmp, func=AF.Exp, scale=t["scale"], bias=t.get("cbias", 0.0)),
                          reads=[("atmp", ti)], writes=[("pt", pi)])
                else:
                    S.add("act", lambda e, t=t, pt=pt, bank=bank, n=n: e.activation(out=pt, in_=ps[bank][:, 0:n], func=AF.Exp, scale=t["scale"]),
                          writes=[PS(bank), ("pt", pi)])
                mm(ps[t["ob"]][:, t["n0"]:t["n1"]], t["v"], pt, t["first"], t["last"], t["vreads"] + [("pt", pi)], t["ob"])
                mm(ps[t["zb"]][:, t["n0"]:t["n1"]], onesb[:], pt, t["first"], t["last"], ["onesb", ("pt", pi)], t["zb"])
            for hk in hooks:
                hk()
            ab.n += nt

        def layer0_attention():
            S.barrier()
            arena.reset()
            dtab = arena.take([5, 512], F32)
            dma("sp", dtab, c_dtab.rearrange("v p n -> p v n"), writes=["dtab"])
            wd = [arena.take([8, 384], BF16) for _ in range(2)]
            dkT = [arena.take([SEQ], BF16) for _ in range(2)]
            dv = [arena.take([16, 128], BF16) for _ in range(2)]
            dqz = [[arena.take([512], BF16) for _ in range(2)] for _ in range(2)]
            for mp_ in range(2):
                for b_ in range(2):
                    S.add("dve", lambda e, mp_=mp_, b_=b_: e.memset(dqz[mp_][b_], 0.0), writes=[("dqz", mp_, b_)])
            a32 = [arena.take([512], F32) for _ in range(3)]
            osq = arena.take([512], BF16)
            rsd = arena.take([512], F32)
            ab = attn_alloc(6, need_tmp=True)
            dscale = 64.0 ** -0.5
            it = 0
            tail = []
            for h in range(4):
                slope = 2.0 ** (-8.0 * (h + 1) / 4)
                wi = h % 2
                w4 = wd[wi].rearrange("p c (a n) -> p c a n", a=3)
                def load_wd(hh):
                    w4n = wd[hh % 2].rearrange("p c (a n) -> p c a n", a=3)
                    for a in range(3):
                        c0 = 672 + a * 512 + hh * 128
                        load_w(w4n[:, :, a, :], ev_w_in[:, c0:c0 + 128].rearrange("(c p) n -> p c n", p=128), [("wd", hh % 2, a)])
                if h == 0:
                    load_wd(0)
                if h + 1 < 4:
                    load_wd(h + 1)
                if h == 1:
                    ensure_conv(0)
                def head_prep(hh, tg):
                    wj = hh % 2
                    w4_ = wd[wj].rearrange("p c (a n) -> p c a n", a=3)
                    sl_ = tg_sl(tg)
                    for k in range(8):
                        mm(ps[6][:], w4_[:, k, 1, :], xhi[:, k, sl_], k == 0, k == 7, [("wd", wj, 1), ("xh", k, tg)], 6)
                    S.add("act", lambda e: e.activation(out=dkT[wj][:, sl_], in_=ps[6][:], func=AF.Copy), writes=[PS(6), ("dkT", wj)])
                    for t4 in range(4):
                        tt = tg * 4 + t4
                        for k in range(8):
                            mm(ps[7][:, t4 * 128:(t4 + 1) * 128], xhi[:, k, tt * 128:(tt + 1) * 128], w4_[:, k, 2, :], k == 0, k == 7,
                               [("wd", wj, 2), ("xh", k, tg)], 7)
                    S.add("dve", lambda e: e.tensor_copy(out=dv[wj][:, tg * 4:(tg + 1) * 4, :], in_=ps[7][:].rearrange("p (t n) -> p t n", t=4)),
                          writes=[PS(7), ("dv", wj)])

                if h == 0:
                    for tg in range(4):
                        head_prep(0, tg)
                for qg in range(4):
                    sl = tg_sl(qg)
                    qb = it % 2

                    def dq_proj(hh, qg_, qb_):
                        w4_ = wd[hh % 2].rearrange("p c (a n) -> p c a n", a=3)
                        sl_ = tg_sl(qg_)
                        for k in range(8):
                            mm(ps[6][:], w4_[:, k, 0, :], xhi[:, k, sl_], k == 0, k == 7, [("wd", hh % 2, 0), ("xh", k, qg_)], 6)
                        S.add("act", lambda e: e.activation(out=dqz[0][qb_][0:64, :], in_=ps[6][0:64, :], func=AF.Copy),
                              writes=[PS(6), ("dqz", 0, qb_)])
                        S.add("dve", lambda e: e.tensor_copy(out=dqz[1][qb_][64:128, :], in_=ps[6][64:128, :]),
                              writes=[PS(6), ("dqz", 1, qb_)])

                    if h + 1 < 4:
                        head_prep(h + 1, qg)
                    if it == 0:
                        dq_proj(0, 0, 0)
                    nxt = (h, qg + 1) if qg < 3 else ((h + 1, 0) if h < 3 else None)
                    if nxt is not None:
                        dq_proj(nxt[0], nxt[1], (it + 1) % 2)
                    streams = []
                    for mp in range(2):
                        rows = slice(mp * 64, mp * 64 + 64)
                        ob, zb = 2 + mp, 4 + mp
                        tiles = []
                        for kc in range(16):
                            delta = qg * 512 - kc * 128
                            if delta >= 128:
                                dt_ap, dsc, cb = dtab[:, 0, :], slope / dscale, -slope * delta
                            elif delta <= -512:
                                dt_ap, dsc, cb = dtab[:, 0, :], -slope / dscale, slope * delta
                            else:
                                dt_ap, dsc, cb = dtab[:, 1 + (-delta) // 128, :], slope / dscale, 0.0
                            tiles.append(dict(kT=dkT[wi][:, kc * 128:(kc + 1) * 128], q=dqz[mp][qb][:, :], kreads=[("dkT", wi), ("dqz", mp, qb)],
                                              v=dv[wi][:, kc, :], vreads=[("dv", wi)], n0=0, n1=512, scale=dscale,
                                              dtab=dt_ap, dscale=dsc, cbias=cb * 1.0))
                        streams.append((tiles, ob, zb))
                    attn_run(ab, streams, hooks=tail)
                    for mp in range(2):
                        ob, zb = 2 + mp, 4 + mp
                        S.add("act", lambda e, zb=zb, mp=mp: e.activation(out=ab.zc[mp][:], in_=ps[zb][:], func=AF.Copy), writes=[PS(zb), ("zc", mp)])
                        S.add("act", lambda e, ob=ob, mp=mp: e.activation(out=ab.oc[mp][:], in_=ps[ob][:], func=AF.Copy), writes=[PS(ob), ("oc", mp)])

                    def tailA():
                        S.add("dve", lambda e: e.tensor_tensor(out=a32[0][:], in0=ab.oc[0][:], in1=ab.zc[1][:], op=ALU.mult),
                              reads=[("oc", 0), ("zc", 1)], writes=[("a32", 0)])
                        S.add("dve", lambda e: e.tensor_tensor(out=a32[1][:], in0=ab.oc[1][:], in1=ab.zc[0][:], op=ALU.mult),
                              reads=[("oc", 1), ("zc", 0)], writes=[("a32", 1)])
                        S.add("dve", lambda e: e.scalar_tensor_tensor(out=a32[2][:], in0=a32[1][:], scalar=lamw[:, 4:5], in1=a32[0][:], op0=ALU.mult, op1=ALU.add),
                              reads=[("a32", 0), ("a32", 1), "lamw"], writes=[("a32", 2)])
                        S.add("act", lambda e: e.activation(out=osq[:], in_=a32[2][:], func=AF.Square), reads=[("a32", 2)], writes=["osq"])

                    def tailB():
                        S.add("dve", lambda e: e.tensor_tensor(out=ab.rz[0][:], in0=ab.zc[0][:], in1=ab.zc[1][:], op=ALU.mult),
                              reads=[("zc", 0), ("zc", 1)], writes=[("rz", 0)])
                        S.add("dve", lambda e: e.scalar_tensor_tensor(out=ab.rz[1][:], in0=ab.rz[0][:], scalar=RMS_EPS, in1=ab.rz[0][:], op0=ALU.mult, op1=ALU.mult),
                              reads=[("rz", 0)], writes=[("rz", 1)])
                        mm(ps[7][:], onesb[:], osq[:], True, True, ["onesb", "osq"], 7)
                        S.add("dve", lambda e: e.scalar_tensor_tensor(out=rsd[:], in0=ps[7][:], scalar=1.0 / 128, in1=ab.rz[1][:], op0=ALU.mult, op1=ALU.add),
                              reads=[("rz", 1)], writes=[PS(7), "rsd"])
                        S.add("act", lambda e: e.activation(out=rsd[:], in_=rsd[:], func=AF.Sqrt), writes=["rsd"])

                    def tailC(h=h, sl=sl, qg=qg):
                        S.add("dve", lambda e: e.reciprocal(out=rsd[:], in_=rsd[:]), writes=["rsd"])
                        S.add("dve", lambda e: e.scalar_tensor_tensor(out=obuf[:, 4 + h, sl], in0=a32[2][:], scalar=lamw[:, 5:6], in1=rsd[:], op0=ALU.mult, op1=ALU.mult),
                              reads=[("a32", 2), "rsd", "lamw"], writes=[("ob", 4 + h, qg)])

                    tail = [tailA, tailB, tailC]
                    it += 1

            for f in tail:
                f()
            S.barrier()
            arena.reset()
            ctab = arena.take([SEQ], F32)
            stab = arena.take([SEQ], F32)
            dma("sp", ctab[0:96, :], c_rope[0], writes=["ctab"])
            dma("sp", stab[0:96, :], c_rope[1], writes=["stab"])
            cqn = arena.take([3, SEQ], BF16)
            ckvn = arena.take([2, SEQ], BF16)
            kr = arena.take([SEQ], BF16)
            vall = arena.take([16, 512], BF16)
            wuq = arena.take([3, 800], BF16)
            wuqr = arena.take([3, 800], BF16)
            wukv = arena.take([2, 1024], BF16)
            mark = arena.off
            wlat = arena.take([8, 672], BF16)
            wkrot = arena.take([8, 96], BF16)
            cq32 = arena.take([3, 512], F32)
            sq = [arena.take([512], BF16) for _ in range(2)]
            t32 = [arena.take([512], F32) for _ in range(2)]
            rst = arena.take([512], F32)
            wvc = arena.take([2, 512], BF16)
            load_w(wlat, ev_w_in[:, 0:672].rearrange("(c p) n -> p c n", p=128), ["wlat"])
            S.add("dve", lambda e: e.memset(wuq[:, :, 768:800], 0.0), writes=["wuqpad"])
            load_w(wuq[:, :, 0:768], ev_w_uq.rearrange("(c p) n -> p c n", p=128), ["wuq"])
            load_w(wukv, ev_w_ukv.rearrange("(c p) n -> p c n", p=128), ["wukv"])
            ensure_conv(1)
            S.add("dve", lambda e: e.memset(wkrot[:], 0.0), writes=["wkrot"])
            S.add("dve", lambda e: e.tensor_scalar(out=wkrot[:, :, 64:80], in0=wlat[:, :, 656:672], scalar1=-1.0, scalar2=None, op0=ALU.mult),
                  reads=["wlat"], writes=["wkrot"])
            S.add("dve", lambda e: e.tensor_copy(out=wkrot[:, :, 80:96], in_=wlat[:, :, 640:656]), reads=["wlat"], writes=["wkrot"])
            S.add("dve", lambda e: e.tensor_copy(out=wvc.rearrange("p c (h j) -> p c h j", h=8),
                                                  in_=wukv.rearrange("p c (h j) -> p c h j", h=8)[:, :, :, 64:128]), reads=["wukv"], writes=["wvc"])
            S.add("dve", lambda e: e.memset(wuqr[:], 0.0), writes=["wuqr"])
            wuq4 = wuq[:, :, 0:768].rearrange("p c (h j) -> p c h j", h=8)
            wuqr4 = wuqr[:, :, 0:768].rearrange("p c (h j) -> p c h j", h=8)
            for c in range(3):
                S.add("dve", lambda e, c=c: e.tensor_scalar(out=wuqr4[:, c, :, 64:80], in0=wuq4[:, c, :, 80:96], scalar1=-1.0, scalar2=None, op0=ALU.mult),
                      reads=["wuq"], writes=["wuqr"])
                S.add("dve", lambda e, c=c: e.tensor_copy(out=wuqr4[:, c, :, 80:96], in_=wuq4[:, c, :, 64:80]), reads=["wuq"], writes=["wuqr"])

            def rms_block(srcbanks, ng, gcols, nfeat, dst, tg):
                sl = tg_sl(tg)
                for j, bank in enumerate(srcbanks):
                    S.add("act", lambda e, j=j, bank=bank: e.activation(out=cq32[:, j, :], in_=ps[bank][:], func=AF.Copy),
                          writes=[PS(bank), ("cq32", j)])
                    S.add("act", lambda e, j=j: e.activation(out=sq[j % 2], in_=cq32[:, j, :], func=AF.Square),
                          reads=[("cq32", j)], writes=[("sq", j % 2)])
                    mm(ps[7][:], onesb[:], sq[j % 2], j == 0, j == ng - 1, ["onesb", ("sq", j % 2)], 7)
                S.add("act", lambda e: e.activation(out=rst, in_=ps[7][:], func=AF.Sqrt, bias=epsc[:, 1:2], scale=1.0 / nfeat),
                      reads=["epsc"], writes=[PS(7), "rst"])
                S.add("dve", lambda e: e.reciprocal(out=rst, in_=rst), writes=["rst"])
                for j in range(ng):
                    S.add("dve", lambda e, j=j: e.scalar_tensor_tensor(out=dst[:, j, sl], in0=cq32[:, j, :], scalar=gcols[:, j:j + 1], in1=rst,
                                                                      op0=ALU.mult, op1=ALU.mult),
                          reads=[("cq32", j), "rst", "qng", "kvng"], writes=[(id(dst), j, tg)])

            for tg in range(4):
                sl = tg_sl(tg)
                for j in range(3):
                    for k in range(8):
                        mm(ps[j][:], wlat[:, k, j * 128:(j + 1) * 128], xhi[:, k, sl], k == 0, k == 7, ["wlat", ("xh", k, tg)], j)
                rms_block([0, 1, 2], 3, qng, 384.0, cqn, tg)
                for j in range(2):
                    for k in range(8):
                        mm(ps[3 + j][:], wlat[:, k, 384 + j * 128:384 + (j + 1) * 128], xhi[:, k, sl], k == 0, k == 7, ["wlat", ("xh", k, tg)], 3 + j)
                rms_block([3, 4], 2, kvng, 256.0, ckvn, tg)
                for k in range(8):
                    mm(ps[5][0:96, :], wlat[:, k, 576:672], xhi[:, k, sl], k == 0, k == 7, ["wlat", ("xh", k, tg)], 5)
                for k in range(8):
                    mm(ps[6][0:96, :], wkrot[:, k, :], xhi[:, k, sl], k == 0, k == 7, ["wkrot", ("xh", k, tg)], 6)
                S.add("dve", lambda e, sl=sl: e.tensor_tensor(out=t32[0][64:96, :], in0=ps[5][64:96, :], in1=ctab[64:96, sl], op=ALU.mult),
                      reads=["ctab"], writes=[PS(5), ("t32", 0)])
                S.add("dve", lambda e, sl=sl: e.tensor_tensor(out=t32[1][64:96, :], in0=ps[6][64:96, :], in1=stab[64:96, sl], op=ALU.mult),
                      reads=["stab"], writes=[PS(6), ("t32", 1)])
                S.add("dve", lambda e, sl=sl: e.tensor_tensor(out=kr[64:96, sl], in0=t32[0][64:96, :], in1=t32[1][64:96, :], op=ALU.add),
                      reads=[("t32", 0), ("t32", 1)], writes=[("kr", tg)])
                wv = wukv.rearrange("p c (h j) -> p c h j", h=8)
                for t4 in range(4):
                    tt = tg * 4 + t4
                    bank = 5 + t4 % 2
                    for c in range(2):
                        mm(ps[bank][:], ckvn[:, c, tt * 128:(tt + 1) * 128], wvc[:, c, :], c == 0, c == 1,
                           ["wvc", (id(ckvn), 0, tg), (id(ckvn), 1, tg)], bank)
                    S.add("act", lambda e, bank=bank, tt=tt: e.activation(out=vall[:, tt, :], in_=ps[bank][:], func=AF.Copy),
                          writes=[PS(bank), ("vall", tt)])

            S.barrier()
            arena.off = mark
            kh = [arena.take([SEQ], BF16) for _ in range(2)]
            qh = [arena.take([512], BF16) for _ in range(2)]
            qt = [arena.take([512], F32) for _ in range(2)]
            ab = attn_alloc(6)
            scale = 96.0 ** -0.5
            it = 0
            mtail = []
            for h in range(8):
                kb = h % 2
                hp, e_ = h // 2, h % 2
                def k_prep(hh):
                    kb_ = hh % 2
                    for tg in range(4):
                        sl_ = tg_sl(tg)
                        for c in range(2):
                            mm(ps[6][:], wukv[:, c, hh * 128:(hh + 1) * 128], ckvn[:, c, sl_], c == 0, c == 1, ["wukv", "ckvn_all"], 6)
                        S.add("dve", lambda e, sl_=sl_: e.tensor_copy(out=kh[kb_][0:64, sl_], in_=ps[6][0:64, :]), writes=[PS(6), ("kh", kb_)])
                    S.add("dve", lambda e: e.tensor_copy(out=kh[kb_][64:96, :], in_=kr[64:96, :]), reads=["kr_all"], writes=[("kh", kb_)])

                if h == 0:
                    k_prep(0)
                for qpair in range(2):
                    if qpair == 1 and h + 1 < 8:
                        k_prep(h + 1)
                    streams = []
                    for j in range(2):
                        qg = qpair * 2 + j
                        sl = tg_sl(qg)
                        qb = j
                        for c in range(3):
                            mm(ps[6][:], wuq[:, c, h * 96:h * 96 + 128], cqn[:, c, sl], c == 0, c == 2, ["wuq", "cqn_all"], 6)
                        S.add("dve", lambda e, sl=sl: e.tensor_tensor(out=qt[0][0:96, :], in0=ps[6][0:96, :], in1=ctab[0:96, sl], op=ALU.mult),
                              reads=["ctab"], writes=[PS(6), ("qt", 0)])
                        for c in range(3):
                            mm(ps[6][:], wuqr[:, c, h * 96:h * 96 + 128], cqn[:, c, sl], c == 0, c == 2, ["wuqr", "cqn_all"], 6)
                        S.add("dve", lambda e, sl=sl: e.tensor_tensor(out=qt[1][0:96, :], in0=ps[6][0:96, :], in1=stab[0:96, sl], op=ALU.mult),
                              reads=["stab"], writes=[PS(6), ("qt", 1)])
                        S.add("dve", lambda e, qb=qb: e.tensor_tensor(out=qh[qb][0:96, :], in0=qt[0][0:96, :], in1=qt[1][0:96, :], op=ALU.add),
                              reads=[("qt", 0), ("qt", 1)], writes=[("qh", qb)])
                        tiles = []
                        for kc in range(16):
                            tiles.append(dict(kT=kh[kb][0:96, kc * 128:(kc + 1) * 128], q=qh[qb][0:96, :], kreads=[("kh", kb), ("qh", qb)],
                                              v=vall[:, kc, hp * 128:(hp + 1) * 128], vreads=[("vall", kc)], n0=0, n1=512, scale=scale))
                        streams.append((tiles, 2 + j, 4 + j))
                    attn_run(ab, streams, hooks=mtail, SB=(0, 1, 6, 7))
                    rows = slice(e_ * 64, e_ * 64 + 64)
                    for j in range(2):
                        qg = qpair * 2 + j
                        sl = tg_sl(qg)
                        ob, zb, ri = 2 + j, 4 + j, j
                        S.add("act", lambda e, zb=zb, ri=ri, rows=rows: e.activation(out=ab.zc[ri][rows, :], in_=ps[zb][rows, :], func=AF.Copy), writes=[PS(zb), ("zc", ri)])
                        S.add("act", lambda e, ob=ob, ri=ri, rows=rows: e.activation(out=ab.oc[ri][rows, :], in_=ps[ob][rows, :], func=AF.Copy), writes=[PS(ob), ("oc", ri)])
                    mtail = []
                    for j in range(2):
                        def mt(ri=j, rows=rows, hp=hp, sl=tg_sl(qpair * 2 + j), qg=qpair * 2 + j):
                            S.add("dve", lambda e: e.reciprocal(out=ab.rz[ri][rows, :], in_=ab.zc[ri][rows, :]), reads=[("zc", ri)], writes=[("rz", ri)])
                            S.add("dve", lambda e: e.tensor_tensor(out=obuf[rows, hp, sl], in0=ab.oc[ri][rows, :], in1=ab.rz[ri][rows, :], op=ALU.mult),
                                  reads=[("rz", ri), ("oc", ri)], writes=[("ob", hp, qg)])
                        mtail.append(mt)
            for f in mtail:
                f()

        def layer1_attention():
            S.barrier()
            arena.reset()
            swat = arena.take([384], F32)
            dma("sp", swat, c_swa, writes=["dtab"])
            wk = arena.take([8, 512], BF16)
            for c in range(8):
                load_w(wk[:, c, :], od_w_in[c * 128:(c + 1) * 128, 1024:1536], ["wk"])
            kT = arena.take([4, SEQ], BF16)
            wkd = arena.take([8, 4, 128], BF16)
            vdup = arena.take([16, 4, 128], BF16)
            wk4 = wk[:, :, 0:256].rearrange("p c (g j) -> p c g j", g=4)
            S.add("dve", lambda e: e.tensor_copy(out=wkd[:, :, :, 0:64], in_=wk4), reads=["wk"], writes=["wkd"])
            S.add("dve", lambda e: e.tensor_copy(out=wkd[:, :, :, 64:128], in_=wk4), reads=["wk"], writes=["wkd"])
            wq = [arena.take([8, 128], BF16) for _ in range(2)]
            qz = [[arena.take([SEQ], BF16) for _ in range(2)] for _ in range(2)]
            for e0 in range(2):
                for b_ in range(2):
                    S.add("dve", lambda e, e0=e0, b_=b_: e.memset(qz[e0][b_], 0.0), writes=[("qz", e0, b_)])
            ab = attn_alloc(6, need_tmp=True)
            sscale = 64.0 ** -0.5
            for tg in range(4):
                sl = tg_sl(tg)
                for j in range(4):
                    for k in range(8):
                        mm(ps[6][:], wkd[:, k, j, :], xhi[:, k, sl], k == 0, k == 7, ["wkd", ("xh", k, tg)], 6)
                    S.add("act", lambda e, j=j, sl=sl: e.activation(out=kT[:, j, sl], in_=ps[6][:], func=AF.Copy), writes=[PS(6), "kT"])
                for t4 in range(4):
                    tt = tg * 4 + t4
                    for k in range(8):
                        mm(ps[6][:, 0:256], xhi[:, k, tt * 128:(tt + 1) * 128], wk[:, k, 256:512], k == 0, k == 7, ["wk", ("xh", k, tg)], 6)
                    pv = ps[6][:, 0:256].rearrange("p (g j) -> p g j", g=4)
                    S.add("dve", lambda e, tt=tt, pv=pv: e.tensor_copy(out=vdup[:, tt, :, 0:64], in_=pv), writes=[PS(6), "vdup"])
                    S.add("act", lambda e, tt=tt, pv=pv: e.activation(out=vdup[:, tt, :, 64:128], in_=pv, func=AF.Copy), writes=[PS(6), "vdup"])
            it = 0
            stail = []
            for hp in range(8):
                wi = hp % 2
                if hp == 0:
                    load_w(wq[0], od_w_in[:, 0:128].rearrange("(c p) n -> p c n", p=128), [("wq", 0)])
                if hp + 1 < 8:
                    load_w(wq[(hp + 1) % 2], od_w_in[:, (hp + 1) * 128:(hp + 2) * 128].rearrange("(c p) n -> p c n", p=128), [("wq", (hp + 1) % 2)])
                def q_proj(hpp, tg):
                    wj = hpp % 2
                    sl_ = tg_sl(tg)
                    for k in range(8):
                        mm(ps[6][:], wq[wj][:, k, :], xhi[:, k, sl_], k == 0, k == 7, [("wq", wj), ("xh", k, tg)], 6)
                    S.add("act", lambda e: e.activation(out=qz[0][wj][0:64, sl_], in_=ps[6][0:64, :], func=AF.Copy), writes=[PS(6), ("qz", 0, wj)])
                    S.add("dve", lambda e: e.tensor_copy(out=qz[1][wj][64:128, sl_], in_=ps[6][64:128, :]), writes=[PS(6), ("qz", 1, wj)])

                if hp == 0:
                    for tg in range(4):
                        q_proj(0, tg)
                for qg in range(4):
                    sl = tg_sl(qg)
                    qb0 = qg * 4
                    if hp + 1 < 8:
                        q_proj(hp + 1, qg)
                    streams = []
                    for e_ in range(2):
                        h = hp * 2 + e_
                        g = h // 4
                        slope = 2.0 ** (-8.0 * (h + 1) / 16)
                        rows = slice(e_ * 64, e_ * 64 + 64)
                        tiles = []
                        for kc in range(qb0 - 1, qb0 + 5):
                            if kc < 0 or kc > 15:
                                continue
                            qa = max(qb0, kc - 1)
                            qz_ = min(qb0 + 3, kc + 1)
                            n0, n1 = (qa - qb0) * 128, (qz_ - qb0 + 1) * 128
                            toff = (qa - (kc - 1)) * 128
                            tiles.append(dict(kT=kT[:, g, kc * 128:(kc + 1) * 128], q=qz[e_][wi][:, qa * 128:(qz_ + 1) * 128],
                                              kreads=["kT", ("qz", e_, wi)], v=vdup[:, kc, g, :], vreads=["vdup"], n0=n0, n1=n1, scale=sscale,
                                              dtab=swat[:, toff:toff + (n1 - n0)], dscale=-slope / sscale, cbias=0.0))
                        streams.append((tiles, 2 + e_, 4 + e_))
                    attn_run(ab, streams, hooks=stail, SB=(0, 1, 6, 7))
                    rb = it % 2
                    it += 1
                    for e_ in range(2):
                        h = hp * 2 + e_
                        rows = slice(e_ * 64, e_ * 64 + 64)
                        ob, zb = 2 + e_, 4 + e_
                        S.add("act", lambda e, zb=zb, rb=rb, rows=rows, h=h: e.activation(out=ab.zc[rb][rows, :], in_=ps[zb][rows, :], func=AF.Identity,
                                                                                          bias=sinkt[rows, h:h + 1], scale=1.0),
                              reads=["sinkt"], writes=[PS(zb), ("zc", rb)])
                        S.add("act", lambda e, ob=ob, rb=rb, rows=rows: e.activation(out=ab.oc[rb][rows, :], in_=ps[ob][rows, :], func=AF.Copy), writes=[PS(ob), ("oc", rb)])

                    def stl1(rb=rb):
                        S.add("dve", lambda e: e.reciprocal(out=ab.rz[rb][:, 0:256], in_=ab.zc[rb][:, 0:256]), reads=[("zc", rb)], writes=[("rz", rb, 0)])

                    def stl2(rb=rb, hp=hp, sl=sl, qg=qg):
                        S.add("dve", lambda e: e.reciprocal(out=ab.rz[rb][:, 256:512], in_=ab.zc[rb][:, 256:512]), reads=[("zc", rb)], writes=[("rz", rb, 1)])
                        S.add("dve", lambda e: e.tensor_tensor(out=obuf[:, hp, sl], in0=ab.oc[rb][:], in1=ab.rz[rb][:], op=ALU.mult),
                              reads=[("rz", rb, 0), ("rz", rb, 1), ("oc", rb)], writes=[("ob", hp, qg)])
                    stail = [stl1, stl2]
            for f in stail:
                f()

        final = []
        for s in range(nseq):
            S.barrier()
            if s == 0:
                load_x(s)
            for layer in layers:
                if layer in ("a0", "a1"):
                    (layer0_attention if layer == "a0" else layer1_attention)()
                    S.barrier()
                    for c in range(8):
                        S.add("dve", lambda e, c=c: e.tensor_copy(out=xhi[:, c, :], in_=obuf[:, c, :]))
                    S.add("dve", lambda e: e.memset(xlo[:], 0.0))
                    S.barrier()
                    continue
                if layer == 0:
                    layer0_attention()
                    st_, pend_ = out_proj_ln(ev_w_out, 0)
                    ffn_ln(0, 1, st_, pend_)
                else:
                    layer1_attention()
                    st_, pend_ = out_proj_ln(od_w_out, 2)
                    ffn_ln(1, 3, st_, pend_)
            S.barrier()
            if s + 1 < nseq:
                final += store_load_x(s, s + 1)
            else:
                final += store_x(s)
        S.emit(es, final_wait_ops=final)
    return nc


def _const_tables():
    half = 16
    inv = (10000.0 ** (-np.arange(half, dtype=np.float32) / half)).astype(np.float32)
    pos = np.arange(SEQ, dtype=np.float32)
    ang = pos[None, :] * inv[:, None]
    cos = np.cos(ang).astype(np.float32)
    sin = np.sin(ang).astype(np.float32)
    rope = np.zeros((2, 96, SEQ), np.float32)
    rope[0, 0:64] = 1.0
    rope[0, 64:80] = cos
    rope[0, 80:96] = cos
    rope[1, 64:80] = sin
    rope[1, 80:96] = sin
    koff = np.arange(128, dtype=np.float32)[:, None]
    qoff = np.arange(512, dtype=np.float32)[None, :]
    dtab = np.zeros((5, 128, 512), np.float32)
    dtab[0] = -(qoff - koff)
    for v in range(4):
        dtab[1 + v] = -np.abs(qoff - koff - 128.0 * v)
    c = np.arange(384, dtype=np.float32)[None, :]
    dd = np.abs((c - 128.0) - koff)
    swa = np.where(dd <= 128.0, dd, 1.0e6).astype(np.float32)
    return rope, dtab, swa


_CACHE = {}


def _get_nc(layers):
    key = tuple(layers)
    if key not in _CACHE:
        _CACHE[key] = build_program(layers=layers)
    return _CACHE[key]


def _run(x, weights, layers):
    nc = _get_nc(layers)
    rope, dtab, swa = _const_tables()
    common = dict(weights)
    common.update(c_ident=np.eye(128, dtype=np.float32), c_rope=rope, c_dtab=dtab, c_swa=swa)
    in_maps = []
    for c in range(8):
        m = dict(common)
        m["x"] = np.ascontiguousarray(x[c * NSEQ:(c + 1) * NSEQ])
        in_maps.append(m)
    res = run_bass_kernel_spmd(nc, in_maps, core_ids=list(range(8)))
    return np.concatenate([r["out"] for r in res.results], axis=0)


def _pack_weights(ev_w_in, ev_q_norm, ev_kv_norm, ev_w_uq, ev_w_ukv, ev_lam_q1, ev_lam_k1, ev_lam_q2, ev_lam_k2,
                  ev_diff_norm, ev_w_out, od_w_in, od_sink, od_w_out, ln1_g, ln1_b, ln2_g, ln2_b,
                  ffn_w1, ffn_b1, ffn_w2, ffn_b2):
    f = lambda a: np.ascontiguousarray(np.asarray(a, dtype=np.float32))
    col = lambda v: np.ascontiguousarray(f(v).reshape(-1, 128).T)
    ln_p = np.stack([np.stack([f(ln1_g)[0], f(ln1_b)[0]]), np.stack([f(ln2_g)[0], f(ln2_b)[0]]),
                     np.stack([f(ln1_g)[1], f(ln1_b)[1]]), np.stack([f(ln2_g)[1], f(ln2_b)[1]])])
    ln_p = np.ascontiguousarray(ln_p.reshape(4, 2, 8, 128).transpose(3, 0, 1, 2).reshape(128, 64))
    b1 = np.ascontiguousarray(f(ffn_b1).reshape(2, 32, 128).transpose(2, 0, 1).reshape(128, 64))
    b2 = np.ascontiguousarray(f(ffn_b2).reshape(2, 8, 128).transpose(2, 0, 1).reshape(128, 16))
    lam = np.concatenate([f(ev_lam_q1)[0], f(ev_lam_k1)[0], f(ev_lam_q2)[0], f(ev_lam_k2)[0]])
    rep = lambda v: np.ascontiguousarray(np.broadcast_to(v[None, :], (128, v.shape[0])))
    return dict(
        ev_w_in=f(ev_w_in)[0], ev_q_norm=col(f(ev_q_norm)[0]), ev_kv_norm=col(f(ev_kv_norm)[0]), ev_w_uq=f(ev_w_uq)[0],
        ev_w_ukv=f(ev_w_ukv)[0], ev_lam=rep(lam),
        ev_diff_norm=col(f(ev_diff_norm)[0]), ev_w_out=f(ev_w_out)[0], od_w_in=f(od_w_in)[0], od_sink=rep(f(od_sink)[0]),
        od_w_out=f(od_w_out)[0], ln_p=ln_p, ffn_w1=f(ffn_w1), ffn_b1=b1, ffn_w2=f(ffn_w2),
        ffn_b2=b2)


LAUNCH_PLAN = [(0, 1)]


def kernel(x, **w):
    weights = _pack_weights(**w)
    cur = np.asarray(x, dtype=np.float32)
    for layers in LAUNCH_PLAN:
        cur = _run(cur, weights, layers)
    return cur.astype(np.float32)
```
